# Optimizing a Trainium2 kernel written in Bass

```python
import math
import jax, jax.numpy as jnp
from jax import lax
import numpy as np

D_MODEL = 4096
BATCH = 1
SEQ = 16384
DEPTH = 4

MIX_WIDTH = 2048
D_FF = 3072
RMS_EPS = 1e-6
N_BRANCHES = 3

SSD_HEAD_DIM = 64
SSD_HEADS = MIX_WIDTH // SSD_HEAD_DIM
SSD_GROUPS = 4
SSD_HEADS_PER_GROUP = SSD_HEADS // SSD_GROUPS
SSD_STATE = 128
SSD_CONV = 4
SSD_CHUNK = 128
SSD_XBC = MIX_WIDTH + 2 * SSD_GROUPS * SSD_STATE
SSD_IN = MIX_WIDTH + SSD_XBC + SSD_HEADS

SGU_CHUNK = 128
SGU_GROUPS = 16
SGU_GROUP_DIM = MIX_WIDTH // SGU_GROUPS
SGU_IN = 2 * MIX_WIDTH

NSA_HEAD_DIM = 128
NSA_HEADS = MIX_WIDTH // NSA_HEAD_DIM
NSA_KV_GROUPS = 4
NSA_HEADS_PER_GROUP = NSA_HEADS // NSA_KV_GROUPS
NSA_KV_WIDTH = NSA_KV_GROUPS * NSA_HEAD_DIM
NSA_N_BRANCH = 3
NSA_IN = MIX_WIDTH + 2 * NSA_N_BRANCH * NSA_KV_WIDTH + NSA_N_BRANCH * NSA_HEADS
CMP_BLOCK = 32
CMP_STRIDE = 16
CMP_HIDDEN = 128
SLC_BLOCK = 64
SLC_TOP_N = 16
WINDOW = 512
Q_BLOCK = 128
NEG_INF = -1e30
FORCED_SCORE = 1e4

GATE_IN = N_BRANCHES * D_MODEL
IN_COLS = SSD_IN + SGU_IN + NSA_IN + GATE_IN

kernel_name = "hybrid_ssd_sgu_nsa_macaron_trunk"


def rms_norm(x, g):
    xf = x.astype(jnp.float32)
    y = xf * lax.rsqrt(jnp.mean(xf * xf, axis=-1, keepdims=True) + RMS_EPS)
    return (y * g.astype(jnp.float32)).astype(x.dtype)


def swiglu(h, w_in, w_out):
    gate, up = jnp.split(h @ w_in, 2, axis=-1)
    return (jax.nn.silu(gate) * up) @ w_out


def masked_softmax(s, mask):
    p = jax.nn.softmax(jnp.where(mask, s.astype(jnp.float32), NEG_INF), axis=-1)
    return p * mask


def causal_depthwise_conv(u, w, b):
    out = lax.conv_general_dilated(
        u, w[:, None, :].astype(u.dtype), window_strides=(1,),
        padding=[(SSD_CONV - 1, 0)], dimension_numbers=('NWC', 'WIO', 'NWC'),
        feature_group_count=u.shape[-1])
    return out + b.astype(u.dtype)


def ssd_chunked_scan(xdt, a, b_in, c_in):
    bsz, s = xdt.shape[:2]
    nc = s // SSD_CHUNK
    G, K, P, N, Q = SSD_GROUPS, SSD_HEADS_PER_GROUP, SSD_HEAD_DIM, SSD_STATE, SSD_CHUNK
    x = xdt.astype(jnp.float32).reshape(bsz, nc, Q, G, K, P)
    bm = b_in.astype(jnp.float32).reshape(bsz, nc, Q, G, N)
    cm = c_in.astype(jnp.float32).reshape(bsz, nc, Q, G, N)
    a_cum = jnp.cumsum(a.astype(jnp.float32).reshape(bsz, nc, Q, G, K), axis=2)
    seg = a_cum[:, :, :, None] - a_cum[:, :, None, :]
    tril = jnp.tril(jnp.ones((Q, Q), bool))[:, :, None, None]
    decay = jnp.exp(jnp.where(tril, seg, -jnp.inf))
    cb = jnp.einsum('bclgn,bcsgn->bclsg', cm, bm)
    y_diag = jnp.einsum('bclsgk,bcsgkp->bclgkp', cb[..., None] * decay, x)
    decay_to_end = jnp.exp(a_cum[:, :, -1:] - a_cum)
    states = jnp.einsum('bcsgn,bcsgk,bcsgkp->bcgkpn', bm, decay_to_end, x)
    chunk_decay = jnp.exp(a_cum[:, :, -1])

    def step(h, inp):
        dec, st = inp
        return h * dec[..., None, None] + st, h

    h0 = jnp.zeros((bsz, G, K, P, N), jnp.float32)
    _, h_in = lax.scan(step, h0, (jnp.moveaxis(chunk_decay, 1, 0), jnp.moveaxis(states, 1, 0)))
    h_in = jnp.moveaxis(h_in, 0, 1)
    y_off = jnp.einsum('bclgn,bcgkpn->bclgkp', cm, h_in) * jnp.exp(a_cum)[..., None]
    return (y_diag + y_off).reshape(bsz, s, G * K, P)


def ssd_mixer(z, xbc, dt_raw, conv_w, conv_b, a_log, dt_bias, d_skip, norm_g):
    bsz, s, _ = z.shape
    xbc = jax.nn.silu(causal_depthwise_conv(xbc, conv_w, conv_b))
    xs, bm, cm = jnp.split(xbc, [MIX_WIDTH, MIX_WIDTH + SSD_GROUPS * SSD_STATE], axis=-1)
    xs = xs.reshape(bsz, s, SSD_HEADS, SSD_HEAD_DIM)
    bm = bm.reshape(bsz, s, SSD_GROUPS, SSD_STATE)
    cm = cm.reshape(bsz, s, SSD_GROUPS, SSD_STATE)
    dt = jax.nn.softplus(dt_raw.astype(jnp.float32) + dt_bias.astype(jnp.float32))
    a = -jnp.exp(a_log.astype(jnp.float32))
    y = ssd_chunked_scan(xs * dt[..., None], dt * a, bm, cm)
    y = y + xs.astype(jnp.float32) * d_skip.astype(jnp.float32)[:, None]
    y = y.reshape(bsz, s, MIX_WIDTH) * jax.nn.silu(z.astype(jnp.float32))
    y = rms_norm(y.reshape(bsz, s, SSD_GROUPS, MIX_WIDTH // SSD_GROUPS), norm_g.reshape(SSD_GROUPS, -1))
    return y.reshape(bsz, s, MIX_WIDTH).astype(z.dtype)


def sgu_mixer(uv, v_norm_g, w_s, b_s):
    bsz, s, _ = uv.shape
    u, v = jnp.split(jax.nn.gelu(uv), 2, axis=-1)
    v = rms_norm(v, v_norm_g).reshape(bsz, s // SGU_CHUNK, SGU_CHUNK, SGU_GROUPS, SGU_GROUP_DIM)
    w = w_s * jnp.tril(jnp.ones((SGU_CHUNK, SGU_CHUNK), w_s.dtype))
    mixed = jnp.einsum('gts,bcsgd->bctgd', w, v) + b_s.T[:, :, None]
    return u * mixed.reshape(bsz, s, MIX_WIDTH)


def compress_blocks(kv, pos, w1, w2):
    bsz, s, g, dh = kv.shape
    n_cmp = (s - CMP_BLOCK) // CMP_STRIDE + 1
    idx = np.arange(n_cmp)[:, None] * CMP_STRIDE + np.arange(CMP_BLOCK)[None, :]
    blocks = kv[:, idx] + pos[:, None, :]
    flat = jnp.swapaxes(blocks, 2, 3).reshape(bsz, n_cmp, g, CMP_BLOCK * dh)
    return jax.nn.gelu(flat @ w1) @ w2


def nsa_mixer(q, kv_all, gate_logits, cmp_pos, cmp_w1, cmp_w2):
    bsz, s, _ = q.shape
    G, K, dh = NSA_KV_GROUPS, NSA_HEADS_PER_GROUP, NSA_HEAD_DIM
    q = q.reshape(bsz, s, G, K, dh)
    kc, vc, ks, vs, kw, vw = [t.reshape(bsz, s, G, dh) for t in jnp.split(kv_all, 6, axis=-1)]
    kc = compress_blocks(kc, cmp_pos[0], cmp_w1[0], cmp_w2[0])
    vc = compress_blocks(vc, cmp_pos[1], cmp_w1[1], cmp_w2[1])
    n_cmp = kc.shape[1]
    n_slc = s // SLC_BLOCK
    top_n = min(SLC_TOP_N, n_slc)
    c_start = np.arange(n_cmp) * CMP_STRIDE
    s_start = np.arange(n_slc) * SLC_BLOCK
    cmp_last = jnp.asarray(c_start + CMP_BLOCK - 1)
    overlap = jnp.asarray(((c_start[:, None] < s_start[None, :] + SLC_BLOCK)
                           & (c_start[:, None] + CMP_BLOCK > s_start[None, :])).astype(np.float32))
    ks_blk = ks.reshape(bsz, n_slc, SLC_BLOCK, G, dh).transpose(0, 3, 1, 2, 4)
    vs_blk = vs.reshape(bsz, n_slc, SLC_BLOCK, G, dh).transpose(0, 3, 1, 2, 4)
    kw_pad = jnp.pad(kw, ((0, 0), (WINDOW, 0), (0, 0), (0, 0)))
    vw_pad = jnp.pad(vw, ((0, 0), (WINDOW, 0), (0, 0), (0, 0)))
    gates = jax.nn.sigmoid(gate_logits.astype(jnp.float32)).reshape(bsz, s, NSA_N_BRANCH, G, K)
    scale = dh ** -0.5
    b_idx = jnp.arange(bsz)[:, None, None, None]
    g_idx = jnp.arange(G)[None, :, None, None]
    blk_ids = jnp.arange(n_slc)
    win_off = jnp.arange(WINDOW + Q_BLOCK) - WINDOW

    def block(qb):
        q0 = qb * Q_BLOCK
        t = q0 + jnp.arange(Q_BLOCK)
        qt = lax.dynamic_slice_in_dim(q, q0, Q_BLOCK, axis=1) * scale
        m_cmp = cmp_last[None, :] <= t[:, None]
        p_cmp = masked_softmax(jnp.einsum('btgkd,bngd->bgktn', qt, kc), m_cmp)
        o_cmp = jnp.einsum('bgktn,bngd->btgkd', p_cmp.astype(vc.dtype), vc)
        imp = jnp.einsum('bgktn,nj->bgtj', p_cmp, overlap)
        cur = t // SLC_BLOCK
        visible = blk_ids[None, :] <= cur[:, None]
        forced = (blk_ids[None, :] == 0) | (blk_ids[None, :] == cur[:, None]) | (blk_ids[None, :] == cur[:, None] - 1)
        score = jnp.where(visible, jnp.where(forced, FORCED_SCORE, imp), -1.0)
        _, sel = lax.top_k(score, top_n)
        k_sel = ks_blk[b_idx, g_idx, sel]
        v_sel = vs_blk[b_idx, g_idx, sel]
        tok = sel[..., None] * SLC_BLOCK + jnp.arange(SLC_BLOCK)
        m_slc = (tok <= t[:, None, None]).reshape(bsz, G, 1, Q_BLOCK, top_n * SLC_BLOCK)
        s_slc = jnp.einsum('btgkd,bgtjsd->bgktjs', qt, k_sel).reshape(bsz, G, K, Q_BLOCK, top_n * SLC_BLOCK)
        p_slc = masked_softmax(s_slc, m_slc).reshape(bsz, G, K, Q_BLOCK, top_n, SLC_BLOCK)
        o_slc = jnp.einsum('bgktjs,bgtjsd->btgkd', p_slc.astype(v_sel.dtype), v_sel)
        k_w = lax.dynamic_slice_in_dim(kw_pad, q0, WINDOW + Q_BLOCK, axis=1)
        v_w = lax.dynamic_slice_in_dim(vw_pad, q0, WINDOW + Q_BLOCK, axis=1)
        kpos = q0 + win_off
        m_win = (kpos[None, :] >= 0) & (kpos[None, :] <= t[:, None]) & (kpos[None, :] > t[:, None] - WINDOW)
        p_win = masked_softmax(jnp.einsum('btgkd,bsgd->bgkts', qt, k_w), m_win)
        o_win = jnp.einsum('bgkts,bsgd->btgkd', p_win.astype(v_w.dtype), v_w)
        g = lax.dynamic_slice_in_dim(gates, q0, Q_BLOCK, axis=1)[..., None]
        o = g[:, :, 0] * o_cmp + g[:, :, 1] * o_slc + g[:, :, 2] * o_win
        return o.reshape(bsz, Q_BLOCK, MIX_WIDTH).astype(q.dtype)

    out = lax.map(block, jnp.arange(s // Q_BLOCK))
    return jnp.swapaxes(out, 0, 1).reshape(bsz, s, MIX_WIDTH)


def setup_inputs(seed: int = 0) -> dict:
    key = jax.random.key(seed)
    ks = jax.random.split(key, 20)
    f32 = jnp.float32

    def nrm(k, shape, scale):
        return jax.random.normal(k, shape, f32) * scale

    dt0 = jnp.exp(jax.random.uniform(ks[8], (DEPTH, SSD_HEADS), f32, math.log(0.001), math.log(0.1)))
    return {
        'x': nrm(ks[0], (BATCH, SEQ, D_MODEL), 1.0),
        'norm_g': 1.0 + nrm(ks[1], (DEPTH, 6, D_MODEL), 0.02),
        'ffn_w_in': nrm(ks[2], (DEPTH, 2, D_MODEL, 2 * D_FF), D_MODEL ** -0.5),
        'ffn_w_out': nrm(ks[3], (DEPTH, 2, D_FF, D_MODEL), D_FF ** -0.5),
        'w_in': nrm(ks[4], (DEPTH, D_MODEL, IN_COLS), D_MODEL ** -0.5),
        'ssd_conv_w': nrm(ks[5], (DEPTH, SSD_CONV, SSD_XBC), SSD_CONV ** -0.5),
        'ssd_conv_b': nrm(ks[6], (DEPTH, SSD_XBC), 0.02),
        'ssd_a_log': jnp.log(jax.random.uniform(ks[7], (DEPTH, SSD_HEADS), f32, 1.0, 16.0)),
        'ssd_dt_bias': dt0 + jnp.log(-jnp.expm1(-dt0)),
        'ssd_d': 1.0 + nrm(ks[9], (DEPTH, SSD_HEADS), 0.1),
        'ssd_norm_g': 1.0 + nrm(ks[10], (DEPTH, MIX_WIDTH), 0.02),
        'sgu_norm_g': 1.0 + nrm(ks[11], (DEPTH, MIX_WIDTH), 0.02),
        'sgu_w': nrm(ks[12], (DEPTH, SGU_GROUPS, SGU_CHUNK, SGU_CHUNK), SGU_CHUNK ** -0.5),
        'sgu_b': 1.0 + nrm(ks[13], (DEPTH, SGU_GROUPS, SGU_CHUNK), 0.1),
        'cmp_pos': nrm(ks[14], (DEPTH, 2, CMP_BLOCK, NSA_HEAD_DIM), 0.02),
        'cmp_w1': nrm(ks[15], (DEPTH, 2, CMP_BLOCK * NSA_HEAD_DIM, CMP_HIDDEN), (CMP_BLOCK * NSA_HEAD_DIM) ** -0.5),
        'cmp_w2': nrm(ks[16], (DEPTH, 2, CMP_HIDDEN, NSA_HEAD_DIM), CMP_HIDDEN ** -0.5),
        'w_branch': nrm(ks[17], (DEPTH, N_BRANCHES, MIX_WIDTH, D_MODEL), MIX_WIDTH ** -0.5),
        'w_out': nrm(ks[18], (DEPTH, D_MODEL, D_MODEL), D_MODEL ** -0.5),
    }


def reference(x, norm_g, ffn_w_in, ffn_w_out, w_in, ssd_conv_w, ssd_conv_b, ssd_a_log, ssd_dt_bias,
              ssd_d, ssd_norm_g, sgu_norm_g, sgu_w, sgu_b, cmp_pos, cmp_w1, cmp_w2, w_branch, w_out):
    bsz, s, _ = x.shape
    splits = [int(v) for v in np.cumsum([MIX_WIDTH, SSD_XBC, SSD_HEADS, SGU_IN, MIX_WIDTH,
                                         2 * NSA_N_BRANCH * NSA_KV_WIDTH, NSA_N_BRANCH * NSA_HEADS])]
    for l in range(DEPTH):
        ng = norm_g[l]
        x = x + 0.5 * rms_norm(swiglu(rms_norm(x, ng[0]), ffn_w_in[l, 0], ffn_w_out[l, 0]), ng[1])
        proj = rms_norm(x, ng[2]) @ w_in[l]
        z, xbc, dt_raw, uv, q, kv_all, nsa_gate, merge_gate = jnp.split(proj, splits, axis=-1)
        y_a = ssd_mixer(z, xbc, dt_raw, ssd_conv_w[l], ssd_conv_b[l], ssd_a_log[l], ssd_dt_bias[l], ssd_d[l], ssd_norm_g[l])
        y_b = sgu_mixer(uv, sgu_norm_g[l], sgu_w[l], sgu_b[l])
        y_c = nsa_mixer(q, kv_all, nsa_gate, cmp_pos[l], cmp_w1[l], cmp_w2[l])
        gate = jax.nn.sigmoid(merge_gate.reshape(bsz, s, N_BRANCHES, D_MODEL))
        merged = gate[:, :, 0] * (y_a @ w_branch[l, 0])
        merged = merged + gate[:, :, 1] * (y_b @ w_branch[l, 1])
        merged = merged + gate[:, :, 2] * (y_c @ w_branch[l, 2])
        x = x + rms_norm(merged @ w_out[l], ng[3])
        x = x + 0.5 * rms_norm(swiglu(rms_norm(x, ng[4]), ffn_w_in[l, 1], ffn_w_out[l, 1]), ng[5])
    return x
```

```python
import contextlib
import math
import numpy as np
import concourse.bass as bass
import concourse.mybir as mybir
from concourse.bass_utils import run_bass_kernel_spmd

F32 = mybir.dt.float32
BF16 = mybir.dt.bfloat16
AF = mybir.ActivationFunctionType
ALU = mybir.AluOpType
AX = mybir.AxisListType

D = 4096
DFF = 3072
MIX = 2048
XBC = 3072
NHEAD = 32
IN_COLS = 26704
C_Z, C_XBC, C_DT, C_U, C_V, C_Q, C_KV, C_NG, C_MG = 0, 2048, 5120, 5152, 7200, 9248, 11296, 14368, 14416
EPS = 1e-6
NEG = -30000.0
NC = 8
QG = [[0, 2, 4, 6], [1, 3, 5, 7]]
PG = [[0, 1], [2, 3], [4, 5], [6, 7]]


class Tl:
    __slots__ = ("t", "w", "r", "name")

    def __init__(self, t, name=""):
        self.t = t
        self.w = None
        self.r = {}
        self.name = name

    def __getitem__(self, k):
        return self.t[k]


class Ctx:
    NDMASEM = 24

    def __init__(self, nc, es):
        self.nc = nc
        self.es = es
        self.eng = {"pe": nc.tensor, "act": nc.scalar, "dve": nc.vector, "pool": nc.gpsimd, "sp": nc.sync}
        self.sem = {e: es.enter_context(nc.semaphore("s_" + e)) for e in self.eng}
        self.sem["cc"] = es.enter_context(nc.semaphore("s_cc"))
        self.cnt = {e: 0 for e in self.sem}
        self.seen = {e: {} for e in self.eng}
        self.dsem, self.dtgt, self.drr = {}, {}, {}
        for q in ("sp", "pool", "act"):
            self.dsem[q] = [es.enter_context(nc.semaphore("d_%s%d" % (q, i))) for i in range(self.NDMASEM)]
            self.dtgt[q] = [0] * self.NDMASEM
            self.drr[q] = 0
        self.nsb = 0
        self.psb = []
        self.psi = 0
        self.psr = (0, 8)

    def sb(self, es, shape, dt, name=None):
        self.nsb += 1
        name = (name or "sb") + "_%d" % self.nsb
        return Tl(es.enter_context(self.nc.sbuf_tensor(name, list(shape), dt)), name)

    def psum_banks(self, n=8):
        for i in range(n):
            t = self.es.enter_context(self.nc.psum_tensor("ps%d" % i, [128, 512], F32))
            self.psb.append(Tl(t, "ps%d" % i))

    def ps(self):
        lo, hi = self.psr
        t = self.psb[lo + self.psi % (hi - lo)]
        self.psi += 1
        return t

    def dram(self, name, shape, dt, **kw):
        return Tl(self.nc.dram_tensor(name, list(shape), dt, **kw).ap(), name)

    def _semof(self, k):
        return self.dsem[k[0]][k[1]] if isinstance(k, tuple) else self.sem[k]

    def _wait(self, e, k, v):
        if self.seen[e].get(k, 0) >= v:
            return
        self.eng[e].wait_ge(self._semof(k), v)
        self.seen[e][k] = v

    def _deps(self, e, reads, writes):
        deps = {}
        for t in reads:
            if t.w is not None:
                k, v = t.w
                deps[k] = max(deps.get(k, 0), v)
        for t in writes:
            if t.w is not None:
                k, v = t.w
                if k != e:
                    deps[k] = max(deps.get(k, 0), v)
            for k, v in t.r.items():
                if k != e:
                    deps[k] = max(deps.get(k, 0), v)
        for k, v in deps.items():
            self._wait(e, k, v)

    def _stamp(self, key, val, reads, writes):
        for t in reads:
            t.r[key] = val
        for t in writes:
            t.w = (key, val)
            t.r = {}

    def op(self, e, fn, reads=(), writes=()):
        self._deps(e, reads, writes)
        ins = fn()
        self.cnt[e] += 1
        ins.then_inc(self.sem[e], 1)
        self._stamp(e, self.cnt[e], reads, writes)
        return ins

    def dma(self, q, pairs, reads=(), writes=(), **kw):
        self._deps(q, reads, writes)
        i = self.drr[q] % self.NDMASEM
        self.drr[q] += 1
        key = (q, i)
        self._wait(q, key, self.dtgt[q][i])
        for (o, a) in pairs:
            self.eng[q].dma_start(out=o, in_=a, **kw).then_inc(self.dsem[q][i], 16)
            self.dtgt[q][i] += 16
        self._stamp(key, self.dtgt[q][i], reads, writes)

    def allgather(self, groups, in_t, out_t, in_ap, out_ap):
        q = "pool"
        self._deps(q, [in_t], [out_t])
        self._wait(q, "cc", self.cnt["cc"])
        self.nc.gpsimd.collective_compute(
            "AllGather", ALU.bypass, replica_groups=groups,
            ins=[in_ap.opt()], outs=[out_ap.opt()]).then_inc(self.sem["cc"], 1)
        self.cnt["cc"] += 1
        self._stamp("cc", self.cnt["cc"], [in_t], [out_t])

    def barrier(self):
        for e in self.eng:
            for e2 in self.sem:
                if e2 != e and self.cnt[e2] > 0:
                    self._wait(e, e2, self.cnt[e2])
            for q in self.dsem:
                for i in range(self.NDMASEM):
                    if self.dtgt[q][i] > 0:
                        self._wait(e, (q, i), self.dtgt[q][i])


def v3(ap, inner):
    return ap.rearrange("p (a b) -> p a b", b=inner)


def bc_rows(ap_row, n, parts=128):
    return bass.AP(ap_row.tensor, ap_row.offset, [[0, parts], [1, n]])


class Cfg:
    def __init__(self, T=2048, DEPTH=4, stop=None, dbg=()):
        self.NC, self.T, self.DEPTH = NC, T, DEPTH
        self.S = NC * T
        self.NT = T // 128
        self.TB = min(1024, T)
        self.TW = min(512, T)
        self.stop = stop
        self.dbg = tuple(dbg)


def _pan(col0, n, w=512):
    out = []
    c = 0
    while c < n:
        out.append((col0 + c, min(w, n - c)))
        c += w
    return out


P_Z = _pan(C_Z, 2048)
P_XBC = _pan(C_XBC, 3072)
P_DT = [(C_DT, 32)]
P_U = _pan(C_U, 2048)
P_V = _pan(C_V, 2048)
P_Q = _pan(C_Q, 2048)
P_KV = _pan(C_KV, 3072)
P_NG = [(C_NG, 48)]
P_MG = _pan(C_MG, 12288)
WIN_PANELS = P_Z + P_XBC + P_DT + P_U + P_V + P_Q + P_KV + P_NG + P_MG

WSPEC = {
    "fa_in": (D, 2 * DFF, _pan(0, 2 * DFF, 256), 256),
    "fa_out": (DFF, D, _pan(0, D), 512),
    "fb_in": (D, 2 * DFF, _pan(0, 2 * DFF, 256), 256),
    "fb_out": (DFF, D, _pan(0, D), 512),
    "w_in": (D, IN_COLS, WIN_PANELS, 512),
    "w_br0": (MIX, D, _pan(0, D), 512),
    "w_br1": (MIX, D, _pan(0, D), 512),
    "w_br2": (MIX, D, _pan(0, D), 512),
    "w_o": (D, D, _pan(0, D), 512),
    "cw1k": (4096, 128, [(0, 128)], 128),
    "cw1v": (4096, 128, [(0, 128)], 128),
}


class K:
    def __init__(self, cfg):
        self.cfg = cfg
        self.nc = bass.Bass("TRN2", target_bir_lowering=False)

    def dtile(self, name, shape, dt):
        kind = "ExternalOutput" if name in self.cfg.dbg else "Internal"
        return self.c.dram(name, shape, dt, kind=kind)

    def build(self):
        cfg, nc = self.cfg, self.nc
        T, S, NT = cfg.T, cfg.S, cfg.NT
        with contextlib.ExitStack() as es:
            c = self.c = Ctx(nc, es)
            c.psum_banks(8)
            self.x_in = c.dram("x", [T, D], F32, kind="ExternalInput")
            self.out = c.dram("out", [T, D], F32, kind="ExternalOutput")
            self.wsh = {}
            for nm, (kk, nn, pans, pw) in WSPEC.items():
                self.wsh[nm] = c.dram("w_" + nm, [cfg.DEPTH, kk // NC, nn], F32, kind="ExternalInput")
            self.p_normg = c.dram("p_normg", [cfg.DEPTH, 6, D], F32, kind="ExternalInput")
            self.cinfo_d = c.dram("cinfo", [128, 32], F32, kind="ExternalInput")
            self.p_conv = c.dram("p_conv", [cfg.DEPTH, XBC, 5], F32, kind="ExternalInput")
            self.p_head = c.dram("p_head", [cfg.DEPTH, 3, NHEAD], F32, kind="ExternalInput")
            self.p_ssdg = c.dram("p_ssdg", [cfg.DEPTH, 64, NHEAD], F32, kind="ExternalInput")
            self.p_sgug = c.dram("p_sgug", [cfg.DEPTH, MIX], F32, kind="ExternalInput")
            self.p_sguw = c.dram("p_sguw", [cfg.DEPTH, 16, 128, 128], F32, kind="ExternalInput")
            self.p_sgub = c.dram("p_sgub", [cfg.DEPTH, 16 * 128], F32, kind="ExternalInput")
            self.p_cpos = c.dram("p_cpos", [cfg.DEPTH, 2, 128, 32], F32, kind="ExternalInput")
            self.p_cw2 = c.dram("p_cw2", [cfg.DEPTH, 2, 128, 128], F32, kind="ExternalInput")
            self.xa = self.dtile("xa", [T, D], F32)
            self.xb = self.dtile("xb", [T, D], F32)
            self.xc = self.dtile("xc", [T, D], F32)
            self.ybuf = self.dtile("ybuf", [T, D], F32)
            self.hT = self.dtile("hT", [D, T], BF16)
            self.hidT = self.dtile("hidT", [DFF, T], BF16)
            self.wloc, self.wfull = {}, {}
            for nm, (kk, nn, pans, pw) in WSPEC.items():
                for l in range(cfg.DEPTH):
                    self.wloc[nm, l] = c.dram("wl_%s%d" % (nm, l), [len(pans), kk // NC, pw], BF16)
                    self.wfull[nm, l] = c.dram("wf_%s%d" % (nm, l), [len(pans), 2, 4, kk // NC, pw], BF16)
            self.szT = self.dtile("szT", [MIX, T], BF16)
            self.xbcT = self.dtile("xbcT", [XBC, T], F32)
            self.dt_tm = self.dtile("dt_tm", [T, 32], F32)
            self.uT = self.dtile("uT", [MIX, T], BF16)
            self.v_tm = self.dtile("v_tm", [T, MIX], F32)
            self.qT = self.dtile("qT", [MIX, T], BF16)
            self.kvT = self.dtile("kvT", [2048, T], BF16)
            self.vtm2 = self.dtile("vtm2", [T, 1024], BF16)
            self.ngate = self.dtile("ngate", [T, 48], F32)
            self.mgT = self.dtile("mgT", [3 * D, T], BF16)
            self.convT = self.dtile("convT", [XBC, T], BF16)
            self.yaT = self.dtile("yaT", [MIX, T], BF16)
            self.ybT = self.dtile("ybT", [MIX, T], BF16)
            self.ycT = self.dtile("ycT", [MIX, T], BF16)
            self.mrg = [self.dtile("mrg%d" % i, [D, T], F32) for i in range(3)]
            self.mergedT = self.dtile("mergedT", [D, T], BF16)
            self.kv_cr = min(2048, 262144 // T)
            self.g_kvT = self.dtile("g_kvT", [2048 // self.kv_cr, 2, 4, self.kv_cr, T], BF16)
            self.vt_cr = min(T, 256)
            self.g_vtm = self.dtile("g_vtm", [T // self.vt_cr, 2, 4, self.vt_cr, 1024], BF16)
            self.halo_loc = self.dtile("halo_loc", [XBC, 4], F32)
            self.g_halo = self.dtile("g_halo", [1, 2, 4, XBC, 4], F32)
            self.s_loc = self.dtile("s_loc", [128, MIX], F32)
            self.g_sloc = self.dtile("g_sloc", [2, 2, 4, 64, MIX], F32)
            self.p_loc = self.dtile("p_loc", [16, 32], F32)
            self.g_ploc = self.dtile("g_ploc", [1, 2, 4, 16, 32], F32)
            self.s_init = self.dtile("s_init", [128, MIX], F32)
            self.cctmp = [c.dram("cctmp%d" % i, [4 * 512 * 1024 // 2], BF16) for i in range(2)]
            self.cci = 0
            self.consts(es)
            self.prep_weights()
            c.barrier()
            xcur = self.x_in
            for l in range(cfg.DEPTH):
                xcur = self.layer(l, xcur)
                if cfg.stop is not None:
                    break
            c.barrier()
        return nc

    def consts(self, es):
        c, nc = self.c, self.nc
        self.ident = c.sb(es, [128, 128], BF16, "ident")
        self.identf = idf = c.sb(es, [128, 128], F32, "identf")
        c.op("pool", lambda: nc.gpsimd.memset(idf[:], 0.0), [], [idf])
        c.op("pool", lambda: nc.gpsimd.affine_select(out=idf[:], in_=idf[:], pattern=[[-1, 128]],
                                                      compare_op=ALU.not_equal, fill=1.0, base=0, channel_multiplier=1),
             [idf], [idf])
        c.op("dve", lambda: nc.vector.tensor_copy(out=self.ident[:], in_=idf[:]), [idf], [self.ident])
        self.epsb = c.sb(es, [128, 1], F32, "epsb")
        c.op("pool", lambda: nc.gpsimd.memset(self.epsb[:], EPS), [], [self.epsb])
        self.cinfo = c.sb(es, [128, 32], F32, "cinfo")
        c.dma("sp", [(self.cinfo[:], self.cinfo_d.t)], [self.cinfo_d], [self.cinfo])

    def exchange(self, src, src_ap, dst, dst_ap, rows, rowbytes):
        c = self.c
        crows = dst_ap.shape[3]
        nchunk = dst_ap.shape[0]
        assert nchunk * crows == rows and crows * rowbytes <= 512 * 1024
        cols = src_ap.shape[1]
        for ch in range(nchunk):
            tmp = self.cctmp[self.cci % 2]
            self.cci += 1
            n = 4 * crows * cols
            if src_ap.dtype == F32:
                tv = tmp.t[0:2 * n].bitcast(F32)
            else:
                tv = tmp.t[0:n]
            c.allgather(QG, src, tmp, src_ap[ch * crows:(ch + 1) * crows, :], tv)
            c.allgather(PG, tmp, dst, tv, dst_ap[ch])

    def wpiece(self, nm, l, b, rank):
        q, r = rank % 2, rank // 2
        return self.wfull[nm, l].t[b, q, r]

    def prep_weights(self):
        c, nc, cfg = self.c, self.nc, self.cfg
        with contextlib.ExitStack() as es:
            st = [c.sb(es, [128, 512], F32, "wst") for _ in range(6)]
            sb_ = [c.sb(es, [128, 512], BF16, "wsb") for _ in range(6)]
            engs = ["act", "dve", "pool"]
            i = 0
            for l in range(cfg.DEPTH):
                for nm, (kk, nn, pans, pw) in WSPEC.items():
                    if cfg.stop == "ffn_a" and not nm.startswith("fa"):
                        continue
                    if cfg.stop in ("proj", "sgu", "ssd") and nm not in ("fa_in", "fa_out", "w_in", "w_br1"):
                        continue
                    if cfg.stop == "nsa" and nm not in ("fa_in", "fa_out", "w_in", "cw1k", "cw1v"):
                        continue
                    rows = kk // NC
                    src, dst = self.wsh[nm], self.wloc[nm, l]
                    for b, (c0, wd) in enumerate(pans):
                        for r0 in range(0, rows, 128):
                            rr = min(128, rows - r0)
                            a, bt = st[i % 6], sb_[i % 6]
                            e = engs[i % 3]
                            c.dma("sp", [(a[0:rr, 0:wd], src.t[l, r0:r0 + rr, c0:c0 + wd])], [src], [a])
                            if e == "act":
                                c.op("act", lambda a=a, bt=bt, rr=rr, wd=wd: nc.scalar.copy(out=bt[0:rr, 0:wd], in_=a[0:rr, 0:wd]), [a], [bt])
                            elif e == "dve":
                                c.op("dve", lambda a=a, bt=bt, rr=rr, wd=wd: nc.vector.tensor_copy(out=bt[0:rr, 0:wd], in_=a[0:rr, 0:wd]), [a], [bt])
                            else:
                                c.op("pool", lambda a=a, bt=bt, rr=rr, wd=wd: nc.gpsimd.tensor_copy(out=bt[0:rr, 0:wd], in_=a[0:rr, 0:wd]), [a], [bt])
                            c.dma("sp", [(dst.t[b, r0:r0 + rr, 0:wd], bt[0:rr, 0:wd])], [bt], [dst])
                            i += 1
                    full = self.wfull[nm, l]
                    self.exchange(dst, dst.t.rearrange("b r n -> (b r) n"), full, full.t, len(pans) * rows, pw * 2)

    def normT(self, xsrc, gain_ap, dstT):
        c, nc, cfg = self.c, self.nc, self.cfg
        with contextlib.ExitStack() as es:
            g = c.sb(es, [128, D], F32, "gain")
            c.dma("sp", [(g[:], bc_rows(gain_ap, D))], [self.p_normg], [g])
            xt = [c.sb(es, [128, D], F32, "nx") for _ in range(2)]
            hb = [c.sb(es, [128, D], BF16, "nh") for _ in range(2)]
            junk = c.sb(es, [128, D], BF16, "njunk")
            stat = [c.sb(es, [128, 4], F32, "nstat") for _ in range(2)]
            GT = min(4, cfg.NT)
            stg = [c.sb(es, [128, 32 * GT * 128], BF16, "nstg") for _ in range(2)]
            for gi in range(cfg.NT // GT):
                sg = stg[gi % 2]
                sgv = sg.t[:].rearrange("p (k t) -> p k t", t=GT * 128)
                for j in range(GT):
                    ti = gi * GT + j
                    x_, h_, s_ = xt[ti % 2], hb[ti % 2], stat[ti % 2]
                    c.dma("sp", [(x_[:], xsrc.t[ti * 128:(ti + 1) * 128, :])], [xsrc], [x_])
                    c.op("pool", lambda s_=s_: nc.gpsimd.memset(s_[:], 0.0), [], [s_])
                    c.op("act", lambda x_=x_, s_=s_: nc.scalar.activation(out=junk[:], in_=x_[:], func=AF.Square, accum_out=s_[:, 0:1]), [x_, s_], [junk, s_])
                    c.op("act", lambda s_=s_: nc.scalar.activation(out=s_[:, 1:2], in_=s_[:, 0:1], func=AF.Sqrt, scale=1.0 / D, bias=self.epsb[:, 0:1]), [s_, self.epsb], [s_])
                    c.op("dve", lambda s_=s_: nc.vector.reciprocal(out=s_[:, 2:3], in_=s_[:, 1:2]), [s_], [s_])
                    c.op("dve", lambda x_=x_, h_=h_, s_=s_: nc.vector.scalar_tensor_tensor(out=h_[:], in0=x_[:], scalar=s_[:, 2:3], in1=g[:], op0=ALU.mult, op1=ALU.mult), [x_, s_, g], [h_])
                    for q4 in range(4):
                        ps = c.ps()
                        psb = ps.t[:].bitcast(BF16)

                        def tr(psb=psb, h_=h_, q4=q4):
                            ins = None
                            for k in range(8):
                                dc = q4 * 8 + k
                                ins = nc.tensor.transpose(psb[:, k * 128:(k + 1) * 128], h_[:, dc * 128:(dc + 1) * 128], self.ident[:])
                            return ins
                        c.op("pe", tr, [h_, self.ident], [ps])
                        dst = sgv[:, q4 * 8:(q4 + 1) * 8, j * 128:(j + 1) * 128]
                        src = psb.rearrange("p (a b) -> p a b", b=128)
                        if q4 % 2 == 0:
                            c.op("act", lambda dst=dst, src=src: nc.scalar.copy(out=dst, in_=src), [ps], [sg])
                        else:
                            c.op("dve", lambda dst=dst, src=src: nc.vector.tensor_copy(out=dst, in_=src), [ps], [sg])
                t0 = gi * GT * 128
                c.dma("sp", [(dstT.t[:, t0:t0 + GT * 128].rearrange("(k p) t -> p k t", p=128), sgv)], [sg], [dstT])
        c.barrier()

    def gemm(self, AT, K_, wnm, l, blocks, mode, epi, pre=None):
        c, nc, cfg = self.c, self.nc, self.cfg
        KC = K_ // 128
        TB, TW = cfg.TB, cfg.TW
        kk, nn, pans, pw = WSPEC[wnm]
        KP = (kk // NC) // 128
        assert KP * NC == KC
        nslot = max(len(b) for b in blocks)
        W = self.wfull[wnm, l]
        with contextlib.ExitStack() as es:
            ab = c.sb(es, [128, KC * TB], BF16, "gA")
            abv = v3(ab.t[:], TB)
            wts = [c.sb(es, [128, KC * nslot * pw], BF16, "gW") for _ in range(2)]
            if pre is not None:
                pre(es)
            for tb in range(cfg.T // TB):
                c.dma("sp", [(abv, AT.t[:, tb * TB:(tb + 1) * TB].rearrange("(k p) t -> p k t", p=128))], [AT], [ab])

                def wview(wt):
                    return wt.t[:].rearrange("p (k s n) -> p k s n", s=nslot, n=pw)

                def loadw(bi):
                    wt = wts[bi % 2]
                    wv = wview(wt)
                    pairs = []
                    for sl, pb in enumerate(blocks[bi]):
                        wd = pans[pb][1]
                        for rank in range(NC):
                            pairs.append((wv[:, rank * KP:(rank + 1) * KP, sl, 0:wd],
                                          self.wpiece(wnm, l, pb, rank)[:, 0:wd].rearrange("(k p) n -> p k n", p=128)))
                    c.dma("sp", pairs, [W], [wt])
                loadw(0)
                for bi, blk in enumerate(blocks):
                    if bi + 1 < len(blocks):
                        loadw(bi + 1)
                    wt = wts[bi % 2]
                    wv = wview(wt)
                    if mode == "ws":
                        wd0 = pans[blk[0]][1]
                        for c0 in range(0, wd0, 128):
                            cw = min(128, wd0 - c0)
                            for t5 in range(TB // TW):
                                for sl, pb in enumerate(blk):
                                    ps = c.ps()

                                    def mm(ps=ps, wv=wv, c0=c0, cw=cw, t5=t5, sl=sl):
                                        ins = None
                                        for k in range(KC):
                                            ins = nc.tensor.matmul(ps[0:cw, 0:TW], lhsT=wv[:, k, sl, c0:c0 + cw], rhs=abv[:, k, t5 * TW:(t5 + 1) * TW],
                                                                   start=(k == 0), stop=(k == KC - 1))
                                        return ins
                                    c.op("pe", mm, [wt, ab], [ps])
                                    epi(ps, pb, c0, cw, tb * TB + t5 * TW)
                    else:
                        for sl, pb in enumerate(blk):
                            wd = pans[pb][1]
                            for tt in range(TB // 128):
                                ps = c.ps()

                                def mm(ps=ps, wv=wv, wd=wd, tt=tt, sl=sl):
                                    ins = None
                                    for k in range(KC):
                                        ins = nc.tensor.matmul(ps[:, 0:wd], lhsT=abv[:, k, tt * 128:(tt + 1) * 128], rhs=wv[:, k, sl, 0:wd],
                                                               start=(k == 0), stop=(k == KC - 1))
                                    return ins
                                c.op("pe", mm, [wt, ab], [ps])
                                epi(ps, pb, 0, wd, tb * TB + tt * 128)
        c.barrier()

    def resid_norm(self, xsrc, ysrc, gain_ap, alpha, xdst):
        c, nc, cfg = self.c, self.nc, self.cfg
        with contextlib.ExitStack() as es:
            g = c.sb(es, [128, D], F32, "gain")
            c.dma("sp", [(g[:], bc_rows(gain_ap, D))], [self.p_normg], [g])
            xt = [c.sb(es, [128, D], F32, "rx") for _ in range(2)]
            yt = [c.sb(es, [128, D], F32, "ry") for _ in range(2)]
            junk = c.sb(es, [128, D], BF16, "rjunk")
            stat = [c.sb(es, [128, 4], F32, "rstat") for _ in range(2)]
            for ti in range(cfg.NT):
                x_, y_, s_ = xt[ti % 2], yt[ti % 2], stat[ti % 2]
                rows = slice(ti * 128, (ti + 1) * 128)
                c.dma("sp", [(y_[:], ysrc.t[rows, :])], [ysrc], [y_])
                c.dma("sp", [(x_[:], xsrc.t[rows, :])], [xsrc], [x_])
                c.op("pool", lambda s_=s_: nc.gpsimd.memset(s_[:], 0.0), [], [s_])
                c.op("act", lambda y_=y_, s_=s_: nc.scalar.activation(out=junk[:], in_=y_[:], func=AF.Square, accum_out=s_[:, 0:1]), [y_, s_], [junk, s_])
                c.op("act", lambda s_=s_: nc.scalar.activation(out=s_[:, 1:2], in_=s_[:, 0:1], func=AF.Sqrt, scale=1.0 / D, bias=self.epsb[:, 0:1]), [s_, self.epsb], [s_])
                c.op("dve", lambda s_=s_: nc.vector.reciprocal(out=s_[:, 2:3], in_=s_[:, 1:2]), [s_], [s_])
                c.op("dve", lambda y_=y_, s_=s_: nc.vector.scalar_tensor_tensor(out=y_[:], in0=y_[:], scalar=s_[:, 2:3], in1=g[:], op0=ALU.mult, op1=ALU.mult), [y_, s_, g], [y_])
                c.op("dve", lambda x_=x_, y_=y_: nc.vector.scalar_tensor_tensor(out=x_[:], in0=y_[:], scalar=float(alpha), in1=x_[:], op0=ALU.mult, op1=ALU.add), [x_, y_], [x_])
                c.dma("sp", [(xdst.t[rows, :], x_[:])], [x_], [xdst])
        c.barrier()

    def ffn(self, l, which, xsrc, xdst):
        c, nc, cfg = self.c, self.nc, self.cfg
        TW = cfg.TW
        gi = 0 if which == "a" else 4
        self.normT(xsrc, self.p_normg.t[l, gi, :], self.hT)
        nb = DFF // 256
        blocks = [[b, nb + b] for b in range(nb)]
        pend = {}

        def pre1(es):
            self.f_sg = [c.sb(es, [128, 512], F32, "fsg") for _ in range(3)]
            self.f_ho = [c.sb(es, [128, 512], BF16, "fho") for _ in range(3)]
            self.f_i = 0

        def epi1(ps, pb, c0, cw, t0):
            if pb < nb:
                pend[(pb, c0, t0)] = ps
                return
            gps = pend.pop((pb - nb, c0, t0))
            gcol = (pb - nb) * 256 + c0
            i = self.f_i
            self.f_i += 1
            sg, ho = self.f_sg[i % 3], self.f_ho[i % 3]
            c.op("act", lambda: nc.scalar.activation(out=sg[:, 0:TW], in_=gps[:, 0:TW], func=AF.Silu), [gps], [sg])
            c.op("dve", lambda: nc.vector.tensor_tensor(out=ho[:, 0:TW], in0=sg[:, 0:TW], in1=ps[:, 0:TW], op=ALU.mult), [sg, ps], [ho])
            c.dma("sp", [(self.hidT.t[gcol:gcol + 128, t0:t0 + TW], ho[:, 0:TW])], [ho], [self.hidT])
        self.gemm(self.hT, D, "f%s_in" % which, l, blocks, "ws", epi1, pre=pre1)

        def pre2(es):
            self.f_yo = [c.sb(es, [128, 512], F32, "fyo") for _ in range(4)]
            self.f_i = 0

        def epi2(ps, pb, c0, wd, t0):
            i = self.f_i
            self.f_i += 1
            yo = self.f_yo[i % 4]
            if i % 2 == 0:
                c.op("act", lambda: nc.scalar.copy(out=yo[:, 0:wd], in_=ps[:, 0:wd]), [ps], [yo])
            else:
                c.op("dve", lambda: nc.vector.tensor_copy(out=yo[:, 0:wd], in_=ps[:, 0:wd]), [ps], [yo])
            c.dma("sp", [(self.ybuf.t[t0:t0 + 128, pb * 512:pb * 512 + wd], yo[:, 0:wd])], [yo], [self.ybuf])
        self.gemm(self.hidT, DFF, "f%s_out" % which, l, [[b] for b in range(D // 512)], "as", epi2, pre=pre2)
        self.resid_norm(xsrc, self.ybuf, self.p_normg.t[l, gi + 1, :], 0.5, xdst)

    def layer(self, l, xcur):
        cfg = self.cfg
        last = (l == cfg.DEPTH - 1)
        self.ffn(l, "a", xcur, self.xa)
        if cfg.stop == "ffn_a":
            return self.xa
        raise NotImplementedError


class KM(K):
    def proj(self, l):
        c, nc, cfg = self.c, self.nc, self.cfg
        TW = cfg.TW
        self.normT(self.xa, self.p_normg.t[l, 2, :], self.hT)
        ws_list = list(range(0, 10)) + list(range(11, 15)) + list(range(19, 23)) + [23, 24, 25, 27] + list(range(30, 54))
        as_list = [10] + list(range(15, 19)) + [26, 28, 29]

        def pre(es):
            self.e_f = [c.sb(es, [128, 512], F32, "ef") for _ in range(3)]
            self.e_b = [c.sb(es, [128, 512], BF16, "eb") for _ in range(3)]
            self.e_i = 0

        def epi_ws(ps, pb, c0, cw, t0):
            i = self.e_i
            self.e_i += 1
            ef, eb = self.e_f[i % 3], self.e_b[i % 3]
            src = ps[0:cw, 0:TW]
            cols = slice(t0, t0 + TW)
            if pb < 4:
                r0 = pb * 512 + c0
                c.op("act", lambda: nc.scalar.activation(out=eb[0:cw, 0:TW], in_=src, func=AF.Silu), [ps], [eb])
                c.dma("sp", [(self.szT.t[r0:r0 + cw, cols], eb[0:cw, 0:TW])], [eb], [self.szT])
            elif pb < 10:
                r0 = (pb - 4) * 512 + c0
                c.op("dve", lambda: nc.vector.tensor_copy(out=ef[0:cw, 0:TW], in_=src), [ps], [ef])
                c.dma("sp", [(self.xbcT.t[r0:r0 + cw, cols], ef[0:cw, 0:TW])], [ef], [self.xbcT])
            elif pb < 15:
                r0 = (pb - 11) * 512 + c0
                c.op("act", lambda: nc.scalar.activation(out=eb[0:cw, 0:TW], in_=src, func=AF.Gelu_apprx_tanh), [ps], [eb])
                c.dma("sp", [(self.uT.t[r0:r0 + cw, cols], eb[0:cw, 0:TW])], [eb], [self.uT])
            elif pb < 23:
                r0 = (pb - 19) * 512 + c0
                c.op("dve", lambda: nc.vector.tensor_copy(out=eb[0:cw, 0:TW], in_=src), [ps], [eb])
                c.dma("sp", [(self.qT.t[r0:r0 + cw, cols], eb[0:cw, 0:TW])], [eb], [self.qT])
            elif pb < 30:
                seg = {23: 0, 24: 1, 25: 2, 27: 3}[pb]
                r0 = seg * 512 + c0
                c.op("act", lambda: nc.scalar.copy(out=eb[0:cw, 0:TW], in_=src), [ps], [eb])
                c.dma("sp", [(self.kvT.t[r0:r0 + cw, cols], eb[0:cw, 0:TW])], [eb], [self.kvT])
            else:
                r0 = (pb - 30) * 512 + c0
                c.op("act", lambda: nc.scalar.activation(out=eb[0:cw, 0:TW], in_=src, func=AF.Sigmoid), [ps], [eb])
                c.dma("sp", [(self.mgT.t[r0:r0 + cw, cols], eb[0:cw, 0:TW])], [eb], [self.mgT])
        self.gemm(self.hT, D, "w_in", l, [[p] for p in ws_list], "ws", epi_ws, pre=pre)

        def epi_as(ps, pb, c0, wd, t0):
            i = self.e_i
            self.e_i += 1
            ef, eb = self.e_f[i % 3], self.e_b[i % 3]
            src = ps[:, 0:wd]
            rows = slice(t0, t0 + 128)
            if pb == 10:
                c.op("dve", lambda: nc.vector.tensor_copy(out=ef[:, 0:wd], in_=src), [ps], [ef])
                c.dma("sp", [(self.dt_tm.t[rows, :], ef[:, 0:wd])], [ef], [self.dt_tm])
            elif pb < 19:
                c.op("act", lambda: nc.scalar.activation(out=ef[:, 0:wd], in_=src, func=AF.Gelu_apprx_tanh), [ps], [ef])
                c.dma("sp", [(self.v_tm.t[rows, (pb - 15) * 512:(pb - 15) * 512 + wd], ef[:, 0:wd])], [ef], [self.v_tm])
            elif pb in (26, 28):
                o = 0 if pb == 26 else 512
                c.op("dve", lambda: nc.vector.tensor_copy(out=eb[:, 0:wd], in_=src), [ps], [eb])
                c.dma("sp", [(self.vtm2.t[rows, o:o + wd], eb[:, 0:wd])], [eb], [self.vtm2])
            else:
                c.op("act", lambda: nc.scalar.activation(out=ef[:, 0:wd], in_=src, func=AF.Sigmoid), [ps], [ef])
                c.dma("sp", [(self.ngate.t[rows, :], ef[:, 0:wd])], [ef], [self.ngate])
        self.gemm(self.hT, D, "w_in", l, [[p] for p in as_list], "as", epi_as, pre=pre)
        T = cfg.T
        c.dma("sp", [(self.halo_loc.t[:, 0:3], self.xbcT.t[:, T - 3:T])], [self.xbcT], [self.halo_loc])
        self.exchange(self.kvT, self.kvT.t, self.g_kvT, self.g_kvT.t, 2048, T * 2)
        self.exchange(self.vtm2, self.vtm2.t, self.g_vtm, self.g_vtm.t, T, 2048)
        self.exchange(self.halo_loc, self.halo_loc.t, self.g_halo, self.g_halo.t, XBC, 16)
        c.barrier()

    def sgu(self, l):
        c, nc, cfg = self.c, self.nc, self.cfg
        with contextlib.ExitStack() as es:
            wmT = c.sb(es, [128, 16 * 128], BF16, "wmT")
            wmv = v3(wmT.t[:], 128)
            bb = c.sb(es, [128, MIX], F32, "sgub")
            gv = c.sb(es, [128, MIX], F32, "sgug")
            c.dma("sp", [(bb[:], bc_rows(self.p_sgub.t[l, :], MIX))], [self.p_sgub], [bb])
            c.dma("sp", [(gv[:], bc_rows(self.p_sgug.t[l, :], MIX))], [self.p_sgug], [gv])
            wf = [c.sb(es, [128, 128], F32, "swf") for _ in range(2)]
            wb = [c.sb(es, [128, 128], BF16, "swb") for _ in range(2)]
            for g in range(16):
                a, b = wf[g % 2], wb[g % 2]
                c.dma("sp", [(a[:], self.p_sguw.t[l, g])], [self.p_sguw], [a])
                c.op("pool", lambda a=a: nc.gpsimd.affine_select(out=a[:], in_=a[:], pattern=[[-1, 128]], compare_op=ALU.is_ge, fill=0.0, base=0, channel_multiplier=1), [a], [a])
                c.op("dve", lambda a=a, b=b: nc.vector.tensor_copy(out=b[:], in_=a[:]), [a], [b])
                ps = c.ps()
                psb = ps.t[:].bitcast(BF16)
                c.op("pe", lambda b=b, psb=psb: nc.tensor.transpose(psb[:, 0:128], b[:], self.ident[:]), [b, self.ident], [ps])
                c.op("act", lambda g=g, psb=psb: nc.scalar.copy(out=wmv[:, g, :], in_=psb[:, 0:128]), [ps], [wmT])
            vt = [c.sb(es, [128, MIX], F32, "sv") for _ in range(2)]
            vn = [c.sb(es, [128, MIX], BF16, "svn") for _ in range(2)]
            ut = [c.sb(es, [128, MIX], BF16, "su") for _ in range(2)]
            yo = [c.sb(es, [128, MIX], BF16, "sy") for _ in range(2)]
            t1 = [c.sb(es, [128, 512], F32, "st1") for _ in range(2)]
            junk = c.sb(es, [128, MIX], BF16, "sjunk")
            stat = [c.sb(es, [128, 4], F32, "sstat") for _ in range(2)]
            k = 0
            for ti in range(cfg.NT):
                v_, n_, u_, y_, s_ = vt[ti % 2], vn[ti % 2], ut[ti % 2], yo[ti % 2], stat[ti % 2]
                tok = slice(ti * 128, (ti + 1) * 128)
                c.dma("sp", [(v_[:], self.v_tm.t[tok, :])], [self.v_tm], [v_])
                c.dma("sp", [(v3(u_.t[:], 128), self.uT.t[:, tok].rearrange("(g d) t -> d g t", d=128))], [self.uT], [u_])
                c.op("pool", lambda s_=s_: nc.gpsimd.memset(s_[:], 0.0), [], [s_])
                c.op("act", lambda v_=v_, s_=s_: nc.scalar.activation(out=junk[:], in_=v_[:], func=AF.Square, accum_out=s_[:, 0:1]), [v_, s_], [junk, s_])
                c.op("act", lambda s_=s_: nc.scalar.activation(out=s_[:, 1:2], in_=s_[:, 0:1], func=AF.Sqrt, scale=1.0 / MIX, bias=self.epsb[:, 0:1]), [s_, self.epsb], [s_])
                c.op("dve", lambda s_=s_: nc.vector.reciprocal(out=s_[:, 2:3], in_=s_[:, 1:2]), [s_], [s_])
                c.op("dve", lambda v_=v_, n_=n_, s_=s_: nc.vector.scalar_tensor_tensor(out=n_[:], in0=v_[:], scalar=s_[:, 2:3], in1=gv[:], op0=ALU.mult, op1=ALU.mult), [v_, s_, gv], [n_])
                for g4 in range(4):
                    ps = c.ps()

                    def mm(ps=ps, n_=n_, g4=g4):
                        ins = None
                        for gg in range(4):
                            g = g4 * 4 + gg
                            ins = nc.tensor.matmul(ps[:, gg * 128:(gg + 1) * 128], lhsT=n_[:, g * 128:(g + 1) * 128], rhs=wmv[:, g, :], start=True, stop=True)
                        return ins
                    c.op("pe", mm, [n_, wmT], [ps])
                    t_ = t1[k % 2]
                    k += 1
                    cs = slice(g4 * 512, (g4 + 1) * 512)
                    c.op("dve", lambda ps=ps, t_=t_, cs=cs: nc.vector.tensor_tensor(out=t_[:], in0=ps[:, 0:512], in1=bb[:, cs], op=ALU.add), [ps, bb], [t_])
                    c.op("pool", lambda t_=t_, y_=y_, u_=u_, cs=cs: nc.gpsimd.tensor_tensor(out=y_[:, cs], in0=t_[:], in1=u_[:, cs], op=ALU.mult), [t_, u_], [y_])
                c.dma("sp", [(self.ybT.t[:, tok].rearrange("(g d) t -> d g t", d=128), v3(y_.t[:], 128))], [y_], [self.ybT])
        c.barrier()

    def ssd(self, l):
        c, nc, cfg = self.c, self.nc, self.cfg
        T, NT = cfg.T, cfg.NT
        with contextlib.ExitStack() as es:
            pc = c.sb(es, [128, 24 * 5], F32, "pc")
            pcv = v3(pc.t[:], 5)
            c.dma("sp", [(pcv, self.p_conv.t[l].rearrange("(cc p) k -> p cc k", p=128))], [self.p_conv], [pc])
            hl = c.sb(es, [128, 8 * 24 * 4], F32, "hl")
            hlv = hl.t[:].rearrange("p (r cc k) -> p r cc k", r=8, k=4)
            pairs = []
            for rank in range(8):
                pairs.append((hlv[:, rank], self.g_halo.t[0, rank % 2, rank // 2].rearrange("(cc p) k -> p cc k", p=128)))
            c.dma("sp", pairs, [self.g_halo], [hl])
            hs = c.sb(es, [128, 24 * 4], F32, "hs")
            hl2 = hl.t[:].rearrange("p (r x) -> p r x", r=8)
            c.op("dve", lambda: nc.vector.tensor_scalar(out=hs[:], in0=hl2[:, 0, :], scalar1=self.cinfo[:, 2:3], scalar2=None, op0=ALU.mult), [hl, self.cinfo], [hs])
            for rank in range(1, 8):
                c.op("dve", lambda rank=rank: nc.vector.scalar_tensor_tensor(out=hs[:], in0=hl2[:, rank, :], scalar=self.cinfo[:, 2 + rank:3 + rank], in1=hs[:], op0=ALU.mult, op1=ALU.add), [hl, hs, self.cinfo], [hs])
            hsv = v3(hs.t[:], 4)
            ut = [c.sb(es, [128, T + 4], F32, "cu") for _ in range(2)]
            acc = [c.sb(es, [128, T], F32, "cacc") for _ in range(2)]
            co = [c.sb(es, [128, T], BF16, "cout") for _ in range(2)]
            for cc in range(24):
                u_, a_, o_ = ut[cc % 2], acc[cc % 2], co[cc % 2]
                c.dma("sp", [(u_[:, 3:T + 3], self.xbcT.t[cc * 128:(cc + 1) * 128, :])], [self.xbcT], [u_])
                c.op("act", lambda u_=u_, cc=cc: nc.scalar.copy(out=u_[:, 0:3], in_=hsv[:, cc, 0:3]), [hs], [u_])
                c.op("dve", lambda u_=u_, a_=a_, cc=cc: nc.vector.tensor_scalar(out=a_[:], in0=u_[:, 0:T], scalar1=pcv[:, cc, 0:1], scalar2=None, op0=ALU.mult), [u_, pc], [a_])
                for k in range(1, 4):
                    c.op("dve", lambda u_=u_, a_=a_, cc=cc, k=k: nc.vector.scalar_tensor_tensor(out=a_[:], in0=u_[:, k:T + k], scalar=pcv[:, cc, k:k + 1], in1=a_[:], op0=ALU.mult, op1=ALU.add), [u_, a_, pc], [a_])
                c.op("act", lambda a_=a_, o_=o_, cc=cc: nc.scalar.activation(out=o_[:], in_=a_[:], func=AF.Silu, bias=pcv[:, cc, 4:5]), [a_, pc], [o_])
                c.dma("sp", [(self.convT.t[cc * 128:(cc + 1) * 128, :], o_[:])], [o_], [self.convT])
        c.barrier()
        for full in (False, True):
            with contextlib.ExitStack() as es:
                self._ssd_scan(es, l, full)
            c.barrier()
            if not full:
                self._ssd_combine(l)

    def _ssd_consts(self, es, l):
        c, nc = self.c, self.nc
        k = {}
        U = k["U"] = c.sb(es, [128, 128], F32, "U")
        c.op("pool", lambda: nc.gpsimd.memset(U[:], 1.0), [], [U])
        c.op("pool", lambda: nc.gpsimd.affine_select(out=U[:], in_=U[:], pattern=[[1, 128]], compare_op=ALU.is_ge, fill=0.0, base=0, channel_multiplier=-1), [U], [U])
        onesf = k["ones"] = c.sb(es, [128, 128], F32, "onesf")
        c.op("pool", lambda: nc.gpsimd.memset(onesf[:], 1.0), [], [onesf])
        mneg = k["mneg"] = c.sb(es, [128, 128], F32, "mneg")
        c.op("pool", lambda: nc.gpsimd.memset(mneg[:], 0.0), [], [mneg])
        c.op("pool", lambda: nc.gpsimd.affine_select(out=mneg[:], in_=mneg[:], pattern=[[1, 128]], compare_op=ALU.is_ge, fill=NEG, base=0, channel_multiplier=-1), [mneg], [mneg])
        one1 = k["one1"] = c.sb(es, [128, 1], F32, "one1")
        c.op("pool", lambda: nc.gpsimd.memset(one1[:], 1.0), [], [one1])
        hp = c.sb(es, [128, 3 * 32], F32, "hp")
        c.dma("sp", [(hp[:], bc_rows(self.p_head.t[l].rearrange("a h -> (a h)"), 96))], [self.p_head], [hp])
        k["hp"] = hp
        Ab = k["Ab"] = c.sb(es, [128, 32], F32, "Ab")
        c.op("act", lambda: nc.scalar.activation(out=Ab[:], in_=hp[:, 0:32], func=AF.Exp), [hp], [Ab])
        c.op("dve", lambda: nc.vector.tensor_scalar(out=Ab[:], in0=Ab[:], scalar1=-1.0, scalar2=None, op0=ALU.mult), [Ab], [Ab])
        gP = k["gP"] = c.sb(es, [64, 32], F32, "gP")
        c.dma("sp", [(gP[:], self.p_ssdg.t[l])], [self.p_ssdg], [gP])
        return k

    def _ssd_scan(self, es, l, full):
        c, nc, cfg = self.c, self.nc, self.cfg
        T, NT = cfg.T, cfg.NT
        k = self._ssd_consts(es, l)
        U, onesf, mneg, one1, hp, Ab, gP = k["U"], k["ones"], k["mneg"], k["one1"], k["hp"], k["Ab"], k["gP"]
        sb = lambda shape, dt, nm: c.sb(es, shape, dt, nm)
        S = sb([128, MIX], F32, "S")
        Sb = sb([128, MIX], BF16, "Sb")
        acst = sb([128, 32], F32, "acst")
        if full:
            c.dma("sp", [(S[:], self.s_init.t)], [self.s_init], [S])
        else:
            c.op("pool", lambda: nc.gpsimd.memset(S[:], 0.0), [], [S])
        c.op("pool", lambda: nc.gpsimd.memset(acst[:], 0.0), [], [acst])
        c.op("act", lambda: nc.scalar.copy(out=Sb[:], in_=S[:]), [S], [Sb])
        dtr = sb([128, 32], F32, "dtr")
        dt = sb([128, 32], F32, "dt")
        a = sb([128, 32], F32, "a")
        acum = sb([128, 32], F32, "acum")
        dte = sb([128, 32], F32, "dte")
        dec = sb([128, 32], F32, "dec")
        Dm = sb([128, 4096], F32, "Dm")
        Rs = sb([128, 4096], F32, "Rs")
        xsT = sb([128, 16 * 128], BF16, "xsT")
        BT = sb([128, 4 * 128], BF16, "BT")
        xs_tm = sb([128, MIX], BF16, "xs_tm")
        B_tm = sb([128, 512], BF16, "B_tm")
        xdt = sb([128, MIX], BF16, "xdt")
        xdte = sb([128, MIX], BF16, "xdte")
        if full:
            CT = sb([128, 4 * 128], BF16, "CT")
            LT = sb([128, 4096], F32, "LT")
            ER = sb([128, 4096], F32, "ER")
            MT = sb([128, 4096], BF16, "MT")
            Cp = sb([128, 4096], BF16, "Cp")
            yT = sb([64, 4096], F32, "yT")
            xsP = sb([64, 4096], BF16, "xsP")
            szP = sb([64, 4096], BF16, "szP")
            tmp = sb([64, 4096], F32, "tmpP")
            ssum = sb([64, 4096], F32, "ssum")
            ssq = sb([64, 512], F32, "ssq")
            yo = sb([64, 4096], BF16, "yoP")
        Dm3, Rs3 = v3(Dm.t[:], 128), v3(Rs.t[:], 128)
        for j in range(NT):
            tok = slice(j * 128, (j + 1) * 128)
            c.dma("sp", [(dtr[:], self.dt_tm.t[tok, :])], [self.dt_tm], [dtr])
            c.dma("sp", [(v3(BT.t[:], 128), self.convT.t[2048:2560, tok].rearrange("(g n) s -> n g s", n=128))], [self.convT], [BT])
            c.dma("sp", [(v3(xsT.t[:], 128), self.convT.t[0:2048, tok].rearrange("(f p) s -> p f s", p=128))], [self.convT], [xsT])
            c.op("dve", lambda: nc.vector.tensor_tensor(out=dt[:], in0=dtr[:], in1=hp[:, 32:64], op=ALU.add), [dtr, hp], [dt])
            c.op("act", lambda: nc.scalar.activation(out=dt[:], in_=dt[:], func=AF.Exp), [dt], [dt])
            c.op("act", lambda: nc.scalar.activation(out=dt[:], in_=dt[:], func=AF.Ln, bias=one1[:, 0:1]), [dt, one1], [dt])
            c.op("dve", lambda: nc.vector.tensor_tensor(out=a[:], in0=dt[:], in1=Ab[:], op=ALU.mult), [dt, Ab], [a])
            ps = c.ps()
            c.op("pe", lambda ps=ps: nc.tensor.matmul(ps[:, 0:32], lhsT=U[:], rhs=a[:], start=True, stop=True), [U, a], [ps])
            c.op("dve", lambda ps=ps: nc.vector.tensor_copy(out=acum[:], in_=ps[:, 0:32]), [ps], [acum])
            c.op("dve", lambda: nc.vector.tensor_tensor(out=Dm3, in0=U[:].unsqueeze(1).to_broadcast([128, 32, 128]), in1=a[:].unsqueeze(2).to_broadcast([128, 32, 128]), op=ALU.mult), [U, a], [Dm])
            for b8 in range(8):
                ps = c.ps()
                cs = slice(b8 * 512, (b8 + 1) * 512)
                c.op("pe", lambda ps=ps, cs=cs: nc.tensor.matmul(ps[:, 0:512], lhsT=onesf[:], rhs=Dm[:, cs], start=True, stop=True), [onesf, Dm], [ps])
                if b8 % 2 == 0:
                    c.op("act", lambda ps=ps, cs=cs: nc.scalar.copy(out=Rs[:, cs], in_=ps[:, 0:512]), [ps], [Rs])
                else:
                    c.op("dve", lambda ps=ps, cs=cs: nc.vector.tensor_copy(out=Rs[:, cs], in_=ps[:, 0:512]), [ps], [Rs])
            c.op("dve", lambda: nc.vector.tensor_tensor(out=dte[:], in0=Rs3[:, :, 127], in1=acum[:], op=ALU.subtract), [Rs, acum], [dte])
            c.op("act", lambda: nc.scalar.activation(out=dte[:], in_=dte[:], func=AF.Exp), [dte], [dte])
            c.op("act", lambda: nc.scalar.activation(out=dec[:], in_=Rs3[:, :, 127], func=AF.Exp), [Rs], [dec])
            c.op("dve", lambda: nc.vector.tensor_tensor(out=acst[:], in0=acst[:], in1=Rs3[:, :, 127], op=ALU.add), [acst, Rs], [acst])
            for half in range(2):
                ps = c.ps()
                psb = ps.t[:].bitcast(BF16)

                def tr(psb=psb, half=half):
                    ins = None
                    for q in range(8):
                        f = half * 8 + q
                        ins = nc.tensor.transpose(psb[:, q * 128:(q + 1) * 128], xsT[:, f * 128:(f + 1) * 128], self.ident[:])
                    return ins
                c.op("pe", tr, [xsT, self.ident], [ps])
                c.op("act", lambda psb=psb, half=half: nc.scalar.copy(out=xs_tm[:, half * 1024:(half + 1) * 1024], in_=psb[:, 0:1024]), [ps], [xs_tm])
            ps = c.ps()
            psb = ps.t[:].bitcast(BF16)

            def trb(psb=psb):
                ins = None
                for g in range(4):
                    ins = nc.tensor.transpose(psb[:, g * 128:(g + 1) * 128], BT[:, g * 128:(g + 1) * 128], self.ident[:])
                return ins
            c.op("pe", trb, [BT, self.ident], [ps])
            c.op("dve", lambda psb=psb: nc.vector.tensor_copy(out=B_tm[:], in_=psb[:, 0:512]), [ps], [B_tm])
            c.op("dve", lambda: nc.vector.tensor_tensor(out=v3(xdt.t[:], 64), in0=v3(xs_tm.t[:], 64), in1=dt[:].unsqueeze(2).to_broadcast([128, 32, 64]), op=ALU.mult), [xs_tm, dt], [xdt])
            c.op("pool", lambda: nc.gpsimd.tensor_tensor(out=v3(xdte.t[:], 64), in0=v3(xdt.t[:], 64), in1=dte[:].unsqueeze(2).to_broadcast([128, 32, 64]), op=ALU.mult), [xdt, dte], [xdte])
            if full:
                c.dma("sp", [(v3(CT.t[:], 128), self.convT.t[2560:3072, tok].rearrange("(g n) s -> n g s", n=128))], [self.convT], [CT])
                c.dma("sp", [(v3(xsP.t[:], 128), self.convT.t[0:2048, tok].rearrange("(h p) s -> p h s", p=64))], [self.convT], [xsP])
                c.dma("sp", [(v3(szP.t[:], 128), self.szT.t[:, tok].rearrange("(h p) s -> p h s", p=64))], [self.szT], [szP])
                c.op("dve", lambda: nc.vector.tensor_tensor(out=v3(LT.t[:], 128), in0=Rs3, in1=acum[:].unsqueeze(2).to_broadcast([128, 32, 128]), op=ALU.subtract), [Rs, acum], [LT])
                c.op("pool", lambda: nc.gpsimd.tensor_tensor(out=v3(LT.t[:], 128), in0=v3(LT.t[:], 128), in1=mneg[:].unsqueeze(1).to_broadcast([128, 32, 128]), op=ALU.add), [LT, mneg], [LT])
                c.op("act", lambda: nc.scalar.activation(out=LT[:], in_=LT[:], func=AF.Exp), [LT], [LT])
                c.op("act", lambda: nc.scalar.activation(out=ER[:], in_=Rs[:], func=AF.Exp), [Rs], [ER])
                pcb = c.ps()

                def cb(pcb=pcb):
                    ins = None
                    for g in range(4):
                        ins = nc.tensor.matmul(pcb[:, g * 128:(g + 1) * 128], lhsT=BT[:, g * 128:(g + 1) * 128], rhs=CT[:, g * 128:(g + 1) * 128], start=True, stop=True)
                    return ins
                c.op("pe", cb, [BT, CT], [pcb])
                cbv = pcb.t[:, 0:512].rearrange("p (g l) -> p g l", l=128).unsqueeze(2).to_broadcast([128, 4, 8, 128])
                c.op("dve", lambda cbv=cbv: nc.vector.tensor_tensor(out=MT.t[:].rearrange("p (g k l) -> p g k l", g=4, k=8), in0=LT.t[:].rearrange("p (g k l) -> p g k l", g=4, k=8), in1=cbv, op=ALU.mult), [LT, pcb], [MT])
                ctv = CT.t[:].rearrange("p (g l) -> p g l", l=128).unsqueeze(2).to_broadcast([128, 4, 8, 128])
                c.op("pool", lambda ctv=ctv: nc.gpsimd.tensor_tensor(out=Cp.t[:].rearrange("p (g k l) -> p g k l", g=4, k=8), in0=ER.t[:].rearrange("p (g k l) -> p g k l", g=4, k=8), in1=ctv, op=ALU.mult), [ER, CT], [Cp])
                for h4 in range(8):
                    ps = c.ps()

                    def ymm(ps=ps, h4=h4):
                        ins = None
                        for hh in range(4):
                            h = h4 * 4 + hh
                            o = ps[0:64, hh * 128:(hh + 1) * 128]
                            nc.tensor.matmul(o, lhsT=xdt[:, h * 64:(h + 1) * 64], rhs=MT[:, h * 128:(h + 1) * 128], start=True, stop=False)
                            ins = nc.tensor.matmul(o, lhsT=Sb[:, h * 64:(h + 1) * 64], rhs=Cp[:, h * 128:(h + 1) * 128], start=False, stop=True)
                        return ins
                    c.op("pe", ymm, [xdt, MT, Sb, Cp], [ps])
                    cs = slice(h4 * 512, (h4 + 1) * 512)
                    if h4 % 2 == 0:
                        c.op("act", lambda ps=ps, cs=cs: nc.scalar.copy(out=yT[:, cs], in_=ps[0:64, 0:512]), [ps], [yT])
                    else:
                        c.op("dve", lambda ps=ps, cs=cs: nc.vector.tensor_copy(out=yT[:, cs], in_=ps[0:64, 0:512]), [ps], [yT])
            psl = []
            for g in range(4):
                ps = c.ps()
                c.op("pe", lambda ps=ps, g=g: nc.tensor.matmul(ps[:, 0:512], lhsT=B_tm[:, g * 128:(g + 1) * 128], rhs=xdte[:, g * 512:(g + 1) * 512], start=True, stop=True), [B_tm, xdte], [ps])
                psl.append(ps)
            c.op("dve", lambda: nc.vector.tensor_tensor(out=v3(S.t[:], 64), in0=v3(S.t[:], 64), in1=dec[:].unsqueeze(2).to_broadcast([128, 32, 64]), op=ALU.mult), [S, dec, Sb], [S])
            for g in range(4):
                cs = slice(g * 512, (g + 1) * 512)
                c.op("dve", lambda g=g, cs=cs: nc.vector.tensor_tensor(out=S[:, cs], in0=S[:, cs], in1=psl[g][:, 0:512], op=ALU.add), [S, psl[g]], [S])
            c.op("act", lambda: nc.scalar.copy(out=Sb[:], in_=S[:]), [S], [Sb])
            if full:
                c.op("dve", lambda: nc.vector.tensor_tensor(out=v3(tmp.t[:], 128), in0=v3(xsP.t[:], 128), in1=hp[0:64, 64:96].unsqueeze(2).to_broadcast([64, 32, 128]), op=ALU.mult), [xsP, hp], [tmp])
                c.op("dve", lambda: nc.vector.tensor_tensor(out=yT[:], in0=yT[:], in1=tmp[:], op=ALU.add), [yT, tmp], [yT])
                c.op("dve", lambda: nc.vector.tensor_tensor(out=yT[:], in0=yT[:], in1=szP[:], op=ALU.mult), [yT, szP], [yT])
                c.op("pool", lambda: nc.gpsimd.tensor_tensor(out=tmp[:], in0=yT[:], in1=yT[:], op=ALU.mult), [yT], [tmp])
                for b8 in range(8):
                    ps = c.ps()
                    cs = slice(b8 * 512, (b8 + 1) * 512)
                    c.op("pe", lambda ps=ps, cs=cs: nc.tensor.matmul(ps[0:64, 0:512], lhsT=onesf[0:64, 0:64], rhs=tmp[:, cs], start=True, stop=True), [onesf, tmp], [ps])
                    c.op("act", lambda ps=ps, cs=cs: nc.scalar.copy(out=ssum[:, cs], in_=ps[0:64, 0:512]), [ps], [ssum])
                c.op("dve", lambda: nc.vector.tensor_reduce(out=v3(ssq.t[:], 128), in_=ssum.t[:].rearrange("p (g k l) -> p g l k", g=4, k=8), axis=AX.X, op=ALU.add), [ssum], [ssq])
                c.op("act", lambda: nc.scalar.activation(out=ssq[:], in_=ssq[:], func=AF.Sqrt, scale=1.0 / 512, bias=self.epsb[0:64, 0:1]), [ssq, self.epsb], [ssq])
                c.op("dve", lambda: nc.vector.reciprocal(out=ssq[:], in_=ssq[:]), [ssq], [ssq])
                rsv = ssq.t[:].rearrange("p (g l) -> p g l", l=128).unsqueeze(2).to_broadcast([64, 4, 8, 128])
                c.op("dve", lambda rsv=rsv: nc.vector.tensor_tensor(out=tmp.t[:].rearrange("p (g k l) -> p g k l", g=4, k=8), in0=yT.t[:].rearrange("p (g k l) -> p g k l", g=4, k=8), in1=rsv, op=ALU.mult), [yT, ssq], [tmp])
                c.op("pool", lambda: nc.gpsimd.tensor_tensor(out=v3(yo.t[:], 128), in0=v3(tmp.t[:], 128), in1=gP[:].unsqueeze(2).to_broadcast([64, 32, 128]), op=ALU.mult), [tmp, gP], [yo])
                c.dma("sp", [(self.yaT.t[:, tok].rearrange("(h p) s -> p h s", p=64), v3(yo.t[:], 128))], [yo], [self.yaT])
        if not full:
            c.dma("sp", [(self.s_loc.t, S[:])], [S], [self.s_loc])
            c.dma("sp", [(self.p_loc.t, acst[0:16, :])], [acst], [self.p_loc])

    def _ssd_combine(self, l):
        c, nc, cfg = self.c, self.nc, self.cfg
        self.exchange(self.s_loc, self.s_loc.t, self.g_sloc, self.g_sloc.t, 128, MIX * 4)
        self.exchange(self.p_loc, self.p_loc.t, self.g_ploc, self.g_ploc.t, 16, 128)
        with contextlib.ExitStack() as es:
            H = c.sb(es, [128, MIX], F32, "H")
            acc = c.sb(es, [128, MIX], F32, "Hacc")
            Sr = [c.sb(es, [128, MIX], F32, "Sr") for _ in range(2)]
            Pr = [c.sb(es, [128, 32], F32, "Pr") for _ in range(2)]
            c.op("pool", lambda: nc.gpsimd.memset(H[:], 0.0), [], [H])
            c.op("pool", lambda: nc.gpsimd.memset(acc[:], 0.0), [], [acc])
            for rank in range(8):
                q, r = rank % 2, rank // 2
                s_, p_ = Sr[rank % 2], Pr[rank % 2]
                c.op("dve", lambda rank=rank: nc.vector.scalar_tensor_tensor(out=acc[:], in0=H[:], scalar=self.cinfo[:, 10 + rank:11 + rank], in1=acc[:], op0=ALU.mult, op1=ALU.add), [H, acc, self.cinfo], [acc])
                if rank == 7:
                    break
                c.dma("sp", [(s_[0:64, :], self.g_sloc.t[0, q, r]), (s_[64:128, :], self.g_sloc.t[1, q, r])], [self.g_sloc], [s_])
                c.dma("sp", [(p_[:], bc_rows(self.g_ploc.t[0, q, r, 0, :], 32))], [self.g_ploc], [p_])
                c.op("act", lambda p_=p_: nc.scalar.activation(out=p_[:], in_=p_[:], func=AF.Exp), [p_], [p_])
                c.op("dve", lambda p_=p_: nc.vector.tensor_tensor(out=v3(H.t[:], 64), in0=v3(H.t[:], 64), in1=p_[:].unsqueeze(2).to_broadcast([128, 32, 64]), op=ALU.mult), [H, p_], [H])
                c.op("dve", lambda s_=s_: nc.vector.tensor_tensor(out=H[:], in0=H[:], in1=s_[:], op=ALU.add), [H, s_], [H])
            c.dma("sp", [(self.s_init.t, acc[:])], [acc], [self.s_init])
        c.barrier()

    def kv_piece(self, seg, g, rank):
        row0 = seg * 512 + g * 128
        return self.g_kvT.t[row0 // self.kv_cr, rank % 2, rank // 2, (row0 % self.kv_cr):(row0 % self.kv_cr) + 128, :]

    def nsa(self, l):
        c, nc, cfg = self.c, self.nc, self.cfg
        T, NT, S = cfg.T, cfg.NT, cfg.S
        NCK, NCMP, NBLK = S // 128, S // 16, S // 64
        NCMPC = NCMP // 128
        NB = min(512, NCMP)
        W = 129 + NBLK
        SCALE = 128 ** -0.5
        acc = [c.psb[k] for k in range(4)]
        c.psr = (4, 8)
        with contextlib.ExitStack() as es0:
            sb0 = lambda shape, dt, nm: c.sb(es0, shape, dt, nm)
            I4 = sb0([128, 512], BF16, "I4")
            for k in range(4):
                c.op("dve", lambda k=k: nc.vector.tensor_copy(out=I4[:, k * 128:(k + 1) * 128], in_=self.ident[:]), [self.ident], [I4])
            zf = sb0([128, 128], F32, "zf")
            Atri = sb0([128, 128], BF16, "Atri")
            Astr = sb0([128, 128], BF16, "Astr")
            hvA = sb0([128, 128], BF16, "hvA")
            AstrH = sb0([128, 128], BF16, "AstrH")
            c.op("pool", lambda: nc.gpsimd.memset(zf[:], 0.0), [], [zf])
            c.op("pool", lambda: nc.gpsimd.affine_select(out=zf[:], in_=zf[:], pattern=[[-1, 128]], compare_op=ALU.is_ge, fill=NEG, base=0, channel_multiplier=1), [zf], [zf])
            c.op("dve", lambda: nc.vector.tensor_copy(out=Atri[:], in_=zf[:]), [zf], [Atri])
            c.op("pool", lambda: nc.gpsimd.memset(zf[:], 0.0), [Atri], [zf])
            c.op("pool", lambda: nc.gpsimd.affine_select(out=zf[:], in_=zf[:], pattern=[[1, 128]], compare_op=ALU.is_gt, fill=NEG, base=0, channel_multiplier=-1), [zf], [zf])
            c.op("dve", lambda: nc.vector.tensor_copy(out=Astr[:], in_=zf[:]), [zf], [Astr])
            hvc = sb0([128, 1], F32, "hvc")
            c.op("dve", lambda: nc.vector.tensor_scalar(out=hvc[:], in0=self.cinfo[:, 18:19], scalar1=-NEG, scalar2=NEG, op0=ALU.mult, op1=ALU.add), [self.cinfo], [hvc])
            c.op("dve", lambda: nc.vector.tensor_copy(out=hvA[:], in_=hvc[:, 0:1].to_broadcast([128, 128])), [hvc], [hvA])
            c.op("dve", lambda: nc.vector.tensor_scalar(out=AstrH[:], in0=zf[:], scalar1=hvc[:, 0:1], scalar2=None, op0=ALU.add), [zf, hvc], [AstrH])
            patt = sb0([128, NCMP], F32, "patt")
            c.op("pool", lambda: nc.gpsimd.iota(patt[:], pattern=[[16, NCMP]], base=31, channel_multiplier=-1, allow_small_or_imprecise_dtypes=True), [], [patt])
            Jt = sb0([128, NBLK], F32, "Jt")
            c.op("pool", lambda: nc.gpsimd.iota(Jt[:], pattern=[[1, NBLK]], base=0, channel_multiplier=0, allow_small_or_imprecise_dtypes=True), [], [Jt])
            pidx = sb0([128, 1], F32, "pidx")
            c.op("pool", lambda: nc.gpsimd.iota(pidx[:], pattern=[[0, 1]], base=0, channel_multiplier=1, allow_small_or_imprecise_dtypes=True), [], [pidx])
            half = sb0([128, 1], F32, "half")
            c.op("dve", lambda: nc.vector.tensor_scalar(out=half[:], in0=pidx[:], scalar1=64.0, scalar2=None, op0=ALU.is_ge), [pidx], [half])
            thr = sb0([128, NT], F32, "thr")
            blk0 = sb0([128, NT], F32, "blk0")
            curc = sb0([128, NT], F32, "curc")
            for i in range(NT):
                c.op("dve", lambda i=i: nc.vector.tensor_scalar(out=thr[:, i:i + 1], in0=self.cinfo[:, 1:2], scalar1=float(128 * i), scalar2=None, op0=ALU.add), [self.cinfo], [thr])
                c.op("dve", lambda i=i: nc.vector.tensor_scalar(out=blk0[:, i:i + 1], in0=self.cinfo[:, 1:2], scalar1=1.0 / 64, scalar2=float(2 * i), op0=ALU.mult, op1=ALU.add), [self.cinfo], [blk0])
            c.op("dve", lambda: nc.vector.tensor_scalar(out=curc[:], in0=blk0[:], scalar1=half[:, 0:1], scalar2=None, op0=ALU.add), [blk0, half], [curc])
            ovc = sb0([128, NCMPC * NBLK], BF16, "ovc")
            with contextlib.ExitStack() as est:
                ovf = c.sb(est, [128, NCMPC * NBLK], F32, "ovf")
                ov2 = c.sb(est, [128, NCMPC * NBLK], F32, "ov2")
                c.op("pool", lambda: nc.gpsimd.iota(ovf[:], pattern=[[2048, NCMPC], [-64, NBLK]], base=0, channel_multiplier=16, allow_small_or_imprecise_dtypes=True), [], [ovf])
                c.op("dve", lambda: nc.vector.tensor_scalar(out=ov2[:], in0=ovf[:], scalar1=64.0, scalar2=None, op0=ALU.is_lt), [ovf], [ov2])
                c.op("dve", lambda: nc.vector.tensor_scalar(out=ovf[:], in0=ovf[:], scalar1=-32.0, scalar2=None, op0=ALU.is_gt), [ovf], [ovf])
                c.op("dve", lambda: nc.vector.tensor_tensor(out=ovc[:], in0=ovf[:], in1=ov2[:], op=ALU.mult), [ovf, ov2], [ovc])
            c.barrier()
            posf = sb0([128, 64], F32, "posf")
            posb = sb0([128, 64], BF16, "posb")
            c.dma("sp", [(v3(posf.t[:], 32), self.p_cpos.t[l].rearrange("a d l -> d a l"))], [self.p_cpos], [posf])
            c.op("dve", lambda: nc.vector.tensor_copy(out=posb[:], in_=posf[:]), [posf], [posb])
            w2f = sb0([128, 256], F32, "w2f")
            w2b = sb0([128, 256], BF16, "w2b")
            c.dma("sp", [(v3(w2f.t[:], 128), self.p_cw2.t[l].rearrange("a h d -> h a d"))], [self.p_cw2], [w2f])
            c.op("dve", lambda: nc.vector.tensor_copy(out=w2b[:], in_=w2f[:]), [w2f], [w2b])
            w1 = []
            for kvi, nm in enumerate(("cw1k", "cw1v")):
                w = sb0([128, 32 * 128], BF16, "w1" + nm)
                wv = v3(w.t[:], 128)
                pairs = [(wv[:, 4 * rank:4 * rank + 4, :], self.wpiece(nm, l, 0, rank).rearrange("(l d) h -> d l h", d=128)) for rank in range(8)]
                c.dma("sp", pairs, [self.wfull[nm, l]], [w])
                w1.append(w)
            cbias = sb0([128, 2], F32, "cbias")
            for kvi in range(2):
                ps = c.ps()
                wv = v3(w1[kvi].t[:], 128)

                def bm(ps=ps, wv=wv, kvi=kvi):
                    ins = None
                    for ll in range(32):
                        ins = nc.tensor.matmul(ps[:, 0:1], lhsT=wv[:, ll, :], rhs=posb[:, kvi * 32 + ll:kvi * 32 + ll + 1], start=(ll == 0), stop=(ll == 31))
                    return ins
                c.op("pe", bm, [w1[kvi], posb], [ps])
                c.op("dve", lambda ps=ps, kvi=kvi: nc.vector.tensor_copy(out=cbias[:, kvi:kvi + 1], in_=ps[:, 0:1]), [ps], [cbias])
            for g in range(4):
                with contextlib.ExitStack() as esg:
                    sbg = lambda shape, dt, nm: c.sb(esg, shape, dt, nm)
                    kcmpT = sbg([128, NCMP], BF16, "kcmpT")
                    vcmpA = sbg([128, NCMPC * W], BF16, "vcmpA")
                    vcv = v3(vcmpA.t[:], W)
                    c.op("pool", lambda: nc.gpsimd.memset(vcv[:, :, 128:129], 1.0), [], [vcmpA])
                    c.op("dve", lambda: nc.vector.tensor_copy(out=vcv[:, :, 129:W], in_=v3(ovc.t[:], NBLK)), [ovc], [vcmpA])
                    with contextlib.ExitStack() as esa:
                        raws = []
                        for seg in range(2):
                            raw = c.sb(esa, [128, S + 32], BF16, "raw")
                            c.op("pool", lambda raw=raw: nc.gpsimd.memset(raw[:, S:S + 32], 0.0), [], [raw])
                            pairs = [(raw[:, rank * T:(rank + 1) * T], self.kv_piece(seg, g, rank)) for rank in range(8)]
                            c.dma("sp", pairs, [self.g_kvT], [raw])
                            raws.append(raw)
                        hid = [c.sb(esa, [128, 512], BF16, "hid") for _ in range(2)]
                        hi = 0
                        for kvi in range(2):
                            raw = raws[kvi]
                            wv = v3(w1[kvi].t[:], 128)
                            for nb in range(NCMP // NB):
                                ps = c.ps()
                                rb = raw[:, 0:1]

                                def hm(ps=ps, wv=wv, rb=rb, nb=nb):
                                    ins = None
                                    for ll in range(32):
                                        rhs = bass.AP(rb.tensor, rb.offset + nb * NB * 16 + ll, [list(rb.ap[0]), [16, NB]])
                                        ins = nc.tensor.matmul(ps[:, 0:NB], lhsT=wv[:, ll, :], rhs=rhs, start=(ll == 0), stop=(ll == 31))
                                    return ins
                                c.op("pe", hm, [w1[kvi], raw], [ps])
                                h_ = hid[hi % 2]
                                hi += 1
                                if g == 0 and kvi == 0 and nb == 0 and "dbg_hid" in cfg.dbg:
                                    dh_ = self.dtile("dbg_hid", [128, 512], F32)
                                    dw_ = self.dtile("dbg_w1", [128, 4096], BF16)
                                    dr_ = self.dtile("dbg_raw", [128, 2048], BF16)
                                    db_ = self.dtile("dbg_cb", [128, 2], F32)
                                    hf_ = c.sb(esa, [128, 512], F32, "hf_")
                                    c.op("dve", lambda ps=ps: nc.vector.tensor_copy(out=hf_[:], in_=ps[:, 0:512]), [ps], [hf_])
                                    c.dma("sp", [(dh_.t, hf_[:])], [hf_], [dh_])
                                    c.dma("sp", [(dw_.t, w1[0][:])], [w1[0]], [dw_])
                                    c.dma("sp", [(dr_.t, raw[:, 0:2048])], [raw], [dr_])
                                    c.dma("sp", [(db_.t, cbias[:])], [cbias], [db_])
                                c.op("act", lambda ps=ps, h_=h_, kvi=kvi: nc.scalar.activation(out=h_[:, 0:NB], in_=ps[:, 0:NB], func=AF.Gelu_apprx_tanh, bias=cbias[:, kvi:kvi + 1]), [ps, cbias], [h_])
                                if kvi == 0:
                                    ps2 = c.ps()
                                    c.op("pe", lambda ps2=ps2, h_=h_: nc.tensor.matmul(ps2[:, 0:NB], lhsT=w2b[:, 0:128], rhs=h_[:, 0:NB], start=True, stop=True), [w2b, h_], [ps2])
                                    c.op("dve", lambda ps2=ps2, nb=nb: nc.vector.tensor_copy(out=kcmpT[:, nb * NB:(nb + 1) * NB], in_=ps2[:, 0:NB]), [ps2], [kcmpT])
                                else:
                                    ps2 = c.ps()

                                    def vm(ps2=ps2, h_=h_):
                                        ins = None
                                        for sub in range(NB // 128):
                                            ins = nc.tensor.matmul(ps2[:, sub * 128:(sub + 1) * 128], lhsT=h_[:, sub * 128:(sub + 1) * 128], rhs=w2b[:, 128:256], start=True, stop=True)
                                        return ins
                                    c.op("pe", vm, [w2b, h_], [ps2])
                                    c.op("dve", lambda ps2=ps2, nb=nb: nc.vector.tensor_copy(out=vcv[:, nb * (NB // 128):(nb + 1) * (NB // 128), 0:128], in_=v3(ps2.t[:, 0:NB], 128)), [ps2], [vcmpA])
                    if g == 0 and "dbg_kc" in cfg.dbg:
                        dk = self.dtile("dbg_kc", [128, NCMP], BF16)
                        dv = self.dtile("dbg_vc", [128, NCMPC * W], BF16)
                        c.dma("sp", [(dk.t, kcmpT[:])], [kcmpT], [dk])
                        c.dma("sp", [(dv.t, vcmpA[:])], [vcmpA], [dv])
                    c.barrier()
                    ksT = sbg([128, S], BF16, "ksT")
                    c.dma("sp", [(ksT[:, rank * T:(rank + 1) * T], self.kv_piece(2, g, rank)) for rank in range(8)], [self.g_kvT], [ksT])
                    vsA = sbg([128, NCK * 129], BF16, "vsA")
                    vsv = v3(vsA.t[:], 129)
                    c.op("pool", lambda: nc.gpsimd.memset(vsv[:, :, 128:129], 1.0), [], [vsA])
                    cr = self.vt_cr
                    pairs = []
                    for rank in range(8):
                        for ch in range(T // cr):
                            kc0 = rank * NT + ch * (cr // 128)
                            pairs.append((vsv[:, kc0:kc0 + cr // 128, 0:128],
                                          self.g_vtm.t[ch, rank % 2, rank // 2, :, g * 128:(g + 1) * 128].rearrange("(cc k) d -> k cc d", k=128)))
                    c.dma("sp", pairs, [self.g_vtm], [vsA])
                    ksL = sbg([128, T], BF16, "ksL")
                    c.dma("sp", [(ksL[:], self.kvT.t[1024 + g * 128:1024 + (g + 1) * 128, :])], [self.kvT], [ksL])
                    vsL = sbg([128, NT * 129], BF16, "vsL")
                    vslv = v3(vsL.t[:], 129)
                    c.op("pool", lambda: nc.gpsimd.memset(vslv[:, :, 128:129], 1.0), [], [vsL])
                    c.dma("sp", [(vslv[:, :, 0:128], self.vtm2.t[:, g * 128:(g + 1) * 128].rearrange("(cc k) d -> k cc d", k=128))], [self.vtm2], [vsL])
                    kwT = sbg([128, 512 + T], BF16, "kwT")
                    c.dma("sp", [(kwT[:, 512:512 + T], self.kvT.t[1536 + g * 128:1536 + (g + 1) * 128, :])], [self.kvT], [kwT])
                    vwA = sbg([128, (4 + NT) * 129], BF16, "vwA")
                    vwv = v3(vwA.t[:], 129)
                    c.op("pool", lambda: nc.gpsimd.memset(vwv[:, :, 128:129], 1.0), [], [vwA])
                    c.dma("sp", [(vwv[:, 4:4 + NT, 0:128], self.vtm2.t[:, 512 + g * 128:512 + (g + 1) * 128].rearrange("(cc k) d -> k cc d", k=128))], [self.vtm2], [vwA])
                    with contextlib.ExitStack() as esh:
                        hk = c.sb(esh, [128, 8 * 512], BF16, "hk")
                        hkv = v3(hk.t[:], 512)
                        c.dma("sp", [(hkv[:, rank, :], self.kv_piece(3, g, rank)[:, T - 512:T]) for rank in range(8)], [self.g_kvT], [hk])
                        hv = c.sb(esh, [128, 8 * 512], BF16, "hv")
                        hvv = hv.t[:].rearrange("p (r cc d) -> p r cc d", r=8, d=128)
                        pairs = []
                        for rank in range(8):
                            for ch in range((T - 512) // cr, T // cr):
                                cc0 = (ch * cr - (T - 512)) // 128
                                pairs.append((hvv[:, rank, cc0:cc0 + cr // 128, :],
                                              self.g_vtm.t[ch, rank % 2, rank // 2, :, 512 + g * 128:512 + (g + 1) * 128].rearrange("(cc k) d -> k cc d", k=128)))
                        c.dma("sp", pairs, [self.g_vtm], [hv])
                        hks = c.sb(esh, [128, 512], F32, "hks")
                        hvs = c.sb(esh, [128, 512], F32, "hvs")
                        hv2 = v3(hv.t[:], 512)
                        c.op("dve", lambda: nc.vector.tensor_scalar(out=hks[:], in0=hkv[:, 0, :], scalar1=self.cinfo[:, 2:3], scalar2=None, op0=ALU.mult), [hk, self.cinfo], [hks])
                        c.op("dve", lambda: nc.vector.tensor_scalar(out=hvs[:], in0=hv2[:, 0, :], scalar1=self.cinfo[:, 2:3], scalar2=None, op0=ALU.mult), [hv, self.cinfo], [hvs])
                        for rank in range(1, 8):
                            c.op("dve", lambda rank=rank: nc.vector.scalar_tensor_tensor(out=hks[:], in0=hkv[:, rank, :], scalar=self.cinfo[:, 2 + rank:3 + rank], in1=hks[:], op0=ALU.mult, op1=ALU.add), [hk, hks, self.cinfo], [hks])
                            c.op("dve", lambda rank=rank: nc.vector.scalar_tensor_tensor(out=hvs[:], in0=hv2[:, rank, :], scalar=self.cinfo[:, 2 + rank:3 + rank], in1=hvs[:], op0=ALU.mult, op1=ALU.add), [hv, hvs, self.cinfo], [hvs])
                        c.op("dve", lambda: nc.vector.tensor_copy(out=kwT[:, 0:512], in_=hks[:]), [hks], [kwT])
                        c.op("dve", lambda: nc.vector.tensor_copy(out=vwv[:, 0:4, 0:128], in_=v3(hvs.t[:], 128)), [hvs], [vwA])
                    c.barrier()
                    qg = sbg([128, 4 * T], BF16, "qg")
                    qgv = v3(qg.t[:], T)
                    c.dma("sp", [(qgv, self.qT.t[g * 512:(g + 1) * 512, :].rearrange("(k d) t -> d k t", d=128))], [self.qT], [qg])
                    q2 = sbg([128, 512], BF16, "q2")
                    gt = sbg([128, 48], F32, "gt")
                    yacc = sbg([128, 512], F32, "yacc")
                    ybf = sbg([128, 512], BF16, "ybf")
                    ycs = sbg([128, 512], BF16, "ycs")
                    mbc = sbg([128, NCMP], BF16, "mbc")
                    et = [sbg([128, 512], BF16, "et") for _ in range(3)]
                    rz = sbg([128, 8], F32, "rz")
                    imp = sbg([128, NBLK], F32, "imp")
                    dd = sbg([128, NBLK], F32, "dd")
                    f1 = sbg([128, NBLK], F32, "f1")
                    f2 = sbg([128, NBLK], F32, "f2")
                    sc = sbg([128, NBLK], F32, "sc")
                    wk = sbg([128, NBLK], F32, "wk")
                    m8 = sbg([128, 16], F32, "m8")
                    nsel = sbg([128, NBLK], BF16, "nsel")
                    nse = sbg([128, S], BF16, "nse")
                    ei = [0]

                    def attend(kT_ap, mask_ap, v_ap, first, last, width):
                        ps = c.ps()

                        def qk(ps=ps):
                            ins = nc.tensor.matmul(ps[:, 0:512], lhsT=kT_ap, rhs=q2[:], start=True, stop=(mask_ap is None))
                            if mask_ap is not None:
                                ins = nc.tensor.matmul(ps[:, 0:512], lhsT=mask_ap, rhs=I4[:], start=False, stop=True)
                            return ins
                        c.op("pe", qk, self._areads, [ps])
                        e_ = et[ei[0] % 3]
                        ei[0] += 1
                        c.op("act", lambda: nc.scalar.activation(out=e_[:], in_=ps[:, 0:512], func=AF.Exp, scale=SCALE), [ps], [e_])

                        def pv():
                            ins = None
                            for k in range(4):
                                ins = nc.tensor.matmul(acc[k][:, 0:width], lhsT=e_[:, k * 128:(k + 1) * 128], rhs=v_ap, start=first, stop=last)
                            return ins
                        c.op("pe", pv, [e_] + self._areads, acc)

                    import os as _os
                    _brs = [int(x) for x in _os.environ.get("NSA_BR", "0,1,2").split(",")]

                    def finish(br, first_branch):
                        if first_branch:
                            c.op("pool", lambda: nc.gpsimd.memset(yacc[:], 0.0), [], [yacc])
                        first_branch = False
                        for k in range(4):
                            c.op("dve", lambda k=k: nc.vector.tensor_scalar(out=rz[:, k:k + 1], in0=acc[k][:, 128:129], scalar1=1e-30, scalar2=None, op0=ALU.max), [acc[k]], [rz])
                            c.op("dve", lambda k=k: nc.vector.reciprocal(out=rz[:, k:k + 1], in_=rz[:, k:k + 1]), [rz], [rz])
                            gi = br * 16 + g * 4 + k
                            c.op("dve", lambda k=k, gi=gi: nc.vector.tensor_tensor(out=rz[:, 4 + k:5 + k], in0=rz[:, k:k + 1], in1=gt[:, gi:gi + 1], op=ALU.mult), [rz, gt], [rz])
                            ys = yacc[:, k * 128:(k + 1) * 128]
                            if br not in _brs:
                                continue
                            if first_branch:
                                c.op("dve", lambda k=k, ys=ys: nc.vector.tensor_scalar(out=ys, in0=acc[k][:, 0:128], scalar1=rz[:, 4 + k:5 + k], scalar2=None, op0=ALU.mult), [acc[k], rz], [yacc])
                            else:
                                c.op("dve", lambda k=k, ys=ys: nc.vector.scalar_tensor_tensor(out=ys, in0=acc[k][:, 0:128], scalar=rz[:, 4 + k:5 + k], in1=ys, op0=ALU.mult, op1=ALU.add), [acc[k], rz, yacc], [yacc])

                    for i in range(NT):
                        tok = slice(i * 128, (i + 1) * 128)
                        c.op("dve", lambda tok=tok: nc.vector.tensor_copy(out=v3(q2.t[:], 128), in_=qgv[:, :, tok]), [qg], [q2])
                        c.dma("sp", [(gt[:], self.ngate.t[tok, :])], [self.ngate], [gt])
                        c.op("dve", lambda i=i: nc.vector.tensor_scalar(out=mbc[:], in0=patt[:], scalar1=thr[:, i:i + 1], scalar2=NEG, op0=ALU.is_gt, op1=ALU.mult), [patt, thr], [mbc])
                        self._areads = [kcmpT, mbc, vcmpA, q2, I4]
                        for ncn in range(NCMPC):
                            attend(kcmpT[:, ncn * 128:(ncn + 1) * 128], mbc[:, ncn * 128:(ncn + 1) * 128], vcv[:, ncn, :], ncn == 0, ncn == NCMPC - 1, W)
                        finish(0, True)
                        for k in range(4):
                            if k == 0:
                                c.op("dve", lambda: nc.vector.tensor_scalar(out=imp[:], in0=acc[0][:, 129:W], scalar1=rz[:, 0:1], scalar2=None, op0=ALU.mult), [acc[0], rz], [imp])
                            else:
                                c.op("dve", lambda k=k: nc.vector.scalar_tensor_tensor(out=imp[:], in0=acc[k][:, 129:W], scalar=rz[:, k:k + 1], in1=imp[:], op0=ALU.mult, op1=ALU.add), [acc[k], rz, imp], [imp])
                        V = nc.vector
                        c.op("dve", lambda i=i: V.tensor_scalar(out=dd[:], in0=Jt[:], scalar1=curc[:, i:i + 1], scalar2=None, op0=ALU.subtract), [Jt, curc], [dd])
                        c.op("dve", lambda: V.tensor_scalar(out=f1[:], in0=dd[:], scalar1=0.0, scalar2=None, op0=ALU.is_equal), [dd], [f1])
                        c.op("dve", lambda: V.tensor_scalar(out=f2[:], in0=dd[:], scalar1=-1.0, scalar2=None, op0=ALU.is_equal), [dd], [f2])
                        c.op("dve", lambda: V.tensor_tensor(out=f1[:], in0=f1[:], in1=f2[:], op=ALU.max), [f1, f2], [f1])
                        c.op("dve", lambda: V.scalar_tensor_tensor(out=sc[:], in0=f1[:], scalar=1e4, in1=imp[:], op0=ALU.mult, op1=ALU.add), [f1, imp], [sc])
                        c.op("dve", lambda: V.tensor_scalar(out=f2[:], in0=dd[:], scalar1=0.0, scalar2=-1e9, op0=ALU.is_gt, op1=ALU.mult), [dd], [f2])
                        c.op("dve", lambda: V.tensor_tensor(out=sc[:], in0=sc[:], in1=f2[:], op=ALU.add), [sc, f2], [sc])
                        c.op("dve", lambda: V.memset(sc[:, 0:1], 1e4), [], [sc])
                        c.op("dve", lambda: V.max(out=m8[:, 0:8], in_=sc[:]), [sc], [m8])
                        c.op("dve", lambda: V.match_replace(out=wk[:], in_to_replace=m8[:, 0:8], in_values=sc[:], imm_value=-1e30), [sc, m8], [wk])
                        c.op("dve", lambda: V.max(out=m8[:, 8:16], in_=wk[:]), [wk], [m8])
                        c.op("dve", lambda: V.tensor_scalar(out=f1[:], in0=sc[:], scalar1=m8[:, 15:16], scalar2=None, op0=ALU.is_ge), [sc, m8], [f1])
                        c.op("dve", lambda i=i: V.tensor_scalar(out=f2[:], in0=Jt[:], scalar1=blk0[:, i:i + 1], scalar2=None, op0=ALU.is_lt), [Jt, blk0], [f2])
                        c.op("dve", lambda: V.tensor_tensor(out=f1[:], in0=f1[:], in1=f2[:], op=ALU.mult), [f1, f2], [f1])
                        c.op("dve", lambda: V.tensor_scalar(out=nsel[:], in0=f1[:], scalar1=-NEG, scalar2=NEG, op0=ALU.mult, op1=ALU.add), [f1], [nsel])
                        c.op("dve", lambda: V.tensor_copy(out=v3(nse.t[:], 64), in_=nsel[:].unsqueeze(2).to_broadcast([128, NBLK, 64])), [nsel], [nse])
                        self._areads = [ksT, ksL, nse, Atri, vsA, vsL, q2, I4]
                        attend(ksL[:, tok], Atri[:], vslv[:, i, :], True, False, 129)
                        for kc in range(NCK):
                            attend(ksT[:, kc * 128:(kc + 1) * 128], nse[:, kc * 128:(kc + 1) * 128], vsv[:, kc, :], False, kc == NCK - 1, 129)
                        finish(1, False)
                        self._areads = [kwT, vwA, Atri, Astr, AstrH, hvA, q2, I4]
                        for r in range(5):
                            idx = 4 + i - r
                            halo = idx < 4
                            if r == 0:
                                m = Atri[:]
                            elif r == 4:
                                m = AstrH[:] if halo else Astr[:]
                            else:
                                m = hvA[:] if halo else None
                            attend(kwT[:, idx * 128:(idx + 1) * 128], m, vwv[:, idx, :], r == 0, r == 4, 129)
                        finish(2, False)
                        c.op("act", lambda: nc.scalar.copy(out=ybf[:], in_=yacc[:]), [yacc], [ybf])
                        ps = c.ps()
                        psb = ps.t[:].bitcast(BF16)

                        def tr(psb=psb):
                            ins = None
                            for k in range(4):
                                ins = nc.tensor.transpose(psb[:, k * 128:(k + 1) * 128], ybf[:, k * 128:(k + 1) * 128], self.ident[:])
                            return ins
                        c.op("pe", tr, [ybf, self.ident], [ps])
                        c.op("act", lambda psb=psb: nc.scalar.copy(out=ycs[:], in_=psb[:, 0:512]), [ps], [ycs])
                        c.dma("sp", [(self.ycT.t[g * 512:(g + 1) * 512, tok].rearrange("(k d) t -> d k t", d=128), v3(ycs.t[:], 128))], [ycs], [self.ycT])
                c.barrier()
        c.psr = (0, 8)
        c.barrier()

    def merge_out(self, l, xsrc, xdst):
        c, nc, cfg = self.c, self.nc, self.cfg
        TW = cfg.TW
        srcs = [self.yaT, self.ybT, self.ycT]
        for bi in range(3):
            if cfg.stop == "sgu" and bi != 1:
                continue

            def pre(es):
                self.e_f = [c.sb(es, [128, 512], F32, "mf") for _ in range(3)]
                self.e_b = [c.sb(es, [128, 512], BF16, "mb") for _ in range(3)]
                self.e_i = 0

            def epi(ps, pb, c0, cw, t0, bi=bi):
                i = self.e_i
                self.e_i += 1
                ef, eb = self.e_f[i % 3], self.e_b[i % 3]
                r0 = pb * 512 + c0
                c.dma("sp", [(eb[0:cw, 0:TW], self.mgT.t[bi * D + r0:bi * D + r0 + cw, t0:t0 + TW])], [self.mgT], [eb])
                c.op("dve", lambda: nc.vector.tensor_tensor(out=ef[0:cw, 0:TW], in0=ps[0:cw, 0:TW], in1=eb[0:cw, 0:TW], op=ALU.mult), [ps, eb], [ef])
                c.dma("sp", [(self.mrg[bi].t[r0:r0 + cw, t0:t0 + TW], ef[0:cw, 0:TW])], [ef], [self.mrg[bi]])
            self.gemm(srcs[bi], MIX, "w_br%d" % bi, l, [[p] for p in range(8)], "ws", epi, pre=pre)
        with contextlib.ExitStack() as es:
            a = [[c.sb(es, [128, cfg.T], F32, "ma") for _ in range(3)] for _ in range(2)]
            o = [c.sb(es, [128, cfg.T], BF16, "mo") for _ in range(2)]
            for dc in range(32):
                aa, oo = a[dc % 2], o[dc % 2]
                rows = slice(dc * 128, (dc + 1) * 128)
                for bi in range(3):
                    c.dma("sp", [(aa[bi][:], self.mrg[bi].t[rows, :])], [self.mrg[bi]], [aa[bi]])
                c.op("dve", lambda aa=aa: nc.vector.tensor_tensor(out=aa[0][:], in0=aa[0][:], in1=aa[1][:], op=ALU.add), [aa[0], aa[1]], [aa[0]])
                c.op("dve", lambda aa=aa, oo=oo: nc.vector.tensor_tensor(out=oo[:], in0=aa[0][:], in1=aa[2][:], op=ALU.add), [aa[0], aa[2]], [oo])
                c.dma("sp", [(self.mergedT.t[rows, :], oo[:])], [oo], [self.mergedT])
        c.barrier()

        def pre2(es):
            self.f_yo = [c.sb(es, [128, 512], F32, "oyo") for _ in range(4)]
            self.f_i = 0

        def epi2(ps, pb, c0, wd, t0):
            i = self.f_i
            self.f_i += 1
            yo = self.f_yo[i % 4]
            if i % 2 == 0:
                c.op("act", lambda: nc.scalar.copy(out=yo[:, 0:wd], in_=ps[:, 0:wd]), [ps], [yo])
            else:
                c.op("dve", lambda: nc.vector.tensor_copy(out=yo[:, 0:wd], in_=ps[:, 0:wd]), [ps], [yo])
            c.dma("sp", [(self.ybuf.t[t0:t0 + 128, pb * 512:pb * 512 + wd], yo[:, 0:wd])], [yo], [self.ybuf])
        self.gemm(self.mergedT, D, "w_o", l, [[b] for b in range(8)], "as", epi2, pre=pre2)
        self.resid_norm(xsrc, self.ybuf, self.p_normg.t[l, 3, :], 1.0, xdst)

    def layer(self, l, xcur):
        cfg = self.cfg
        last = (l == cfg.DEPTH - 1)
        self.ffn(l, "a", xcur, self.xa)
        if cfg.stop == "ffn_a":
            return self.xa
        self.proj(l)
        if cfg.stop == "proj":
            return self.xa
        if cfg.stop != "nsa":
            self.sgu(l)
        if cfg.stop == "sgu":
            return self.xa
        if cfg.stop != "nsa":
            self.ssd(l)
        if cfg.stop == "ssd":
            return self.xa
        self.nsa(l)
        if cfg.stop == "nsa":
            return self.xa
        self.merge_out(l, self.xa, self.xb)
        if cfg.stop == "mix":
            return self.xb
        dst = self.out if last else self.xc
        self.ffn(l, "b", self.xb, dst)
        return dst


def build_program(cfg):
    return KM(cfg).build()


def shard_inputs(cfg, inputs, l0=0, x_override=None):
    T, DEPTH = cfg.T, cfg.DEPTH
    sl = slice(l0, l0 + DEPTH)
    f = lambda a: np.asarray(a, np.float32)
    x = f(inputs["x"])[0] if x_override is None else x_override
    wb = f(inputs["w_branch"])[sl]
    cw1 = f(inputs["cmp_w1"])[sl]
    wfull = {
        "fa_in": f(inputs["ffn_w_in"])[sl, 0], "fb_in": f(inputs["ffn_w_in"])[sl, 1],
        "fa_out": f(inputs["ffn_w_out"])[sl, 0], "fb_out": f(inputs["ffn_w_out"])[sl, 1],
        "w_in": f(inputs["w_in"])[sl], "w_br0": wb[:, 0], "w_br1": wb[:, 1], "w_br2": wb[:, 2],
        "w_o": f(inputs["w_out"])[sl], "cw1k": cw1[:, 0], "cw1v": cw1[:, 1],
    }
    conv = np.concatenate([f(inputs["ssd_conv_w"])[sl].transpose(0, 2, 1), f(inputs["ssd_conv_b"])[sl][:, :, None]], axis=2)
    head = np.stack([f(inputs["ssd_a_log"])[sl], f(inputs["ssd_dt_bias"])[sl], f(inputs["ssd_d"])[sl]], axis=1)
    ssdg = f(inputs["ssd_norm_g"])[sl].reshape(DEPTH, NHEAD, 64).transpose(0, 2, 1)
    cpos = f(inputs["cmp_pos"])[sl].transpose(0, 1, 3, 2)
    shared = {
        "p_normg": np.ascontiguousarray(f(inputs["norm_g"])[sl]),
        "p_conv": np.ascontiguousarray(conv), "p_head": np.ascontiguousarray(head), "p_ssdg": np.ascontiguousarray(ssdg),
        "p_sgug": np.ascontiguousarray(f(inputs["sgu_norm_g"])[sl]),
        "p_sguw": np.ascontiguousarray(f(inputs["sgu_w"])[sl]),
        "p_sgub": np.ascontiguousarray(f(inputs["sgu_b"])[sl].reshape(DEPTH, -1)),
        "p_cpos": np.ascontiguousarray(cpos), "p_cw2": np.ascontiguousarray(f(inputs["cmp_w2"])[sl]),
    }
    maps = []
    for ci in range(NC):
        m = {"x": np.ascontiguousarray(x[ci * T:(ci + 1) * T])}
        for nm, (kk, nn, pans, pw) in WSPEC.items():
            r = kk // NC
            m["w_" + nm] = np.ascontiguousarray(wfull[nm][:, ci * r:(ci + 1) * r, :])
        info = np.zeros((128, 32), np.float32)
        info[:, 0] = ci
        info[:, 1] = ci * T
        if ci > 0:
            info[:, 2 + ci - 1] = 1.0
        info[:, 10 + ci] = 1.0
        info[:, 18] = 1.0 if ci > 0 else 0.0
        m["cinfo"] = info
        m.update(shared)
        maps.append(m)
    return maps


LAYERS_PER_LAUNCH = 4
TOTAL_DEPTH = 4


def kernel(**inputs):
    cfg = Cfg(DEPTH=LAYERS_PER_LAUNCH)
    nc = build_program(cfg)
    x = None
    for l0 in range(0, TOTAL_DEPTH, LAYERS_PER_LAUNCH):
        maps = shard_inputs(cfg, inputs, l0=l0, x_override=x)
        res = run_bass_kernel_spmd(nc, maps, core_ids=list(range(NC)))
        x = np.concatenate([res.results[c]["out"] for c in range(NC)], axis=0)
    return x[None].astype(np.float32)
```

```python
import contextlib
import math
import numpy as np
import concourse.bass as bass
import concourse.mybir as mybir
from concourse.bass_utils import run_bass_kernel_spmd

F32 = mybir.dt.float32
BF16 = mybir.dt.bfloat16
AF = mybir.ActivationFunctionType
ALU = mybir.AluOpType
AX = mybir.AxisListType

D = 4096
DFF = 3072
MIX = 2048
XBC = 3072
NHEAD = 32
IN_COLS = 26704
C_Z, C_XBC, C_DT, C_U, C_V, C_Q, C_KV, C_NG, C_MG = 0, 2048, 5120, 5152, 7200, 9248, 11296, 14368, 14416
EPS = 1e-6
NEG = -30000.0
NC = 8
QG = [[0, 2, 4, 6], [1, 3, 5, 7]]
PG = [[0, 1], [2, 3], [4, 5], [6, 7]]


class Tl:
    __slots__ = ("t", "w", "r", "name")

    def __init__(self, t, name=""):
        self.t = t
        self.w = None
        self.r = {}
        self.name = name

    def __getitem__(self, k):
        return self.t[k]


class Ctx:
    NDMASEM = 24

    def __init__(self, nc, es):
        self.nc = nc
        self.es = es
        self.eng = {"pe": nc.tensor, "act": nc.scalar, "dve": nc.vector, "pool": nc.gpsimd, "sp": nc.sync}
        self.sem = {e: es.enter_context(nc.semaphore("s_" + e)) for e in self.eng}
        self.sem["cc"] = es.enter_context(nc.semaphore("s_cc"))
        self.cnt = {e: 0 for e in self.sem}
        self.seen = {e: {} for e in self.eng}
        self.dsem, self.dtgt, self.drr = {}, {}, {}
        for q in ("sp", "pool", "act"):
            self.dsem[q] = [es.enter_context(nc.semaphore("d_%s%d" % (q, i))) for i in range(self.NDMASEM)]
            self.dtgt[q] = [0] * self.NDMASEM
            self.drr[q] = 0
        self.nsb = 0
        self.psb = []
        self.psi = 0
        self.psr = (0, 8)

    def sb(self, es, shape, dt, name=None):
        self.nsb += 1
        name = (name or "sb") + "_%d" % self.nsb
        return Tl(es.enter_context(self.nc.sbuf_tensor(name, list(shape), dt)), name)

    def psum_banks(self, n=8):
        for i in range(n):
            t = self.es.enter_context(self.nc.psum_tensor("ps%d" % i, [128, 512], F32))
            self.psb.append(Tl(t, "ps%d" % i))

    def ps(self):
        lo, hi = self.psr
        t = self.psb[lo + self.psi % (hi - lo)]
        self.psi += 1
        return t

    def dram(self, name, shape, dt, **kw):
        return Tl(self.nc.dram_tensor(name, list(shape), dt, **kw).ap(), name)

    def _semof(self, k):
        return self.dsem[k[0]][k[1]] if isinstance(k, tuple) else self.sem[k]

    def _wait(self, e, k, v):
        if self.seen[e].get(k, 0) >= v:
            return
        self.eng[e].wait_ge(self._semof(k), v)
        self.seen[e][k] = v

    def _deps(self, e, reads, writes):
        deps = {}
        for t in reads:
            if t.w is not None:
                k, v = t.w
                deps[k] = max(deps.get(k, 0), v)
        for t in writes:
            if t.w is not None:
                k, v = t.w
                if k != e:
                    deps[k] = max(deps.get(k, 0), v)
            for k, v in t.r.items():
                if k != e:
                    deps[k] = max(deps.get(k, 0), v)
        for k, v in deps.items():
            self._wait(e, k, v)

    def _stamp(self, key, val, reads, writes):
        for t in reads:
            t.r[key] = val
        for t in writes:
            t.w = (key, val)
            t.r = {}

    def op(self, e, fn, reads=(), writes=()):
        self._deps(e, reads, writes)
        ins = fn()
        self.cnt[e] += 1
        ins.then_inc(self.sem[e], 1)
        self._stamp(e, self.cnt[e], reads, writes)
        return ins

    def dma(self, q, pairs, reads=(), writes=(), **kw):
        self._deps(q, reads, writes)
        i = self.drr[q] % self.NDMASEM
        self.drr[q] += 1
        key = (q, i)
        self._wait(q, key, self.dtgt[q][i])
        for (o, a) in pairs:
            self.eng[q].dma_start(out=o, in_=a, **kw).then_inc(self.dsem[q][i], 16)
            self.dtgt[q][i] += 16
        self._stamp(key, self.dtgt[q][i], reads, writes)

    def allgather(self, groups, in_t, out_t, in_ap, out_ap):
        q = "pool"
        self._deps(q, [in_t], [out_t])
        self._wait(q, "cc", self.cnt["cc"])
        self.nc.gpsimd.collective_compute(
            "AllGather", ALU.bypass, replica_groups=groups,
            ins=[in_ap.opt()], outs=[out_ap.opt()]).then_inc(self.sem["cc"], 1)
        self.cnt["cc"] += 1
        self._stamp("cc", self.cnt["cc"], [in_t], [out_t])

    def barrier(self, full=False):
        for e in self.eng:
            for e2 in self.sem:
                if e2 == "cc" and not full:
                    continue
                if e2 != e and self.cnt[e2] > 0:
                    self._wait(e, e2, self.cnt[e2])
            for q in self.dsem:
                for i in range(self.NDMASEM):
                    if self.dtgt[q][i] > 0:
                        self._wait(e, (q, i), self.dtgt[q][i])


def v3(ap, inner):
    return ap.rearrange("p (a b) -> p a b", b=inner)


def bc_rows(ap_row, n, parts=128):
    return bass.AP(ap_row.tensor, ap_row.offset, [[0, parts], [1, n]])


class Cfg:
    def __init__(self, T=2048, DEPTH=4, stop=None, dbg=()):
        self.NC, self.T, self.DEPTH = NC, T, DEPTH
        self.S = NC * T
        self.NT = T // 128
        self.TB = min(1024, T)
        self.TW = min(512, T)
        self.stop = stop
        self.dbg = tuple(dbg)


def _pan(col0, n, w=512):
    out = []
    c = 0
    while c < n:
        out.append((col0 + c, min(w, n - c)))
        c += w
    return out


P_Z = _pan(C_Z, 2048)
P_XBC = _pan(C_XBC, 3072)
P_DT = [(C_DT, 32)]
P_U = _pan(C_U, 2048)
P_V = _pan(C_V, 2048)
P_Q = _pan(C_Q, 2048)
P_KV = _pan(C_KV, 3072)
P_NG = [(C_NG, 48)]
P_MG = _pan(C_MG, 12288)
WIN_PANELS = P_Z + P_XBC + P_DT + P_U + P_V + P_Q + P_KV + P_NG + P_MG

WSPEC = {
    "fa_in": (D, 2 * DFF, _pan(0, 2 * DFF, 256), 256),
    "fa_out": (DFF, D, _pan(0, D), 512),
    "fb_in": (D, 2 * DFF, _pan(0, 2 * DFF, 256), 256),
    "fb_out": (DFF, D, _pan(0, D), 512),
    "w_in": (D, IN_COLS, WIN_PANELS, 512),
    "w_br0": (MIX, D, _pan(0, D), 512),
    "w_br1": (MIX, D, _pan(0, D), 512),
    "w_br2": (MIX, D, _pan(0, D), 512),
    "w_o": (D, D, _pan(0, D), 512),
    "cw1k": (4096, 128, [(0, 128)], 128),
    "cw1v": (4096, 128, [(0, 128)], 128),
}


class K:
    def __init__(self, cfg):
        self.cfg = cfg
        self.nc = bass.Bass("TRN2", target_bir_lowering=False)

    def dtile(self, name, shape, dt):
        kind = "ExternalOutput" if name in self.cfg.dbg else "Internal"
        return self.c.dram(name, shape, dt, kind=kind)

    def build(self):
        cfg, nc = self.cfg, self.nc
        T, S, NT = cfg.T, cfg.S, cfg.NT
        with contextlib.ExitStack() as es:
            c = self.c = Ctx(nc, es)
            c.psum_banks(8)
            self.x_in = c.dram("x", [T, D], F32, kind="ExternalInput")
            self.out = c.dram("out", [T, D], F32, kind="ExternalOutput")
            self.wsh = {}
            for nm, (kk, nn, pans, pw) in WSPEC.items():
                self.wsh[nm] = c.dram("w_" + nm, [cfg.DEPTH, kk // NC, nn], F32, kind="ExternalInput")
            self.p_normg = c.dram("p_normg", [cfg.DEPTH, 6, D], F32, kind="ExternalInput")
            self.cinfo_d = c.dram("cinfo", [128, 32], F32, kind="ExternalInput")
            self.p_conv = c.dram("p_conv", [cfg.DEPTH, XBC, 5], F32, kind="ExternalInput")
            self.p_head = c.dram("p_head", [cfg.DEPTH, 3, NHEAD], F32, kind="ExternalInput")
            self.p_ssdg = c.dram("p_ssdg", [cfg.DEPTH, 64, NHEAD], F32, kind="ExternalInput")
            self.p_sgug = c.dram("p_sgug", [cfg.DEPTH, MIX], F32, kind="ExternalInput")
            self.p_sguw = c.dram("p_sguw", [cfg.DEPTH, 16, 128, 128], F32, kind="ExternalInput")
            self.p_sgub = c.dram("p_sgub", [cfg.DEPTH, 16 * 128], F32, kind="ExternalInput")
            self.p_cpos = c.dram("p_cpos", [cfg.DEPTH, 2, 128, 32], F32, kind="ExternalInput")
            self.p_cw2 = c.dram("p_cw2", [cfg.DEPTH, 2, 128, 128], F32, kind="ExternalInput")
            self.xa = self.dtile("xa", [T, D], F32)
            self.xb = self.dtile("xb", [T, D], F32)
            self.xc = self.dtile("xc", [T, D], F32)
            self.ybuf = self.dtile("ybuf", [T, D], F32)
            self.hT = self.dtile("hT", [D, T], BF16)
            self.hidT = self.dtile("hidT", [DFF, T], BF16)
            self.wloc, self.wfull = {}, {}
            for nm, (kk, nn, pans, pw) in WSPEC.items():
                for l in range(cfg.DEPTH):
                    self.wloc[nm, l] = c.dram("wl_%s%d" % (nm, l), [len(pans), kk // NC, pw], BF16)
                    self.wfull[nm, l] = c.dram("wf_%s%d" % (nm, l), [len(pans), 2, 4, kk // NC, pw], BF16)
            self.szT = self.dtile("szT", [MIX, T], BF16)
            self.xbcT = self.dtile("xbcT", [XBC, T], F32)
            self.dt_tm = self.dtile("dt_tm", [T, 32], F32)
            self.uT = self.dtile("uT", [MIX, T], BF16)
            self.v_tm = self.dtile("v_tm", [T, MIX], F32)
            self.qT = self.dtile("qT", [MIX, T], BF16)
            self.kvT = self.dtile("kvT", [2048, T], BF16)
            self.vtm2 = self.dtile("vtm2", [T, 1024], BF16)
            self.ngate = self.dtile("ngate", [T, 48], F32)
            self.mgT = self.dtile("mgT", [3 * D, T], BF16)
            self.convT = self.dtile("convT", [XBC, T], BF16)
            self.yaT = self.dtile("yaT", [MIX, T], BF16)
            self.ybT = self.dtile("ybT", [MIX, T], BF16)
            self.ycT = self.dtile("ycT", [MIX, T], BF16)
            self.mrg = [self.dtile("mrg%d" % i, [D, T], F32) for i in range(3)]
            self.mergedT = self.dtile("mergedT", [D, T], BF16)
            self.kv_cr = min(2048, 262144 // T)
            self.g_kvT = self.dtile("g_kvT", [2048 // self.kv_cr, 2, 4, self.kv_cr, T], BF16)
            self.vt_cr = min(T, 256)
            self.g_vtm = self.dtile("g_vtm", [T // self.vt_cr, 2, 4, self.vt_cr, 1024], BF16)
            self.halo_loc = self.dtile("halo_loc", [XBC, 4], F32)
            self.g_halo = self.dtile("g_halo", [1, 2, 4, XBC, 4], F32)
            self.s_loc = self.dtile("s_loc", [128, MIX], F32)
            self.g_sloc = self.dtile("g_sloc", [2, 2, 4, 64, MIX], F32)
            self.p_loc = self.dtile("p_loc", [16, 32], F32)
            self.g_ploc = self.dtile("g_ploc", [1, 2, 4, 16, 32], F32)
            self.s_init = self.dtile("s_init", [128, MIX], F32)
            self.cctmp = [c.dram("cctmp%d" % i, [4 * 512 * 1024 // 2], BF16) for i in range(2)]
            self.cci = 0
            self.consts(es)
            with contextlib.ExitStack() as esp:
                self.prep_weights(0, esp)
                c.barrier()
            xcur = self.x_in
            for l in range(cfg.DEPTH):
                xcur = self.layer(l, xcur)
                if cfg.stop is not None:
                    break
            c.barrier(full=True)
        return nc

    def consts(self, es):
        c, nc = self.c, self.nc
        self.ident = c.sb(es, [128, 128], BF16, "ident")
        self.identf = idf = c.sb(es, [128, 128], F32, "identf")
        c.op("pool", lambda: nc.gpsimd.memset(idf[:], 0.0), [], [idf])
        c.op("pool", lambda: nc.gpsimd.affine_select(out=idf[:], in_=idf[:], pattern=[[-1, 128]],
                                                      compare_op=ALU.not_equal, fill=1.0, base=0, channel_multiplier=1),
             [idf], [idf])
        c.op("dve", lambda: nc.vector.tensor_copy(out=self.ident[:], in_=idf[:]), [idf], [self.ident])
        self.epsb = c.sb(es, [128, 1], F32, "epsb")
        c.op("pool", lambda: nc.gpsimd.memset(self.epsb[:], EPS), [], [self.epsb])
        self.cinfo = c.sb(es, [128, 32], F32, "cinfo")
        c.dma("sp", [(self.cinfo[:], self.cinfo_d.t)], [self.cinfo_d], [self.cinfo])

    def exchange(self, src, src_ap, dst, dst_ap, rows, rowbytes):
        c = self.c
        crows = dst_ap.shape[3]
        nchunk = dst_ap.shape[0]
        assert nchunk * crows == rows and crows * rowbytes <= 512 * 1024
        cols = src_ap.shape[1]
        for ch in range(nchunk):
            tmp = self.cctmp[self.cci % 2]
            self.cci += 1
            n = 4 * crows * cols
            if src_ap.dtype == F32:
                tv = tmp.t[0:2 * n].bitcast(F32)
            else:
                tv = tmp.t[0:n]
            c.allgather(QG, src, tmp, src_ap[ch * crows:(ch + 1) * crows, :], tv)
            c.allgather(PG, tmp, dst, tv, dst_ap[ch])

    def wpiece(self, nm, l, b, rank):
        q, r = rank % 2, rank // 2
        return self.wfull[nm, l].t[b, q, r]

    def prep_weights(self, l, es):
        c, nc, cfg = self.c, self.nc, self.cfg
        st = [c.sb(es, [128, 512], F32, "wst") for _ in range(4)]
        sb_ = [c.sb(es, [128, 512], BF16, "wsb") for _ in range(4)]
        i = 0
        for nm, (kk, nn, pans, pw) in WSPEC.items():
            if cfg.stop == "ffn_a" and not nm.startswith("fa"):
                continue
            if cfg.stop in ("proj", "sgu", "ssd") and nm not in ("fa_in", "fa_out", "w_in", "w_br1"):
                continue
            if cfg.stop == "nsa" and nm not in ("fa_in", "fa_out", "w_in", "cw1k", "cw1v"):
                continue
            rows = kk // NC
            src, dst = self.wsh[nm], self.wloc[nm, l]
            for b, (c0, wd) in enumerate(pans):
                for r0 in range(0, rows, 128):
                    rr = min(128, rows - r0)
                    a, bt = st[i % 4], sb_[i % 4]
                    c.dma("sp", [(a[0:rr, 0:wd], src.t[l, r0:r0 + rr, c0:c0 + wd])], [src], [a])
                    if i % 2 == 0:
                        c.op("act", lambda a=a, bt=bt, rr=rr, wd=wd: nc.scalar.copy(out=bt[0:rr, 0:wd], in_=a[0:rr, 0:wd]), [a], [bt])
                    else:
                        c.op("dve", lambda a=a, bt=bt, rr=rr, wd=wd: nc.vector.tensor_copy(out=bt[0:rr, 0:wd], in_=a[0:rr, 0:wd]), [a], [bt])
                    c.dma("sp", [(dst.t[b, r0:r0 + rr, 0:wd], bt[0:rr, 0:wd])], [bt], [dst])
                    i += 1
            full = self.wfull[nm, l]
            self.exchange(dst, dst.t.rearrange("b r n -> (b r) n"), full, full.t, len(pans) * rows, pw * 2)

    def normT(self, xsrc, gain_ap, dstT):
        c, nc, cfg = self.c, self.nc, self.cfg
        with contextlib.ExitStack() as es:
            g = c.sb(es, [128, D], F32, "gain")
            c.dma("sp", [(g[:], bc_rows(gain_ap, D))], [self.p_normg], [g])
            xt = [c.sb(es, [128, D], F32, "nx") for _ in range(2)]
            hb = [c.sb(es, [128, D], BF16, "nh") for _ in range(2)]
            junk = c.sb(es, [128, D], BF16, "njunk")
            stat = [c.sb(es, [128, 4], F32, "nstat") for _ in range(2)]
            GT = min(4, cfg.NT)
            stg = [c.sb(es, [128, 32 * GT * 128], BF16, "nstg") for _ in range(2)]
            for gi in range(cfg.NT // GT):
                sg = stg[gi % 2]
                sgv = sg.t[:].rearrange("p (k t) -> p k t", t=GT * 128)
                for j in range(GT):
                    ti = gi * GT + j
                    x_, h_, s_ = xt[ti % 2], hb[ti % 2], stat[ti % 2]
                    c.dma("sp", [(x_[:], xsrc.t[ti * 128:(ti + 1) * 128, :])], [xsrc], [x_])
                    c.op("dve", lambda s_=s_: nc.vector.memset(s_[:], 0.0), [], [s_])
                    c.op("act", lambda x_=x_, s_=s_: nc.scalar.activation(out=junk[:], in_=x_[:], func=AF.Square, accum_out=s_[:, 0:1]), [x_, s_], [junk, s_])
                    c.op("act", lambda s_=s_: nc.scalar.activation(out=s_[:, 1:2], in_=s_[:, 0:1], func=AF.Sqrt, scale=1.0 / D, bias=self.epsb[:, 0:1]), [s_, self.epsb], [s_])
                    c.op("dve", lambda s_=s_: nc.vector.reciprocal(out=s_[:, 2:3], in_=s_[:, 1:2]), [s_], [s_])
                    c.op("dve", lambda x_=x_, h_=h_, s_=s_: nc.vector.scalar_tensor_tensor(out=h_[:], in0=x_[:], scalar=s_[:, 2:3], in1=g[:], op0=ALU.mult, op1=ALU.mult), [x_, s_, g], [h_])
                    for q4 in range(4):
                        ps = c.ps()
                        psb = ps.t[:].bitcast(BF16)

                        def tr(psb=psb, h_=h_, q4=q4):
                            ins = None
                            for k in range(8):
                                dc = q4 * 8 + k
                                ins = nc.tensor.transpose(psb[:, k * 128:(k + 1) * 128], h_[:, dc * 128:(dc + 1) * 128], self.ident[:])
                            return ins
                        c.op("pe", tr, [h_, self.ident], [ps])
                        dst = sgv[:, q4 * 8:(q4 + 1) * 8, j * 128:(j + 1) * 128]
                        src = psb.rearrange("p (a b) -> p a b", b=128)
                        if q4 % 2 == 0:
                            c.op("act", lambda dst=dst, src=src: nc.scalar.copy(out=dst, in_=src), [ps], [sg])
                        else:
                            c.op("dve", lambda dst=dst, src=src: nc.vector.tensor_copy(out=dst, in_=src), [ps], [sg])
                t0 = gi * GT * 128
                c.dma("sp", [(dstT.t[:, t0:t0 + GT * 128].rearrange("(k p) t -> p k t", p=128), sgv)], [sg], [dstT])
        c.barrier()

    def gemm(self, AT, K_, wnm, l, blocks, mode, epi, pre=None):
        c, nc, cfg = self.c, self.nc, self.cfg
        KC = K_ // 128
        TB, TW = cfg.TB, cfg.TW
        kk, nn, pans, pw = WSPEC[wnm]
        KP = (kk // NC) // 128
        assert KP * NC == KC
        nslot = max(len(b) for b in blocks)
        W = self.wfull[wnm, l]
        with contextlib.ExitStack() as es:
            ab = c.sb(es, [128, KC * TB], BF16, "gA")
            abv = v3(ab.t[:], TB)
            wts = [c.sb(es, [128, KC * nslot * pw], BF16, "gW") for _ in range(2)]
            if pre is not None:
                pre(es)
            for tb in range(cfg.T // TB):
                c.dma("sp", [(abv, AT.t[:, tb * TB:(tb + 1) * TB].rearrange("(k p) t -> p k t", p=128))], [AT], [ab])

                def wview(wt):
                    return wt.t[:].rearrange("p (k s n) -> p k s n", s=nslot, n=pw)

                def loadw(bi):
                    wt = wts[bi % 2]
                    wv = wview(wt)
                    pairs = []
                    for sl, pb in enumerate(blocks[bi]):
                        wd = pans[pb][1]
                        for rank in range(NC):
                            pairs.append((wv[:, rank * KP:(rank + 1) * KP, sl, 0:wd],
                                          self.wpiece(wnm, l, pb, rank)[:, 0:wd].rearrange("(k p) n -> p k n", p=128)))
                    c.dma("sp", pairs, [W], [wt])
                loadw(0)
                for bi, blk in enumerate(blocks):
                    if bi + 1 < len(blocks):
                        loadw(bi + 1)
                    wt = wts[bi % 2]
                    wv = wview(wt)
                    if mode == "ws":
                        wd0 = pans[blk[0]][1]
                        for c0 in range(0, wd0, 128):
                            cw = min(128, wd0 - c0)
                            for t5 in range(TB // TW):
                                for sl, pb in enumerate(blk):
                                    ps = c.ps()

                                    def mm(ps=ps, wv=wv, c0=c0, cw=cw, t5=t5, sl=sl):
                                        ins = None
                                        for k in range(KC):
                                            ins = nc.tensor.matmul(ps[0:cw, 0:TW], lhsT=wv[:, k, sl, c0:c0 + cw], rhs=abv[:, k, t5 * TW:(t5 + 1) * TW],
                                                                   start=(k == 0), stop=(k == KC - 1))
                                        return ins
                                    c.op("pe", mm, [wt, ab], [ps])
                                    epi(ps, pb, c0, cw, tb * TB + t5 * TW)
                    else:
                        for sl, pb in enumerate(blk):
                            wd = pans[pb][1]
                            for tt in range(TB // 128):
                                ps = c.ps()

                                def mm(ps=ps, wv=wv, wd=wd, tt=tt, sl=sl):
                                    ins = None
                                    for k in range(KC):
                                        ins = nc.tensor.matmul(ps[:, 0:wd], lhsT=abv[:, k, tt * 128:(tt + 1) * 128], rhs=wv[:, k, sl, 0:wd],
                                                               start=(k == 0), stop=(k == KC - 1))
                                    return ins
                                c.op("pe", mm, [wt, ab], [ps])
                                epi(ps, pb, 0, wd, tb * TB + tt * 128)
        c.barrier()

    def resid_norm(self, xsrc, ysrc, gain_ap, alpha, xdst):
        c, nc, cfg = self.c, self.nc, self.cfg
        with contextlib.ExitStack() as es:
            g = c.sb(es, [128, D], F32, "gain")
            c.dma("sp", [(g[:], bc_rows(gain_ap, D))], [self.p_normg], [g])
            xt = [c.sb(es, [128, D], F32, "rx") for _ in range(2)]
            yt = [c.sb(es, [128, D], F32, "ry") for _ in range(2)]
            junk = c.sb(es, [128, D], BF16, "rjunk")
            stat = [c.sb(es, [128, 4], F32, "rstat") for _ in range(2)]
            for ti in range(cfg.NT):
                x_, y_, s_ = xt[ti % 2], yt[ti % 2], stat[ti % 2]
                rows = slice(ti * 128, (ti + 1) * 128)
                c.dma("sp", [(y_[:], ysrc.t[rows, :])], [ysrc], [y_])
                c.dma("sp", [(x_[:], xsrc.t[rows, :])], [xsrc], [x_])
                c.op("dve", lambda s_=s_: nc.vector.memset(s_[:], 0.0), [], [s_])
                c.op("act", lambda y_=y_, s_=s_: nc.scalar.activation(out=junk[:], in_=y_[:], func=AF.Square, accum_out=s_[:, 0:1]), [y_, s_], [junk, s_])
                c.op("act", lambda s_=s_: nc.scalar.activation(out=s_[:, 1:2], in_=s_[:, 0:1], func=AF.Sqrt, scale=1.0 / D, bias=self.epsb[:, 0:1]), [s_, self.epsb], [s_])
                c.op("dve", lambda s_=s_: nc.vector.reciprocal(out=s_[:, 2:3], in_=s_[:, 1:2]), [s_], [s_])
                c.op("dve", lambda y_=y_, s_=s_: nc.vector.scalar_tensor_tensor(out=y_[:], in0=y_[:], scalar=s_[:, 2:3], in1=g[:], op0=ALU.mult, op1=ALU.mult), [y_, s_, g], [y_])
                c.op("dve", lambda x_=x_, y_=y_: nc.vector.scalar_tensor_tensor(out=x_[:], in0=y_[:], scalar=float(alpha), in1=x_[:], op0=ALU.mult, op1=ALU.add), [x_, y_], [x_])
                c.dma("sp", [(xdst.t[rows, :], x_[:])], [x_], [xdst])
        c.barrier()

    def ffn(self, l, which, xsrc, xdst):
        c, nc, cfg = self.c, self.nc, self.cfg
        TW = cfg.TW
        gi = 0 if which == "a" else 4
        self.normT(xsrc, self.p_normg.t[l, gi, :], self.hT)
        nb = DFF // 256
        blocks = [[b, nb + b] for b in range(nb)]
        pend = {}

        def pre1(es):
            self.f_sg = [c.sb(es, [128, 512], F32, "fsg") for _ in range(3)]
            self.f_ho = [c.sb(es, [128, 512], BF16, "fho") for _ in range(3)]
            self.f_i = 0

        def epi1(ps, pb, c0, cw, t0):
            if pb < nb:
                pend[(pb, c0, t0)] = ps
                return
            gps = pend.pop((pb - nb, c0, t0))
            gcol = (pb - nb) * 256 + c0
            i = self.f_i
            self.f_i += 1
            sg, ho = self.f_sg[i % 3], self.f_ho[i % 3]
            c.op("act", lambda: nc.scalar.activation(out=sg[:, 0:TW], in_=gps[:, 0:TW], func=AF.Silu), [gps], [sg])
            c.op("dve", lambda: nc.vector.tensor_tensor(out=ho[:, 0:TW], in0=sg[:, 0:TW], in1=ps[:, 0:TW], op=ALU.mult), [sg, ps], [ho])
            c.dma("sp", [(self.hidT.t[gcol:gcol + 128, t0:t0 + TW], ho[:, 0:TW])], [ho], [self.hidT])
        self.gemm(self.hT, D, "f%s_in" % which, l, blocks, "ws", epi1, pre=pre1)

        def pre2(es):
            self.f_yo = [c.sb(es, [128, 512], F32, "fyo") for _ in range(4)]
            self.f_i = 0

        def epi2(ps, pb, c0, wd, t0):
            i = self.f_i
            self.f_i += 1
            yo = self.f_yo[i % 4]
            if i % 2 == 0:
                c.op("act", lambda: nc.scalar.copy(out=yo[:, 0:wd], in_=ps[:, 0:wd]), [ps], [yo])
            else:
                c.op("dve", lambda: nc.vector.tensor_copy(out=yo[:, 0:wd], in_=ps[:, 0:wd]), [ps], [yo])
            c.dma("sp", [(self.ybuf.t[t0:t0 + 128, pb * 512:pb * 512 + wd], yo[:, 0:wd])], [yo], [self.ybuf])
        self.gemm(self.hidT, DFF, "f%s_out" % which, l, [[b] for b in range(D // 512)], "as", epi2, pre=pre2)
        self.resid_norm(xsrc, self.ybuf, self.p_normg.t[l, gi + 1, :], 0.5, xdst)

    def layer(self, l, xcur):
        cfg = self.cfg
        last = (l == cfg.DEPTH - 1)
        self.ffn(l, "a", xcur, self.xa)
        if cfg.stop == "ffn_a":
            return self.xa
        raise NotImplementedError


class KM(K):
    def proj(self, l):
        c, nc, cfg = self.c, self.nc, self.cfg
        TW = cfg.TW
        self.normT(self.xa, self.p_normg.t[l, 2, :], self.hT)
        ws_list = list(range(0, 10)) + list(range(11, 15)) + list(range(19, 23)) + [23, 24, 25, 27] + list(range(30, 54))
        as_list = [10] + list(range(15, 19)) + [26, 28, 29]

        def pre(es):
            self.e_f = [c.sb(es, [128, 512], F32, "ef") for _ in range(3)]
            self.e_b = [c.sb(es, [128, 512], BF16, "eb") for _ in range(3)]
            self.e_i = 0

        def epi_ws(ps, pb, c0, cw, t0):
            i = self.e_i
            self.e_i += 1
            ef, eb = self.e_f[i % 3], self.e_b[i % 3]
            src = ps[0:cw, 0:TW]
            cols = slice(t0, t0 + TW)
            if pb < 4:
                r0 = pb * 512 + c0
                c.op("act", lambda: nc.scalar.activation(out=eb[0:cw, 0:TW], in_=src, func=AF.Silu), [ps], [eb])
                c.dma("sp", [(self.szT.t[r0:r0 + cw, cols], eb[0:cw, 0:TW])], [eb], [self.szT])
            elif pb < 10:
                r0 = (pb - 4) * 512 + c0
                c.op("dve", lambda: nc.vector.tensor_copy(out=ef[0:cw, 0:TW], in_=src), [ps], [ef])
                c.dma("sp", [(self.xbcT.t[r0:r0 + cw, cols], ef[0:cw, 0:TW])], [ef], [self.xbcT])
            elif pb < 15:
                r0 = (pb - 11) * 512 + c0
                c.op("act", lambda: nc.scalar.activation(out=eb[0:cw, 0:TW], in_=src, func=AF.Gelu_apprx_tanh), [ps], [eb])
                c.dma("sp", [(self.uT.t[r0:r0 + cw, cols], eb[0:cw, 0:TW])], [eb], [self.uT])
            elif pb < 23:
                r0 = (pb - 19) * 512 + c0
                c.op("dve", lambda: nc.vector.tensor_copy(out=eb[0:cw, 0:TW], in_=src), [ps], [eb])
                c.dma("sp", [(self.qT.t[r0:r0 + cw, cols], eb[0:cw, 0:TW])], [eb], [self.qT])
            elif pb < 30:
                seg = {23: 0, 24: 1, 25: 2, 27: 3}[pb]
                r0 = seg * 512 + c0
                c.op("act", lambda: nc.scalar.copy(out=eb[0:cw, 0:TW], in_=src), [ps], [eb])
                c.dma("sp", [(self.kvT.t[r0:r0 + cw, cols], eb[0:cw, 0:TW])], [eb], [self.kvT])
            else:
                r0 = (pb - 30) * 512 + c0
                c.op("act", lambda: nc.scalar.activation(out=eb[0:cw, 0:TW], in_=src, func=AF.Sigmoid), [ps], [eb])
                c.dma("sp", [(self.mgT.t[r0:r0 + cw, cols], eb[0:cw, 0:TW])], [eb], [self.mgT])
        self.gemm(self.hT, D, "w_in", l, [[p] for p in ws_list], "ws", epi_ws, pre=pre)

        def epi_as(ps, pb, c0, wd, t0):
            i = self.e_i
            self.e_i += 1
            ef, eb = self.e_f[i % 3], self.e_b[i % 3]
            src = ps[:, 0:wd]
            rows = slice(t0, t0 + 128)
            if pb == 10:
                c.op("dve", lambda: nc.vector.tensor_copy(out=ef[:, 0:wd], in_=src), [ps], [ef])
                c.dma("sp", [(self.dt_tm.t[rows, :], ef[:, 0:wd])], [ef], [self.dt_tm])
            elif pb < 19:
                c.op("act", lambda: nc.scalar.activation(out=ef[:, 0:wd], in_=src, func=AF.Gelu_apprx_tanh), [ps], [ef])
                c.dma("sp", [(self.v_tm.t[rows, (pb - 15) * 512:(pb - 15) * 512 + wd], ef[:, 0:wd])], [ef], [self.v_tm])
            elif pb in (26, 28):
                o = 0 if pb == 26 else 512
                c.op("dve", lambda: nc.vector.tensor_copy(out=eb[:, 0:wd], in_=src), [ps], [eb])
                c.dma("sp", [(self.vtm2.t[rows, o:o + wd], eb[:, 0:wd])], [eb], [self.vtm2])
            else:
                c.op("act", lambda: nc.scalar.activation(out=ef[:, 0:wd], in_=src, func=AF.Sigmoid), [ps], [ef])
                c.dma("sp", [(self.ngate.t[rows, :], ef[:, 0:wd])], [ef], [self.ngate])
        self.gemm(self.hT, D, "w_in", l, [[p] for p in as_list], "as", epi_as, pre=pre)
        T = cfg.T
        c.dma("sp", [(self.halo_loc.t[:, 0:3], self.xbcT.t[:, T - 3:T])], [self.xbcT], [self.halo_loc])
        self.exchange(self.kvT, self.kvT.t, self.g_kvT, self.g_kvT.t, 2048, T * 2)
        self.exchange(self.vtm2, self.vtm2.t, self.g_vtm, self.g_vtm.t, T, 2048)
        self.exchange(self.halo_loc, self.halo_loc.t, self.g_halo, self.g_halo.t, XBC, 16)
        c.barrier()

    def sgu(self, l):
        c, nc, cfg = self.c, self.nc, self.cfg
        with contextlib.ExitStack() as es:
            wmT = c.sb(es, [128, 16 * 128], BF16, "wmT")
            wmv = v3(wmT.t[:], 128)
            bb = c.sb(es, [128, MIX], F32, "sgub")
            gv = c.sb(es, [128, MIX], F32, "sgug")
            c.dma("sp", [(bb[:], bc_rows(self.p_sgub.t[l, :], MIX))], [self.p_sgub], [bb])
            c.dma("sp", [(gv[:], bc_rows(self.p_sgug.t[l, :], MIX))], [self.p_sgug], [gv])
            wf = [c.sb(es, [128, 128], F32, "swf") for _ in range(2)]
            wb = [c.sb(es, [128, 128], BF16, "swb") for _ in range(2)]
            for g in range(16):
                a, b = wf[g % 2], wb[g % 2]
                c.dma("sp", [(a[:], self.p_sguw.t[l, g])], [self.p_sguw], [a])
                c.op("pool", lambda a=a: nc.gpsimd.affine_select(out=a[:], in_=a[:], pattern=[[-1, 128]], compare_op=ALU.is_ge, fill=0.0, base=0, channel_multiplier=1), [a], [a])
                c.op("dve", lambda a=a, b=b: nc.vector.tensor_copy(out=b[:], in_=a[:]), [a], [b])
                ps = c.ps()
                psb = ps.t[:].bitcast(BF16)
                c.op("pe", lambda b=b, psb=psb: nc.tensor.transpose(psb[:, 0:128], b[:], self.ident[:]), [b, self.ident], [ps])
                c.op("act", lambda g=g, psb=psb: nc.scalar.copy(out=wmv[:, g, :], in_=psb[:, 0:128]), [ps], [wmT])
            vt = [c.sb(es, [128, MIX], F32, "sv") for _ in range(2)]
            vn = [c.sb(es, [128, MIX], BF16, "svn") for _ in range(2)]
            ut = [c.sb(es, [128, MIX], BF16, "su") for _ in range(2)]
            yo = [c.sb(es, [128, MIX], BF16, "sy") for _ in range(2)]
            t1 = [c.sb(es, [128, 512], F32, "st1") for _ in range(2)]
            junk = c.sb(es, [128, MIX], BF16, "sjunk")
            stat = [c.sb(es, [128, 4], F32, "sstat") for _ in range(2)]
            k = 0
            for ti in range(cfg.NT):
                v_, n_, u_, y_, s_ = vt[ti % 2], vn[ti % 2], ut[ti % 2], yo[ti % 2], stat[ti % 2]
                tok = slice(ti * 128, (ti + 1) * 128)
                c.dma("sp", [(v_[:], self.v_tm.t[tok, :])], [self.v_tm], [v_])
                c.dma("sp", [(v3(u_.t[:], 128), self.uT.t[:, tok].rearrange("(g d) t -> d g t", d=128))], [self.uT], [u_])
                c.op("dve", lambda s_=s_: nc.vector.memset(s_[:], 0.0), [], [s_])
                c.op("act", lambda v_=v_, s_=s_: nc.scalar.activation(out=junk[:], in_=v_[:], func=AF.Square, accum_out=s_[:, 0:1]), [v_, s_], [junk, s_])
                c.op("act", lambda s_=s_: nc.scalar.activation(out=s_[:, 1:2], in_=s_[:, 0:1], func=AF.Sqrt, scale=1.0 / MIX, bias=self.epsb[:, 0:1]), [s_, self.epsb], [s_])
                c.op("dve", lambda s_=s_: nc.vector.reciprocal(out=s_[:, 2:3], in_=s_[:, 1:2]), [s_], [s_])
                c.op("dve", lambda v_=v_, n_=n_, s_=s_: nc.vector.scalar_tensor_tensor(out=n_[:], in0=v_[:], scalar=s_[:, 2:3], in1=gv[:], op0=ALU.mult, op1=ALU.mult), [v_, s_, gv], [n_])
                for g4 in range(4):
                    ps = c.ps()

                    def mm(ps=ps, n_=n_, g4=g4):
                        ins = None
                        for gg in range(4):
                            g = g4 * 4 + gg
                            ins = nc.tensor.matmul(ps[:, gg * 128:(gg + 1) * 128], lhsT=n_[:, g * 128:(g + 1) * 128], rhs=wmv[:, g, :], start=True, stop=True)
                        return ins
                    c.op("pe", mm, [n_, wmT], [ps])
                    t_ = t1[k % 2]
                    k += 1
                    cs = slice(g4 * 512, (g4 + 1) * 512)
                    c.op("dve", lambda ps=ps, t_=t_, cs=cs: nc.vector.tensor_tensor(out=t_[:], in0=ps[:, 0:512], in1=bb[:, cs], op=ALU.add), [ps, bb], [t_])
                    c.op("pool", lambda t_=t_, y_=y_, u_=u_, cs=cs: nc.gpsimd.tensor_tensor(out=y_[:, cs], in0=t_[:], in1=u_[:, cs], op=ALU.mult), [t_, u_], [y_])
                c.dma("sp", [(self.ybT.t[:, tok].rearrange("(g d) t -> d g t", d=128), v3(y_.t[:], 128))], [y_], [self.ybT])
        c.barrier()

    def ssd(self, l):
        c, nc, cfg = self.c, self.nc, self.cfg
        T, NT = cfg.T, cfg.NT
        with contextlib.ExitStack() as es:
            pc = c.sb(es, [128, 24 * 5], F32, "pc")
            pcv = v3(pc.t[:], 5)
            c.dma("sp", [(pcv, self.p_conv.t[l].rearrange("(cc p) k -> p cc k", p=128))], [self.p_conv], [pc])
            hl = c.sb(es, [128, 8 * 24 * 4], F32, "hl")
            hlv = hl.t[:].rearrange("p (r cc k) -> p r cc k", r=8, k=4)
            pairs = []
            for rank in range(8):
                pairs.append((hlv[:, rank], self.g_halo.t[0, rank % 2, rank // 2].rearrange("(cc p) k -> p cc k", p=128)))
            c.dma("sp", pairs, [self.g_halo], [hl])
            hs = c.sb(es, [128, 24 * 4], F32, "hs")
            hl2 = hl.t[:].rearrange("p (r x) -> p r x", r=8)
            c.op("dve", lambda: nc.vector.tensor_scalar(out=hs[:], in0=hl2[:, 0, :], scalar1=self.cinfo[:, 2:3], scalar2=None, op0=ALU.mult), [hl, self.cinfo], [hs])
            for rank in range(1, 8):
                c.op("dve", lambda rank=rank: nc.vector.scalar_tensor_tensor(out=hs[:], in0=hl2[:, rank, :], scalar=self.cinfo[:, 2 + rank:3 + rank], in1=hs[:], op0=ALU.mult, op1=ALU.add), [hl, hs, self.cinfo], [hs])
            hsv = v3(hs.t[:], 4)
            ut = [c.sb(es, [128, T + 4], F32, "cu") for _ in range(2)]
            acc = [c.sb(es, [128, T], F32, "cacc") for _ in range(2)]
            co = [c.sb(es, [128, T], BF16, "cout") for _ in range(2)]
            for cc in range(24):
                u_, a_, o_ = ut[cc % 2], acc[cc % 2], co[cc % 2]
                c.dma("sp", [(u_[:, 3:T + 3], self.xbcT.t[cc * 128:(cc + 1) * 128, :])], [self.xbcT], [u_])
                c.op("act", lambda u_=u_, cc=cc: nc.scalar.copy(out=u_[:, 0:3], in_=hsv[:, cc, 0:3]), [hs], [u_])
                c.op("dve", lambda u_=u_, a_=a_, cc=cc: nc.vector.tensor_scalar(out=a_[:], in0=u_[:, 0:T], scalar1=pcv[:, cc, 0:1], scalar2=None, op0=ALU.mult), [u_, pc], [a_])
                for k in range(1, 4):
                    c.op("dve", lambda u_=u_, a_=a_, cc=cc, k=k: nc.vector.scalar_tensor_tensor(out=a_[:], in0=u_[:, k:T + k], scalar=pcv[:, cc, k:k + 1], in1=a_[:], op0=ALU.mult, op1=ALU.add), [u_, a_, pc], [a_])
                c.op("act", lambda a_=a_, o_=o_, cc=cc: nc.scalar.activation(out=o_[:], in_=a_[:], func=AF.Silu, bias=pcv[:, cc, 4:5]), [a_, pc], [o_])
                c.dma("sp", [(self.convT.t[cc * 128:(cc + 1) * 128, :], o_[:])], [o_], [self.convT])
        c.barrier()
        for full in (False, True):
            with contextlib.ExitStack() as es:
                self._ssd_scan(es, l, full)
            c.barrier()
            if not full:
                self._ssd_combine(l)

    def _ssd_consts(self, es, l):
        c, nc = self.c, self.nc
        k = {}
        U = k["U"] = c.sb(es, [128, 128], F32, "U")
        c.op("pool", lambda: nc.gpsimd.memset(U[:], 1.0), [], [U])
        c.op("pool", lambda: nc.gpsimd.affine_select(out=U[:], in_=U[:], pattern=[[1, 128]], compare_op=ALU.is_ge, fill=0.0, base=0, channel_multiplier=-1), [U], [U])
        onesf = k["ones"] = c.sb(es, [128, 128], F32, "onesf")
        c.op("pool", lambda: nc.gpsimd.memset(onesf[:], 1.0), [], [onesf])
        mneg = k["mneg"] = c.sb(es, [128, 128], F32, "mneg")
        c.op("pool", lambda: nc.gpsimd.memset(mneg[:], 0.0), [], [mneg])
        c.op("pool", lambda: nc.gpsimd.affine_select(out=mneg[:], in_=mneg[:], pattern=[[1, 128]], compare_op=ALU.is_ge, fill=NEG, base=0, channel_multiplier=-1), [mneg], [mneg])
        one1 = k["one1"] = c.sb(es, [128, 1], F32, "one1")
        c.op("pool", lambda: nc.gpsimd.memset(one1[:], 1.0), [], [one1])
        hp = c.sb(es, [128, 3 * 32], F32, "hp")
        c.dma("sp", [(hp[:], bc_rows(self.p_head.t[l].rearrange("a h -> (a h)"), 96))], [self.p_head], [hp])
        k["hp"] = hp
        Ab = k["Ab"] = c.sb(es, [128, 32], F32, "Ab")
        c.op("act", lambda: nc.scalar.activation(out=Ab[:], in_=hp[:, 0:32], func=AF.Exp), [hp], [Ab])
        c.op("dve", lambda: nc.vector.tensor_scalar(out=Ab[:], in0=Ab[:], scalar1=-1.0, scalar2=None, op0=ALU.mult), [Ab], [Ab])
        gP = k["gP"] = c.sb(es, [64, 32], F32, "gP")
        c.dma("sp", [(gP[:], self.p_ssdg.t[l])], [self.p_ssdg], [gP])
        return k

    def _ssd_scan(self, es, l, full):
        c, nc, cfg = self.c, self.nc, self.cfg
        T, NT = cfg.T, cfg.NT
        k = self._ssd_consts(es, l)
        U, onesf, mneg, one1, hp, Ab, gP = k["U"], k["ones"], k["mneg"], k["one1"], k["hp"], k["Ab"], k["gP"]
        sb = lambda shape, dt, nm: c.sb(es, shape, dt, nm)
        S = sb([128, MIX], F32, "S")
        Sb = sb([128, MIX], BF16, "Sb")
        acst = sb([128, 32], F32, "acst")
        if full:
            c.dma("sp", [(S[:], self.s_init.t)], [self.s_init], [S])
        else:
            c.op("pool", lambda: nc.gpsimd.memset(S[:], 0.0), [], [S])
        c.op("pool", lambda: nc.gpsimd.memset(acst[:], 0.0), [], [acst])
        c.op("act", lambda: nc.scalar.copy(out=Sb[:], in_=S[:]), [S], [Sb])
        dtr = sb([128, 32], F32, "dtr")
        dt = sb([128, 32], F32, "dt")
        a = sb([128, 32], F32, "a")
        acum = sb([128, 32], F32, "acum")
        dte = sb([128, 32], F32, "dte")
        dec = sb([128, 32], F32, "dec")
        Dm = sb([128, 4096], F32, "Dm")
        Rs = sb([128, 4096], F32, "Rs")
        xsT = sb([128, 16 * 128], BF16, "xsT")
        BT = sb([128, 4 * 128], BF16, "BT")
        xs_tm = sb([128, MIX], BF16, "xs_tm")
        B_tm = sb([128, 512], BF16, "B_tm")
        xdt = sb([128, MIX], BF16, "xdt")
        xdte = sb([128, MIX], BF16, "xdte")
        if full:
            CT = sb([128, 4 * 128], BF16, "CT")
            LT = sb([128, 4096], F32, "LT")
            ER = sb([128, 4096], F32, "ER")
            MT = sb([128, 4096], BF16, "MT")
            Cp = sb([128, 4096], BF16, "Cp")
            yT = sb([64, 4096], F32, "yT")
            xsP = sb([64, 4096], BF16, "xsP")
            szP = sb([64, 4096], BF16, "szP")
            tmp = sb([64, 4096], F32, "tmpP")
            ssum = sb([64, 4096], F32, "ssum")
            ssq = sb([64, 512], F32, "ssq")
            yo = sb([64, 4096], BF16, "yoP")
        Dm3, Rs3 = v3(Dm.t[:], 128), v3(Rs.t[:], 128)
        for j in range(NT):
            tok = slice(j * 128, (j + 1) * 128)
            c.dma("sp", [(dtr[:], self.dt_tm.t[tok, :])], [self.dt_tm], [dtr])
            c.dma("sp", [(v3(BT.t[:], 128), self.convT.t[2048:2560, tok].rearrange("(g n) s -> n g s", n=128))], [self.convT], [BT])
            c.dma("sp", [(v3(xsT.t[:], 128), self.convT.t[0:2048, tok].rearrange("(f p) s -> p f s", p=128))], [self.convT], [xsT])
            c.op("dve", lambda: nc.vector.tensor_tensor(out=dt[:], in0=dtr[:], in1=hp[:, 32:64], op=ALU.add), [dtr, hp], [dt])
            c.op("act", lambda: nc.scalar.activation(out=dt[:], in_=dt[:], func=AF.Exp), [dt], [dt])
            c.op("act", lambda: nc.scalar.activation(out=dt[:], in_=dt[:], func=AF.Ln, bias=one1[:, 0:1]), [dt, one1], [dt])
            c.op("dve", lambda: nc.vector.tensor_tensor(out=a[:], in0=dt[:], in1=Ab[:], op=ALU.mult), [dt, Ab], [a])
            ps = c.ps()
            c.op("pe", lambda ps=ps: nc.tensor.matmul(ps[:, 0:32], lhsT=U[:], rhs=a[:], start=True, stop=True), [U, a], [ps])
            c.op("dve", lambda ps=ps: nc.vector.tensor_copy(out=acum[:], in_=ps[:, 0:32]), [ps], [acum])
            c.op("dve", lambda: nc.vector.tensor_tensor(out=Dm3, in0=U[:].unsqueeze(1).to_broadcast([128, 32, 128]), in1=a[:].unsqueeze(2).to_broadcast([128, 32, 128]), op=ALU.mult), [U, a], [Dm])
            for b8 in range(8):
                ps = c.ps()
                cs = slice(b8 * 512, (b8 + 1) * 512)
                c.op("pe", lambda ps=ps, cs=cs: nc.tensor.matmul(ps[:, 0:512], lhsT=onesf[:], rhs=Dm[:, cs], start=True, stop=True), [onesf, Dm], [ps])
                if b8 % 2 == 0:
                    c.op("act", lambda ps=ps, cs=cs: nc.scalar.copy(out=Rs[:, cs], in_=ps[:, 0:512]), [ps], [Rs])
                else:
                    c.op("dve", lambda ps=ps, cs=cs: nc.vector.tensor_copy(out=Rs[:, cs], in_=ps[:, 0:512]), [ps], [Rs])
            c.op("dve", lambda: nc.vector.tensor_tensor(out=dte[:], in0=Rs3[:, :, 127], in1=acum[:], op=ALU.subtract), [Rs, acum], [dte])
            c.op("act", lambda: nc.scalar.activation(out=dte[:], in_=dte[:], func=AF.Exp), [dte], [dte])
            c.op("act", lambda: nc.scalar.activation(out=dec[:], in_=Rs3[:, :, 127], func=AF.Exp), [Rs], [dec])
            c.op("dve", lambda: nc.vector.tensor_tensor(out=acst[:], in0=acst[:], in1=Rs3[:, :, 127], op=ALU.add), [acst, Rs], [acst])
            for half in range(2):
                ps = c.ps()
                psb = ps.t[:].bitcast(BF16)

                def tr(psb=psb, half=half):
                    ins = None
                    for q in range(8):
                        f = half * 8 + q
                        ins = nc.tensor.transpose(psb[:, q * 128:(q + 1) * 128], xsT[:, f * 128:(f + 1) * 128], self.ident[:])
                    return ins
                c.op("pe", tr, [xsT, self.ident], [ps])
                c.op("act", lambda psb=psb, half=half: nc.scalar.copy(out=xs_tm[:, half * 1024:(half + 1) * 1024], in_=psb[:, 0:1024]), [ps], [xs_tm])
            ps = c.ps()
            psb = ps.t[:].bitcast(BF16)

            def trb(psb=psb):
                ins = None
                for g in range(4):
                    ins = nc.tensor.transpose(psb[:, g * 128:(g + 1) * 128], BT[:, g * 128:(g + 1) * 128], self.ident[:])
                return ins
            c.op("pe", trb, [BT, self.ident], [ps])
            c.op("dve", lambda psb=psb: nc.vector.tensor_copy(out=B_tm[:], in_=psb[:, 0:512]), [ps], [B_tm])
            c.op("dve", lambda: nc.vector.tensor_tensor(out=v3(xdt.t[:], 64), in0=v3(xs_tm.t[:], 64), in1=dt[:].unsqueeze(2).to_broadcast([128, 32, 64]), op=ALU.mult), [xs_tm, dt], [xdt])
            c.op("pool", lambda: nc.gpsimd.tensor_tensor(out=v3(xdte.t[:], 64), in0=v3(xdt.t[:], 64), in1=dte[:].unsqueeze(2).to_broadcast([128, 32, 64]), op=ALU.mult), [xdt, dte], [xdte])
            if full:
                c.dma("sp", [(v3(CT.t[:], 128), self.convT.t[2560:3072, tok].rearrange("(g n) s -> n g s", n=128))], [self.convT], [CT])
                c.dma("sp", [(v3(xsP.t[:], 128), self.convT.t[0:2048, tok].rearrange("(h p) s -> p h s", p=64))], [self.convT], [xsP])
                c.dma("sp", [(v3(szP.t[:], 128), self.szT.t[:, tok].rearrange("(h p) s -> p h s", p=64))], [self.szT], [szP])
                c.op("dve", lambda: nc.vector.tensor_tensor(out=v3(LT.t[:], 128), in0=Rs3, in1=acum[:].unsqueeze(2).to_broadcast([128, 32, 128]), op=ALU.subtract), [Rs, acum], [LT])
                c.op("pool", lambda: nc.gpsimd.tensor_tensor(out=v3(LT.t[:], 128), in0=v3(LT.t[:], 128), in1=mneg[:].unsqueeze(1).to_broadcast([128, 32, 128]), op=ALU.add), [LT, mneg], [LT])
                c.op("act", lambda: nc.scalar.activation(out=LT[:], in_=LT[:], func=AF.Exp), [LT], [LT])
                c.op("act", lambda: nc.scalar.activation(out=ER[:], in_=Rs[:], func=AF.Exp), [Rs], [ER])
                pcb = c.ps()

                def cb(pcb=pcb):
                    ins = None
                    for g in range(4):
                        ins = nc.tensor.matmul(pcb[:, g * 128:(g + 1) * 128], lhsT=BT[:, g * 128:(g + 1) * 128], rhs=CT[:, g * 128:(g + 1) * 128], start=True, stop=True)
                    return ins
                c.op("pe", cb, [BT, CT], [pcb])
                cbv = pcb.t[:, 0:512].rearrange("p (g l) -> p g l", l=128).unsqueeze(2).to_broadcast([128, 4, 8, 128])
                c.op("dve", lambda cbv=cbv: nc.vector.tensor_tensor(out=MT.t[:].rearrange("p (g k l) -> p g k l", g=4, k=8), in0=LT.t[:].rearrange("p (g k l) -> p g k l", g=4, k=8), in1=cbv, op=ALU.mult), [LT, pcb], [MT])
                ctv = CT.t[:].rearrange("p (g l) -> p g l", l=128).unsqueeze(2).to_broadcast([128, 4, 8, 128])
                c.op("pool", lambda ctv=ctv: nc.gpsimd.tensor_tensor(out=Cp.t[:].rearrange("p (g k l) -> p g k l", g=4, k=8), in0=ER.t[:].rearrange("p (g k l) -> p g k l", g=4, k=8), in1=ctv, op=ALU.mult), [ER, CT], [Cp])
                for h4 in range(8):
                    ps = c.ps()

                    def ymm(ps=ps, h4=h4):
                        ins = None
                        for hh in range(4):
                            h = h4 * 4 + hh
                            o = ps[0:64, hh * 128:(hh + 1) * 128]
                            nc.tensor.matmul(o, lhsT=xdt[:, h * 64:(h + 1) * 64], rhs=MT[:, h * 128:(h + 1) * 128], start=True, stop=False)
                            ins = nc.tensor.matmul(o, lhsT=Sb[:, h * 64:(h + 1) * 64], rhs=Cp[:, h * 128:(h + 1) * 128], start=False, stop=True)
                        return ins
                    c.op("pe", ymm, [xdt, MT, Sb, Cp], [ps])
                    cs = slice(h4 * 512, (h4 + 1) * 512)
                    if h4 % 2 == 0:
                        c.op("act", lambda ps=ps, cs=cs: nc.scalar.copy(out=yT[:, cs], in_=ps[0:64, 0:512]), [ps], [yT])
                    else:
                        c.op("dve", lambda ps=ps, cs=cs: nc.vector.tensor_copy(out=yT[:, cs], in_=ps[0:64, 0:512]), [ps], [yT])
            psl = []
            for g in range(4):
                ps = c.ps()
                c.op("pe", lambda ps=ps, g=g: nc.tensor.matmul(ps[:, 0:512], lhsT=B_tm[:, g * 128:(g + 1) * 128], rhs=xdte[:, g * 512:(g + 1) * 512], start=True, stop=True), [B_tm, xdte], [ps])
                psl.append(ps)
            c.op("dve", lambda: nc.vector.tensor_tensor(out=v3(S.t[:], 64), in0=v3(S.t[:], 64), in1=dec[:].unsqueeze(2).to_broadcast([128, 32, 64]), op=ALU.mult), [S, dec, Sb], [S])
            for g in range(4):
                cs = slice(g * 512, (g + 1) * 512)
                c.op("dve", lambda g=g, cs=cs: nc.vector.tensor_tensor(out=S[:, cs], in0=S[:, cs], in1=psl[g][:, 0:512], op=ALU.add), [S, psl[g]], [S])
            c.op("act", lambda: nc.scalar.copy(out=Sb[:], in_=S[:]), [S], [Sb])
            if full:
                c.op("dve", lambda: nc.vector.tensor_tensor(out=v3(tmp.t[:], 128), in0=v3(xsP.t[:], 128), in1=hp[0:64, 64:96].unsqueeze(2).to_broadcast([64, 32, 128]), op=ALU.mult), [xsP, hp], [tmp])
                c.op("dve", lambda: nc.vector.tensor_tensor(out=yT[:], in0=yT[:], in1=tmp[:], op=ALU.add), [yT, tmp], [yT])
                c.op("dve", lambda: nc.vector.tensor_tensor(out=yT[:], in0=yT[:], in1=szP[:], op=ALU.mult), [yT, szP], [yT])
                c.op("pool", lambda: nc.gpsimd.tensor_tensor(out=tmp[:], in0=yT[:], in1=yT[:], op=ALU.mult), [yT], [tmp])
                for b8 in range(8):
                    ps = c.ps()
                    cs = slice(b8 * 512, (b8 + 1) * 512)
                    c.op("pe", lambda ps=ps, cs=cs: nc.tensor.matmul(ps[0:64, 0:512], lhsT=onesf[0:64, 0:64], rhs=tmp[:, cs], start=True, stop=True), [onesf, tmp], [ps])
                    c.op("act", lambda ps=ps, cs=cs: nc.scalar.copy(out=ssum[:, cs], in_=ps[0:64, 0:512]), [ps], [ssum])
                c.op("dve", lambda: nc.vector.tensor_reduce(out=v3(ssq.t[:], 128), in_=ssum.t[:].rearrange("p (g k l) -> p g l k", g=4, k=8), axis=AX.X, op=ALU.add), [ssum], [ssq])
                c.op("act", lambda: nc.scalar.activation(out=ssq[:], in_=ssq[:], func=AF.Sqrt, scale=1.0 / 512, bias=self.epsb[0:64, 0:1]), [ssq, self.epsb], [ssq])
                c.op("dve", lambda: nc.vector.reciprocal(out=ssq[:], in_=ssq[:]), [ssq], [ssq])
                rsv = ssq.t[:].rearrange("p (g l) -> p g l", l=128).unsqueeze(2).to_broadcast([64, 4, 8, 128])
                c.op("dve", lambda rsv=rsv: nc.vector.tensor_tensor(out=tmp.t[:].rearrange("p (g k l) -> p g k l", g=4, k=8), in0=yT.t[:].rearrange("p (g k l) -> p g k l", g=4, k=8), in1=rsv, op=ALU.mult), [yT, ssq], [tmp])
                c.op("pool", lambda: nc.gpsimd.tensor_tensor(out=v3(yo.t[:], 128), in0=v3(tmp.t[:], 128), in1=gP[:].unsqueeze(2).to_broadcast([64, 32, 128]), op=ALU.mult), [tmp, gP], [yo])
                c.dma("sp", [(self.yaT.t[:, tok].rearrange("(h p) s -> p h s", p=64), v3(yo.t[:], 128))], [yo], [self.yaT])
        if not full:
            c.dma("sp", [(self.s_loc.t, S[:])], [S], [self.s_loc])
            c.dma("sp", [(self.p_loc.t, acst[0:16, :])], [acst], [self.p_loc])

    def _ssd_combine(self, l):
        c, nc, cfg = self.c, self.nc, self.cfg
        self.exchange(self.s_loc, self.s_loc.t, self.g_sloc, self.g_sloc.t, 128, MIX * 4)
        self.exchange(self.p_loc, self.p_loc.t, self.g_ploc, self.g_ploc.t, 16, 128)
        with contextlib.ExitStack() as es:
            H = c.sb(es, [128, MIX], F32, "H")
            acc = c.sb(es, [128, MIX], F32, "Hacc")
            Sr = [c.sb(es, [128, MIX], F32, "Sr") for _ in range(2)]
            Pr = [c.sb(es, [128, 32], F32, "Pr") for _ in range(2)]
            c.op("pool", lambda: nc.gpsimd.memset(H[:], 0.0), [], [H])
            c.op("pool", lambda: nc.gpsimd.memset(acc[:], 0.0), [], [acc])
            for rank in range(8):
                q, r = rank % 2, rank // 2
                s_, p_ = Sr[rank % 2], Pr[rank % 2]
                c.op("dve", lambda rank=rank: nc.vector.scalar_tensor_tensor(out=acc[:], in0=H[:], scalar=self.cinfo[:, 10 + rank:11 + rank], in1=acc[:], op0=ALU.mult, op1=ALU.add), [H, acc, self.cinfo], [acc])
                if rank == 7:
                    break
                c.dma("sp", [(s_[0:64, :], self.g_sloc.t[0, q, r]), (s_[64:128, :], self.g_sloc.t[1, q, r])], [self.g_sloc], [s_])
                c.dma("sp", [(p_[:], bc_rows(self.g_ploc.t[0, q, r, 0, :], 32))], [self.g_ploc], [p_])
                c.op("act", lambda p_=p_: nc.scalar.activation(out=p_[:], in_=p_[:], func=AF.Exp), [p_], [p_])
                c.op("dve", lambda p_=p_: nc.vector.tensor_tensor(out=v3(H.t[:], 64), in0=v3(H.t[:], 64), in1=p_[:].unsqueeze(2).to_broadcast([128, 32, 64]), op=ALU.mult), [H, p_], [H])
                c.op("dve", lambda s_=s_: nc.vector.tensor_tensor(out=H[:], in0=H[:], in1=s_[:], op=ALU.add), [H, s_], [H])
            c.dma("sp", [(self.s_init.t, acc[:])], [acc], [self.s_init])
        c.barrier()

    def kv_piece(self, seg, g, rank):
        row0 = seg * 512 + g * 128
        return self.g_kvT.t[row0 // self.kv_cr, rank % 2, rank // 2, (row0 % self.kv_cr):(row0 % self.kv_cr) + 128, :]

    def nsa(self, l):
        c, nc, cfg = self.c, self.nc, self.cfg
        T, NT, S = cfg.T, cfg.NT, cfg.S
        NCK, NCMP, NBLK = S // 128, S // 16, S // 64
        NCMPC = NCMP // 128
        NB = min(512, NCMP)
        W = 129 + NBLK
        SCALE = 128 ** -0.5
        acc = [c.psb[k] for k in range(4)]
        c.psr = (4, 8)
        with contextlib.ExitStack() as es0:
            sb0 = lambda shape, dt, nm: c.sb(es0, shape, dt, nm)
            I4 = sb0([128, 512], BF16, "I4")
            for k in range(4):
                c.op("dve", lambda k=k: nc.vector.tensor_copy(out=I4[:, k * 128:(k + 1) * 128], in_=self.ident[:]), [self.ident], [I4])
            zf = sb0([128, 128], F32, "zf")
            Atri = sb0([128, 128], BF16, "Atri")
            Astr = sb0([128, 128], BF16, "Astr")
            hvA = sb0([128, 128], BF16, "hvA")
            AstrH = sb0([128, 128], BF16, "AstrH")
            c.op("pool", lambda: nc.gpsimd.memset(zf[:], 0.0), [], [zf])
            c.op("pool", lambda: nc.gpsimd.affine_select(out=zf[:], in_=zf[:], pattern=[[-1, 128]], compare_op=ALU.is_ge, fill=NEG, base=0, channel_multiplier=1), [zf], [zf])
            c.op("dve", lambda: nc.vector.tensor_copy(out=Atri[:], in_=zf[:]), [zf], [Atri])
            c.op("pool", lambda: nc.gpsimd.memset(zf[:], 0.0), [Atri], [zf])
            c.op("pool", lambda: nc.gpsimd.affine_select(out=zf[:], in_=zf[:], pattern=[[1, 128]], compare_op=ALU.is_gt, fill=NEG, base=0, channel_multiplier=-1), [zf], [zf])
            c.op("dve", lambda: nc.vector.tensor_copy(out=Astr[:], in_=zf[:]), [zf], [Astr])
            hvc = sb0([128, 1], F32, "hvc")
            c.op("dve", lambda: nc.vector.tensor_scalar(out=hvc[:], in0=self.cinfo[:, 18:19], scalar1=-NEG, scalar2=NEG, op0=ALU.mult, op1=ALU.add), [self.cinfo], [hvc])
            c.op("dve", lambda: nc.vector.tensor_copy(out=hvA[:], in_=hvc[:, 0:1].to_broadcast([128, 128])), [hvc], [hvA])
            c.op("dve", lambda: nc.vector.tensor_scalar(out=AstrH[:], in0=zf[:], scalar1=hvc[:, 0:1], scalar2=None, op0=ALU.add), [zf, hvc], [AstrH])
            patt = sb0([128, NCMP], F32, "patt")
            c.op("pool", lambda: nc.gpsimd.iota(patt[:], pattern=[[16, NCMP]], base=31, channel_multiplier=-1, allow_small_or_imprecise_dtypes=True), [], [patt])
            Jt = sb0([128, NBLK], F32, "Jt")
            c.op("pool", lambda: nc.gpsimd.iota(Jt[:], pattern=[[1, NBLK]], base=0, channel_multiplier=0, allow_small_or_imprecise_dtypes=True), [], [Jt])
            pidx = sb0([128, 1], F32, "pidx")
            c.op("pool", lambda: nc.gpsimd.iota(pidx[:], pattern=[[0, 1]], base=0, channel_multiplier=1, allow_small_or_imprecise_dtypes=True), [], [pidx])
            half = sb0([128, 1], F32, "half")
            c.op("dve", lambda: nc.vector.tensor_scalar(out=half[:], in0=pidx[:], scalar1=64.0, scalar2=None, op0=ALU.is_ge), [pidx], [half])
            thr = sb0([128, NT], F32, "thr")
            blk0 = sb0([128, NT], F32, "blk0")
            curc = sb0([128, NT], F32, "curc")
            for i in range(NT):
                c.op("dve", lambda i=i: nc.vector.tensor_scalar(out=thr[:, i:i + 1], in0=self.cinfo[:, 1:2], scalar1=float(128 * i), scalar2=None, op0=ALU.add), [self.cinfo], [thr])
                c.op("dve", lambda i=i: nc.vector.tensor_scalar(out=blk0[:, i:i + 1], in0=self.cinfo[:, 1:2], scalar1=1.0 / 64, scalar2=float(2 * i), op0=ALU.mult, op1=ALU.add), [self.cinfo], [blk0])
            c.op("dve", lambda: nc.vector.tensor_scalar(out=curc[:], in0=blk0[:], scalar1=half[:, 0:1], scalar2=None, op0=ALU.add), [blk0, half], [curc])
            ovc = sb0([128, NCMPC * NBLK], BF16, "ovc")
            with contextlib.ExitStack() as est:
                ovf = c.sb(est, [128, NCMPC * NBLK], F32, "ovf")
                ov2 = c.sb(est, [128, NCMPC * NBLK], F32, "ov2")
                c.op("pool", lambda: nc.gpsimd.iota(ovf[:], pattern=[[2048, NCMPC], [-64, NBLK]], base=0, channel_multiplier=16, allow_small_or_imprecise_dtypes=True), [], [ovf])
                c.op("dve", lambda: nc.vector.tensor_scalar(out=ov2[:], in0=ovf[:], scalar1=64.0, scalar2=None, op0=ALU.is_lt), [ovf], [ov2])
                c.op("dve", lambda: nc.vector.tensor_scalar(out=ovf[:], in0=ovf[:], scalar1=-32.0, scalar2=None, op0=ALU.is_gt), [ovf], [ovf])
                c.op("dve", lambda: nc.vector.tensor_tensor(out=ovc[:], in0=ovf[:], in1=ov2[:], op=ALU.mult), [ovf, ov2], [ovc])
            c.barrier()
            posf = sb0([128, 64], F32, "posf")
            posb = sb0([128, 64], BF16, "posb")
            c.dma("sp", [(v3(posf.t[:], 32), self.p_cpos.t[l].rearrange("a d l -> d a l"))], [self.p_cpos], [posf])
            c.op("dve", lambda: nc.vector.tensor_copy(out=posb[:], in_=posf[:]), [posf], [posb])
            w2f = sb0([128, 256], F32, "w2f")
            w2b = sb0([128, 256], BF16, "w2b")
            c.dma("sp", [(v3(w2f.t[:], 128), self.p_cw2.t[l].rearrange("a h d -> h a d"))], [self.p_cw2], [w2f])
            c.op("dve", lambda: nc.vector.tensor_copy(out=w2b[:], in_=w2f[:]), [w2f], [w2b])
            w1 = []
            for kvi, nm in enumerate(("cw1k", "cw1v")):
                w = sb0([128, 32 * 128], BF16, "w1" + nm)
                wv = v3(w.t[:], 128)
                pairs = [(wv[:, 4 * rank:4 * rank + 4, :], self.wpiece(nm, l, 0, rank).rearrange("(l d) h -> d l h", d=128)) for rank in range(8)]
                c.dma("sp", pairs, [self.wfull[nm, l]], [w])
                w1.append(w)
            cbias = sb0([128, 2], F32, "cbias")
            for kvi in range(2):
                ps = c.ps()
                wv = v3(w1[kvi].t[:], 128)

                def bm(ps=ps, wv=wv, kvi=kvi):
                    ins = None
                    for ll in range(32):
                        ins = nc.tensor.matmul(ps[:, 0:1], lhsT=wv[:, ll, :], rhs=posb[:, kvi * 32 + ll:kvi * 32 + ll + 1], start=(ll == 0), stop=(ll == 31))
                    return ins
                c.op("pe", bm, [w1[kvi], posb], [ps])
                c.op("dve", lambda ps=ps, kvi=kvi: nc.vector.tensor_copy(out=cbias[:, kvi:kvi + 1], in_=ps[:, 0:1]), [ps], [cbias])
            for g in range(4):
                with contextlib.ExitStack() as esg:
                    sbg = lambda shape, dt, nm: c.sb(esg, shape, dt, nm)
                    kcmpT = sbg([128, NCMP], BF16, "kcmpT")
                    vcmpA = sbg([128, NCMPC * W], BF16, "vcmpA")
                    vcv = v3(vcmpA.t[:], W)
                    c.op("pool", lambda: nc.gpsimd.memset(vcv[:, :, 128:129], 1.0), [], [vcmpA])
                    c.op("dve", lambda: nc.vector.tensor_copy(out=vcv[:, :, 129:W], in_=v3(ovc.t[:], NBLK)), [ovc], [vcmpA])
                    with contextlib.ExitStack() as esa:
                        raws = []
                        for seg in range(2):
                            raw = c.sb(esa, [128, S + 32], BF16, "raw")
                            c.op("pool", lambda raw=raw: nc.gpsimd.memset(raw[:, S:S + 32], 0.0), [], [raw])
                            pairs = [(raw[:, rank * T:(rank + 1) * T], self.kv_piece(seg, g, rank)) for rank in range(8)]
                            c.dma("sp", pairs, [self.g_kvT], [raw])
                            raws.append(raw)
                        hid = [c.sb(esa, [128, 512], BF16, "hid") for _ in range(2)]
                        hi = 0
                        for kvi in range(2):
                            raw = raws[kvi]
                            wv = v3(w1[kvi].t[:], 128)
                            for nb in range(NCMP // NB):
                                ps = c.ps()
                                rb = raw[:, 0:1]

                                def hm(ps=ps, wv=wv, rb=rb, nb=nb):
                                    ins = None
                                    for ll in range(32):
                                        rhs = bass.AP(rb.tensor, rb.offset + nb * NB * 16 + ll, [list(rb.ap[0]), [16, NB]])
                                        ins = nc.tensor.matmul(ps[:, 0:NB], lhsT=wv[:, ll, :], rhs=rhs, start=(ll == 0), stop=(ll == 31))
                                    return ins
                                c.op("pe", hm, [w1[kvi], raw], [ps])
                                h_ = hid[hi % 2]
                                hi += 1
                                if g == 0 and kvi == 0 and nb == 0 and "dbg_hid" in cfg.dbg:
                                    dh_ = self.dtile("dbg_hid", [128, 512], F32)
                                    dw_ = self.dtile("dbg_w1", [128, 4096], BF16)
                                    dr_ = self.dtile("dbg_raw", [128, 2048], BF16)
                                    db_ = self.dtile("dbg_cb", [128, 2], F32)
                                    hf_ = c.sb(esa, [128, 512], F32, "hf_")
                                    c.op("dve", lambda ps=ps: nc.vector.tensor_copy(out=hf_[:], in_=ps[:, 0:512]), [ps], [hf_])
                                    c.dma("sp", [(dh_.t, hf_[:])], [hf_], [dh_])
                                    c.dma("sp", [(dw_.t, w1[0][:])], [w1[0]], [dw_])
                                    c.dma("sp", [(dr_.t, raw[:, 0:2048])], [raw], [dr_])
                                    c.dma("sp", [(db_.t, cbias[:])], [cbias], [db_])
                                c.op("act", lambda ps=ps, h_=h_, kvi=kvi: nc.scalar.activation(out=h_[:, 0:NB], in_=ps[:, 0:NB], func=AF.Gelu_apprx_tanh, bias=cbias[:, kvi:kvi + 1]), [ps, cbias], [h_])
                                if kvi == 0:
                                    ps2 = c.ps()
                                    c.op("pe", lambda ps2=ps2, h_=h_: nc.tensor.matmul(ps2[:, 0:NB], lhsT=w2b[:, 0:128], rhs=h_[:, 0:NB], start=True, stop=True), [w2b, h_], [ps2])
                                    c.op("dve", lambda ps2=ps2, nb=nb: nc.vector.tensor_copy(out=kcmpT[:, nb * NB:(nb + 1) * NB], in_=ps2[:, 0:NB]), [ps2], [kcmpT])
                                else:
                                    ps2 = c.ps()

                                    def vm(ps2=ps2, h_=h_):
                                        ins = None
                                        for sub in range(NB // 128):
                                            ins = nc.tensor.matmul(ps2[:, sub * 128:(sub + 1) * 128], lhsT=h_[:, sub * 128:(sub + 1) * 128], rhs=w2b[:, 128:256], start=True, stop=True)
                                        return ins
                                    c.op("pe", vm, [w2b, h_], [ps2])
                                    c.op("dve", lambda ps2=ps2, nb=nb: nc.vector.tensor_copy(out=vcv[:, nb * (NB // 128):(nb + 1) * (NB // 128), 0:128], in_=v3(ps2.t[:, 0:NB], 128)), [ps2], [vcmpA])
                    if g == 0 and "dbg_kc" in cfg.dbg:
                        dk = self.dtile("dbg_kc", [128, NCMP], BF16)
                        dv = self.dtile("dbg_vc", [128, NCMPC * W], BF16)
                        c.dma("sp", [(dk.t, kcmpT[:])], [kcmpT], [dk])
                        c.dma("sp", [(dv.t, vcmpA[:])], [vcmpA], [dv])
                    c.barrier()
                    ksT = sbg([128, S], BF16, "ksT")
                    c.dma("sp", [(ksT[:, rank * T:(rank + 1) * T], self.kv_piece(2, g, rank)) for rank in range(8)], [self.g_kvT], [ksT])
                    vsA = sbg([128, NCK * 129], BF16, "vsA")
                    vsv = v3(vsA.t[:], 129)
                    c.op("pool", lambda: nc.gpsimd.memset(vsv[:, :, 128:129], 1.0), [], [vsA])
                    cr = self.vt_cr
                    pairs = []
                    for rank in range(8):
                        for ch in range(T // cr):
                            kc0 = rank * NT + ch * (cr // 128)
                            pairs.append((vsv[:, kc0:kc0 + cr // 128, 0:128],
                                          self.g_vtm.t[ch, rank % 2, rank // 2, :, g * 128:(g + 1) * 128].rearrange("(cc k) d -> k cc d", k=128)))
                    c.dma("sp", pairs, [self.g_vtm], [vsA])
                    ksL = sbg([128, T], BF16, "ksL")
                    c.dma("sp", [(ksL[:], self.kvT.t[1024 + g * 128:1024 + (g + 1) * 128, :])], [self.kvT], [ksL])
                    vsL = sbg([128, NT * 129], BF16, "vsL")
                    vslv = v3(vsL.t[:], 129)
                    c.op("pool", lambda: nc.gpsimd.memset(vslv[:, :, 128:129], 1.0), [], [vsL])
                    c.dma("sp", [(vslv[:, :, 0:128], self.vtm2.t[:, g * 128:(g + 1) * 128].rearrange("(cc k) d -> k cc d", k=128))], [self.vtm2], [vsL])
                    kwT = sbg([128, 512 + T], BF16, "kwT")
                    c.dma("sp", [(kwT[:, 512:512 + T], self.kvT.t[1536 + g * 128:1536 + (g + 1) * 128, :])], [self.kvT], [kwT])
                    vwA = sbg([128, (4 + NT) * 129], BF16, "vwA")
                    vwv = v3(vwA.t[:], 129)
                    c.op("pool", lambda: nc.gpsimd.memset(vwv[:, :, 128:129], 1.0), [], [vwA])
                    c.dma("sp", [(vwv[:, 4:4 + NT, 0:128], self.vtm2.t[:, 512 + g * 128:512 + (g + 1) * 128].rearrange("(cc k) d -> k cc d", k=128))], [self.vtm2], [vwA])
                    with contextlib.ExitStack() as esh:
                        hk = c.sb(esh, [128, 8 * 512], BF16, "hk")
                        hkv = v3(hk.t[:], 512)
                        c.dma("sp", [(hkv[:, rank, :], self.kv_piece(3, g, rank)[:, T - 512:T]) for rank in range(8)], [self.g_kvT], [hk])
                        hv = c.sb(esh, [128, 8 * 512], BF16, "hv")
                        hvv = hv.t[:].rearrange("p (r cc d) -> p r cc d", r=8, d=128)
                        pairs = []
                        for rank in range(8):
                            for ch in range((T - 512) // cr, T // cr):
                                cc0 = (ch * cr - (T - 512)) // 128
                                pairs.append((hvv[:, rank, cc0:cc0 + cr // 128, :],
                                              self.g_vtm.t[ch, rank % 2, rank // 2, :, 512 + g * 128:512 + (g + 1) * 128].rearrange("(cc k) d -> k cc d", k=128)))
                        c.dma("sp", pairs, [self.g_vtm], [hv])
                        hks = c.sb(esh, [128, 512], F32, "hks")
                        hvs = c.sb(esh, [128, 512], F32, "hvs")
                        hv2 = v3(hv.t[:], 512)
                        c.op("dve", lambda: nc.vector.tensor_scalar(out=hks[:], in0=hkv[:, 0, :], scalar1=self.cinfo[:, 2:3], scalar2=None, op0=ALU.mult), [hk, self.cinfo], [hks])
                        c.op("dve", lambda: nc.vector.tensor_scalar(out=hvs[:], in0=hv2[:, 0, :], scalar1=self.cinfo[:, 2:3], scalar2=None, op0=ALU.mult), [hv, self.cinfo], [hvs])
                        for rank in range(1, 8):
                            c.op("dve", lambda rank=rank: nc.vector.scalar_tensor_tensor(out=hks[:], in0=hkv[:, rank, :], scalar=self.cinfo[:, 2 + rank:3 + rank], in1=hks[:], op0=ALU.mult, op1=ALU.add), [hk, hks, self.cinfo], [hks])
                            c.op("dve", lambda rank=rank: nc.vector.scalar_tensor_tensor(out=hvs[:], in0=hv2[:, rank, :], scalar=self.cinfo[:, 2 + rank:3 + rank], in1=hvs[:], op0=ALU.mult, op1=ALU.add), [hv, hvs, self.cinfo], [hvs])
                        c.op("dve", lambda: nc.vector.tensor_copy(out=kwT[:, 0:512], in_=hks[:]), [hks], [kwT])
                        c.op("dve", lambda: nc.vector.tensor_copy(out=vwv[:, 0:4, 0:128], in_=v3(hvs.t[:], 128)), [hvs], [vwA])
                    c.barrier()
                    qg = sbg([128, 4 * T], BF16, "qg")
                    qgv = v3(qg.t[:], T)
                    c.dma("sp", [(qgv, self.qT.t[g * 512:(g + 1) * 512, :].rearrange("(k d) t -> d k t", d=128))], [self.qT], [qg])
                    q2 = sbg([128, 512], BF16, "q2")
                    gt = sbg([128, 48], F32, "gt")
                    yacc = sbg([128, 512], F32, "yacc")
                    ybf = sbg([128, 512], BF16, "ybf")
                    ycs = sbg([128, 512], BF16, "ycs")
                    mbc = sbg([128, NCMP], BF16, "mbc")
                    et = [sbg([128, 512], BF16, "et") for _ in range(3)]
                    rz = sbg([128, 8], F32, "rz")
                    imp = sbg([128, NBLK], F32, "imp")
                    dd = sbg([128, NBLK], F32, "dd")
                    f1 = sbg([128, NBLK], F32, "f1")
                    f2 = sbg([128, NBLK], F32, "f2")
                    sc = sbg([128, NBLK], F32, "sc")
                    wk = sbg([128, NBLK], F32, "wk")
                    m8 = sbg([128, 16], F32, "m8")
                    nsel = sbg([128, NBLK], BF16, "nsel")
                    nse = sbg([128, S], BF16, "nse")
                    ei = [0]

                    pend = []

                    def flush():
                        while pend:
                            pend.pop(0)()

                    def attend(kT_ap, mask_ap, v_ap, first, last, width):
                        ps = c.ps()
                        areads = list(self._areads)

                        def qk(ps=ps):
                            ins = nc.tensor.matmul(ps[:, 0:512], lhsT=kT_ap, rhs=q2[:], start=True, stop=(mask_ap is None))
                            if mask_ap is not None:
                                ins = nc.tensor.matmul(ps[:, 0:512], lhsT=mask_ap, rhs=I4[:], start=False, stop=True)
                            return ins
                        c.op("pe", qk, areads, [ps])
                        e_ = et[ei[0] % 3]
                        ei[0] += 1
                        c.op("act", lambda: nc.scalar.activation(out=e_[:], in_=ps[:, 0:512], func=AF.Exp, scale=SCALE), [ps], [e_])

                        def pv():
                            ins = None
                            for k in range(4):
                                ins = nc.tensor.matmul(acc[k][:, 0:width], lhsT=e_[:, k * 128:(k + 1) * 128], rhs=v_ap, start=first, stop=last)
                            return ins
                        flush()
                        pend.append(lambda: c.op("pe", pv, [e_] + areads, acc))

                    import os as _os
                    _brs = [int(x) for x in _os.environ.get("NSA_BR", "0,1,2").split(",")]

                    def finish(br, first_branch):
                        flush()
                        if first_branch:
                            c.op("pool", lambda: nc.gpsimd.memset(yacc[:], 0.0), [], [yacc])
                        first_branch = False
                        for k in range(4):
                            c.op("dve", lambda k=k: nc.vector.tensor_scalar(out=rz[:, k:k + 1], in0=acc[k][:, 128:129], scalar1=1e-30, scalar2=None, op0=ALU.max), [acc[k]], [rz])
                            c.op("dve", lambda k=k: nc.vector.reciprocal(out=rz[:, k:k + 1], in_=rz[:, k:k + 1]), [rz], [rz])
                            gi = br * 16 + g * 4 + k
                            c.op("dve", lambda k=k, gi=gi: nc.vector.tensor_tensor(out=rz[:, 4 + k:5 + k], in0=rz[:, k:k + 1], in1=gt[:, gi:gi + 1], op=ALU.mult), [rz, gt], [rz])
                            ys = yacc[:, k * 128:(k + 1) * 128]
                            if br not in _brs:
                                continue
                            if first_branch:
                                c.op("dve", lambda k=k, ys=ys: nc.vector.tensor_scalar(out=ys, in0=acc[k][:, 0:128], scalar1=rz[:, 4 + k:5 + k], scalar2=None, op0=ALU.mult), [acc[k], rz], [yacc])
                            else:
                                c.op("dve", lambda k=k, ys=ys: nc.vector.scalar_tensor_tensor(out=ys, in0=acc[k][:, 0:128], scalar=rz[:, 4 + k:5 + k], in1=ys, op0=ALU.mult, op1=ALU.add), [acc[k], rz, yacc], [yacc])

                    for i in range(NT):
                        tok = slice(i * 128, (i + 1) * 128)
                        c.op("dve", lambda tok=tok: nc.vector.tensor_copy(out=v3(q2.t[:], 128), in_=qgv[:, :, tok]), [qg], [q2])
                        c.dma("sp", [(gt[:], self.ngate.t[tok, :])], [self.ngate], [gt])
                        c.op("dve", lambda i=i: nc.vector.tensor_scalar(out=mbc[:], in0=patt[:], scalar1=thr[:, i:i + 1], scalar2=NEG, op0=ALU.is_gt, op1=ALU.mult), [patt, thr], [mbc])
                        self._areads = [kcmpT, mbc, vcmpA, q2, I4]
                        for ncn in range(NCMPC):
                            attend(kcmpT[:, ncn * 128:(ncn + 1) * 128], mbc[:, ncn * 128:(ncn + 1) * 128], vcv[:, ncn, :], ncn == 0, ncn == NCMPC - 1, W)
                        finish(0, True)
                        for k in range(4):
                            if k == 0:
                                c.op("dve", lambda: nc.vector.tensor_scalar(out=imp[:], in0=acc[0][:, 129:W], scalar1=rz[:, 0:1], scalar2=None, op0=ALU.mult), [acc[0], rz], [imp])
                            else:
                                c.op("dve", lambda k=k: nc.vector.scalar_tensor_tensor(out=imp[:], in0=acc[k][:, 129:W], scalar=rz[:, k:k + 1], in1=imp[:], op0=ALU.mult, op1=ALU.add), [acc[k], rz, imp], [imp])
                        V = nc.vector
                        c.op("dve", lambda i=i: V.tensor_scalar(out=dd[:], in0=Jt[:], scalar1=curc[:, i:i + 1], scalar2=None, op0=ALU.subtract), [Jt, curc], [dd])
                        c.op("dve", lambda: V.tensor_scalar(out=f1[:], in0=dd[:], scalar1=0.0, scalar2=None, op0=ALU.is_equal), [dd], [f1])
                        c.op("dve", lambda: V.tensor_scalar(out=f2[:], in0=dd[:], scalar1=-1.0, scalar2=None, op0=ALU.is_equal), [dd], [f2])
                        c.op("dve", lambda: V.tensor_tensor(out=f1[:], in0=f1[:], in1=f2[:], op=ALU.max), [f1, f2], [f1])
                        c.op("dve", lambda: V.scalar_tensor_tensor(out=sc[:], in0=f1[:], scalar=1e4, in1=imp[:], op0=ALU.mult, op1=ALU.add), [f1, imp], [sc])
                        c.op("dve", lambda: V.tensor_scalar(out=f2[:], in0=dd[:], scalar1=0.0, scalar2=-1e9, op0=ALU.is_gt, op1=ALU.mult), [dd], [f2])
                        c.op("dve", lambda: V.tensor_tensor(out=sc[:], in0=sc[:], in1=f2[:], op=ALU.add), [sc, f2], [sc])
                        c.op("dve", lambda: V.memset(sc[:, 0:1], 1e4), [], [sc])
                        c.op("dve", lambda: V.max(out=m8[:, 0:8], in_=sc[:]), [sc], [m8])
                        c.op("dve", lambda: V.match_replace(out=wk[:], in_to_replace=m8[:, 0:8], in_values=sc[:], imm_value=-1e30), [sc, m8], [wk])
                        c.op("dve", lambda: V.max(out=m8[:, 8:16], in_=wk[:]), [wk], [m8])
                        c.op("dve", lambda: V.tensor_scalar(out=f1[:], in0=sc[:], scalar1=m8[:, 15:16], scalar2=None, op0=ALU.is_ge), [sc, m8], [f1])
                        c.op("dve", lambda i=i: V.tensor_scalar(out=f2[:], in0=Jt[:], scalar1=blk0[:, i:i + 1], scalar2=None, op0=ALU.is_lt), [Jt, blk0], [f2])
                        c.op("dve", lambda: V.tensor_tensor(out=f1[:], in0=f1[:], in1=f2[:], op=ALU.mult), [f1, f2], [f1])
                        c.op("dve", lambda: V.tensor_scalar(out=nsel[:], in0=f1[:], scalar1=-NEG, scalar2=NEG, op0=ALU.mult, op1=ALU.add), [f1], [nsel])
                        c.op("dve", lambda: V.tensor_copy(out=v3(nse.t[:], 64), in_=nsel[:].unsqueeze(2).to_broadcast([128, NBLK, 64])), [nsel], [nse])
                        self._areads = [ksT, ksL, nse, Atri, vsA, vsL, q2, I4]
                        attend(ksL[:, tok], Atri[:], vslv[:, i, :], True, False, 129)
                        for kc in range(NCK):
                            attend(ksT[:, kc * 128:(kc + 1) * 128], nse[:, kc * 128:(kc + 1) * 128], vsv[:, kc, :], False, kc == NCK - 1, 129)
                        finish(1, False)
                        self._areads = [kwT, vwA, Atri, Astr, AstrH, hvA, q2, I4]
                        for r in range(5):
                            idx = 4 + i - r
                            halo = idx < 4
                            if r == 0:
                                m = Atri[:]
                            elif r == 4:
                                m = AstrH[:] if halo else Astr[:]
                            else:
                                m = hvA[:] if halo else None
                            attend(kwT[:, idx * 128:(idx + 1) * 128], m, vwv[:, idx, :], r == 0, r == 4, 129)
                        finish(2, False)
                        c.op("act", lambda: nc.scalar.copy(out=ybf[:], in_=yacc[:]), [yacc], [ybf])
                        ps = c.ps()
                        psb = ps.t[:].bitcast(BF16)

                        def tr(psb=psb):
                            ins = None
                            for k in range(4):
                                ins = nc.tensor.transpose(psb[:, k * 128:(k + 1) * 128], ybf[:, k * 128:(k + 1) * 128], self.ident[:])
                            return ins
                        c.op("pe", tr, [ybf, self.ident], [ps])
                        c.op("act", lambda psb=psb: nc.scalar.copy(out=ycs[:], in_=psb[:, 0:512]), [ps], [ycs])
                        c.dma("sp", [(self.ycT.t[g * 512:(g + 1) * 512, tok].rearrange("(k d) t -> d k t", d=128), v3(ycs.t[:], 128))], [ycs], [self.ycT])
                c.barrier()
        c.psr = (0, 8)
        c.barrier()

    def merge_out(self, l, xsrc, xdst):
        c, nc, cfg = self.c, self.nc, self.cfg
        TW = cfg.TW
        srcs = [self.yaT, self.ybT, self.ycT]
        for bi in range(3):
            if cfg.stop == "sgu" and bi != 1:
                continue

            def pre(es):
                self.e_f = [c.sb(es, [128, 512], F32, "mf") for _ in range(3)]
                self.e_b = [c.sb(es, [128, 512], BF16, "mb") for _ in range(3)]
                self.e_i = 0

            def epi(ps, pb, c0, cw, t0, bi=bi):
                i = self.e_i
                self.e_i += 1
                ef, eb = self.e_f[i % 3], self.e_b[i % 3]
                r0 = pb * 512 + c0
                c.dma("sp", [(eb[0:cw, 0:TW], self.mgT.t[bi * D + r0:bi * D + r0 + cw, t0:t0 + TW])], [self.mgT], [eb])
                c.op("dve", lambda: nc.vector.tensor_tensor(out=ef[0:cw, 0:TW], in0=ps[0:cw, 0:TW], in1=eb[0:cw, 0:TW], op=ALU.mult), [ps, eb], [ef])
                c.dma("sp", [(self.mrg[bi].t[r0:r0 + cw, t0:t0 + TW], ef[0:cw, 0:TW])], [ef], [self.mrg[bi]])
            self.gemm(srcs[bi], MIX, "w_br%d" % bi, l, [[p] for p in range(8)], "ws", epi, pre=pre)
        with contextlib.ExitStack() as es:
            a = [[c.sb(es, [128, cfg.T], F32, "ma") for _ in range(3)] for _ in range(2)]
            o = [c.sb(es, [128, cfg.T], BF16, "mo") for _ in range(2)]
            for dc in range(32):
                aa, oo = a[dc % 2], o[dc % 2]
                rows = slice(dc * 128, (dc + 1) * 128)
                for bi in range(3):
                    c.dma("sp", [(aa[bi][:], self.mrg[bi].t[rows, :])], [self.mrg[bi]], [aa[bi]])
                c.op("dve", lambda aa=aa: nc.vector.tensor_tensor(out=aa[0][:], in0=aa[0][:], in1=aa[1][:], op=ALU.add), [aa[0], aa[1]], [aa[0]])
                c.op("dve", lambda aa=aa, oo=oo: nc.vector.tensor_tensor(out=oo[:], in0=aa[0][:], in1=aa[2][:], op=ALU.add), [aa[0], aa[2]], [oo])
                c.dma("sp", [(self.mergedT.t[rows, :], oo[:])], [oo], [self.mergedT])
        c.barrier()

        def pre2(es):
            self.f_yo = [c.sb(es, [128, 512], F32, "oyo") for _ in range(4)]
            self.f_i = 0

        def epi2(ps, pb, c0, wd, t0):
            i = self.f_i
            self.f_i += 1
            yo = self.f_yo[i % 4]
            if i % 2 == 0:
                c.op("act", lambda: nc.scalar.copy(out=yo[:, 0:wd], in_=ps[:, 0:wd]), [ps], [yo])
            else:
                c.op("dve", lambda: nc.vector.tensor_copy(out=yo[:, 0:wd], in_=ps[:, 0:wd]), [ps], [yo])
            c.dma("sp", [(self.ybuf.t[t0:t0 + 128, pb * 512:pb * 512 + wd], yo[:, 0:wd])], [yo], [self.ybuf])
        self.gemm(self.mergedT, D, "w_o", l, [[b] for b in range(8)], "as", epi2, pre=pre2)
        self.resid_norm(xsrc, self.ybuf, self.p_normg.t[l, 3, :], 1.0, xdst)

    def layer(self, l, xcur):
        cfg = self.cfg
        last = (l == cfg.DEPTH - 1)
        self.ffn(l, "a", xcur, self.xa)
        if cfg.stop == "ffn_a":
            return self.xa
        self.proj(l)
        if cfg.stop == "proj":
            return self.xa
        if cfg.stop != "nsa":
            self.sgu(l)
        if cfg.stop == "sgu":
            return self.xa
        if cfg.stop != "nsa":
            self.ssd(l)
        if cfg.stop == "ssd":
            return self.xa
        self.nsa(l)
        if cfg.stop == "nsa":
            return self.xa
        with contextlib.ExitStack() as esp:
            if not last and cfg.stop is None:
                self.prep_weights(l + 1, esp)
            self.merge_out(l, self.xa, self.xb)
            if cfg.stop == "mix":
                return self.xb
            dst = self.out if last else self.xc
            self.ffn(l, "b", self.xb, dst)
        return dst


def build_program(cfg):
    return KM(cfg).build()


def shard_inputs(cfg, inputs, l0=0, x_override=None):
    T, DEPTH = cfg.T, cfg.DEPTH
    sl = slice(l0, l0 + DEPTH)
    f = lambda a: np.asarray(a, np.float32)
    x = f(inputs["x"])[0] if x_override is None else x_override
    wb = f(inputs["w_branch"])[sl]
    cw1 = f(inputs["cmp_w1"])[sl]
    wfull = {
        "fa_in": f(inputs["ffn_w_in"])[sl, 0], "fb_in": f(inputs["ffn_w_in"])[sl, 1],
        "fa_out": f(inputs["ffn_w_out"])[sl, 0], "fb_out": f(inputs["ffn_w_out"])[sl, 1],
        "w_in": f(inputs["w_in"])[sl], "w_br0": wb[:, 0], "w_br1": wb[:, 1], "w_br2": wb[:, 2],
        "w_o": f(inputs["w_out"])[sl], "cw1k": cw1[:, 0], "cw1v": cw1[:, 1],
    }
    conv = np.concatenate([f(inputs["ssd_conv_w"])[sl].transpose(0, 2, 1), f(inputs["ssd_conv_b"])[sl][:, :, None]], axis=2)
    head = np.stack([f(inputs["ssd_a_log"])[sl], f(inputs["ssd_dt_bias"])[sl], f(inputs["ssd_d"])[sl]], axis=1)
    ssdg = f(inputs["ssd_norm_g"])[sl].reshape(DEPTH, NHEAD, 64).transpose(0, 2, 1)
    cpos = f(inputs["cmp_pos"])[sl].transpose(0, 1, 3, 2)
    shared = {
        "p_normg": np.ascontiguousarray(f(inputs["norm_g"])[sl]),
        "p_conv": np.ascontiguousarray(conv), "p_head": np.ascontiguousarray(head), "p_ssdg": np.ascontiguousarray(ssdg),
        "p_sgug": np.ascontiguousarray(f(inputs["sgu_norm_g"])[sl]),
        "p_sguw": np.ascontiguousarray(f(inputs["sgu_w"])[sl]),
        "p_sgub": np.ascontiguousarray(f(inputs["sgu_b"])[sl].reshape(DEPTH, -1)),
        "p_cpos": np.ascontiguousarray(cpos), "p_cw2": np.ascontiguousarray(f(inputs["cmp_w2"])[sl]),
    }
    maps = []
    for ci in range(NC):
        m = {"x": np.ascontiguousarray(x[ci * T:(ci + 1) * T])}
        for nm, (kk, nn, pans, pw) in WSPEC.items():
            r = kk // NC
            m["w_" + nm] = np.ascontiguousarray(wfull[nm][:, ci * r:(ci + 1) * r, :])
        info = np.zeros((128, 32), np.float32)
        info[:, 0] = ci
        info[:, 1] = ci * T
        if ci > 0:
            info[:, 2 + ci - 1] = 1.0
        info[:, 10 + ci] = 1.0
        info[:, 18] = 1.0 if ci > 0 else 0.0
        m["cinfo"] = info
        m.update(shared)
        maps.append(m)
    return maps


LAYERS_PER_LAUNCH = 4
TOTAL_DEPTH = 4


def kernel(**inputs):
    cfg = Cfg(DEPTH=LAYERS_PER_LAUNCH)
    nc = build_program(cfg)
    x = None
    for l0 in range(0, TOTAL_DEPTH, LAYERS_PER_LAUNCH):
        maps = shard_inputs(cfg, inputs, l0=l0, x_override=x)
        res = run_bass_kernel_spmd(nc, maps, core_ids=list(range(NC)))
        x = np.concatenate([res.results[c]["out"] for c in range(NC)], axis=0)
    return x[None].astype(np.float32)
```
